# Optimizing a Trainium2 kernel written in Bass

```python
import math
import jax, jax.numpy as jnp
from jax import lax
import numpy as np

D_MODEL = 1024
BATCH = 8
SEQ = 2048
DEPTH = 1
DEC_BATCH = 128
DEC_SEQ = 1
PAST_LEN = 16384
PAGE_SIZE = 128

D_MIX = 2 * D_MODEL
HGRN_HEADS = 8
HGRN_DK = 128
HGRN_DV = 128
D_HGRN = HGRN_HEADS * HGRN_DV
HGRN_CHUNK = 64
SSM_HEADS = 16
SSM_HEAD_DIM = 64
D_SSM = SSM_HEADS * SSM_HEAD_DIM
SSM_STATE = 128
SSM_GROUPS = 2
SSM_CONV = 4
SSM_CHUNK = 128
CONV_DIM = D_SSM + 2 * SSM_GROUPS * SSM_STATE
DT_MIN = 0.001
DT_MAX = 0.1
D_FF = 2816
FFN_CONV = 3
EPS = 1e-6
IN_SIZES = (HGRN_HEADS * HGRN_DK, HGRN_HEADS * HGRN_DK, D_HGRN, D_HGRN, D_SSM, CONV_DIM, SSM_HEADS)
D_IN = sum(IN_SIZES)

kernel_name = 'hymba_hgrn2_mamba2_convffn_step'


def _chunk_len(L, C):
    return C if L % C == 0 else L


def rmsnorm(x, w):
    xf = x.astype(jnp.float32)
    y = xf * lax.rsqrt(jnp.mean(xf * xf, axis=-1, keepdims=True) + EPS)
    return (y * w.astype(jnp.float32)).astype(x.dtype)


def causal_dwconv(x, buf, w, b):
    L = x.shape[1]
    xe = jnp.concatenate([buf.astype(x.dtype), x], axis=1)
    y = b.astype(x.dtype)
    for j in range(w.shape[0]):
        y = y + xe[:, j:j + L] * w[j].astype(x.dtype)
    return y, xe[:, L:]


def hgrn2_chunk(S, inp):
    q, k, v, g = inp
    C = q.shape[2]
    b = jnp.cumsum(g, axis=2)
    o = jnp.einsum('bhtk,bhkv->bhtv', q * jnp.exp(b), S)
    causal = jnp.tril(jnp.ones((C, C), dtype=bool))[:, :, None]
    diff = b[:, :, :, None, :] - b[:, :, None, :, :]
    decay = jnp.exp(jnp.where(causal, diff, -jnp.inf))
    A = jnp.einsum('bhtsk,bhsk->bhts', decay * q[:, :, :, None, :], k)
    o = o + jnp.einsum('bhts,bhsv->bhtv', A, v)
    b_last = b[:, :, -1:]
    S_new = jnp.exp(b_last[:, :, 0])[..., None] * S + jnp.einsum('bhsk,bhsv->bhkv', k * jnp.exp(b_last - b), v)
    return S_new, o


def hgrn2_mix(q, k, v, log_f, S0):
    Bsz, L, H, _ = q.shape
    C = _chunk_len(L, HGRN_CHUNK)
    nc = L // C

    def to_chunks(t):
        return t.reshape(Bsz, nc, C, H, t.shape[-1]).transpose(1, 0, 3, 2, 4)

    S_T, o = lax.scan(hgrn2_chunk, S0, (to_chunks(q), to_chunks(k), to_chunks(v), to_chunks(log_f)))
    o = o.transpose(1, 0, 3, 2, 4).reshape(Bsz, L, H, -1)
    return o, S_T


def segsum(x):
    T = x.shape[-1]
    cs = jnp.cumsum(x, axis=-1)
    d = cs[..., :, None] - cs[..., None, :]
    return jnp.where(jnp.tril(jnp.ones((T, T), dtype=bool)), d, -jnp.inf)


def ssd_scan(X, A, Bh, Ch, S0):
    Bsz, L, H, P = X.shape
    T = _chunk_len(L, SSM_CHUNK)
    nc = L // T
    X = X.reshape(Bsz, nc, T, H, P)
    Bh = Bh.reshape(Bsz, nc, T, H, -1)
    Ch = Ch.reshape(Bsz, nc, T, H, -1)
    A = A.reshape(Bsz, nc, T, H).transpose(0, 3, 1, 2)
    A_cs = jnp.cumsum(A, axis=-1)
    Lmat = jnp.exp(segsum(A))
    y_diag = jnp.einsum('bclhn,bcshn,bhcls,bcshp->bclhp', Ch, Bh, Lmat, X)
    decay_states = jnp.exp(A_cs[..., -1:] - A_cs)
    states = jnp.einsum('bclhn,bhcl,bclhp->bchpn', Bh, decay_states, X)
    states = jnp.concatenate([S0[:, None], states], axis=1)
    decay_chunk = jnp.exp(segsum(jnp.pad(A_cs[..., -1], ((0, 0), (0, 0), (1, 0)))))
    states = jnp.einsum('bhzc,bchpn->bzhpn', decay_chunk, states)
    y_off = jnp.einsum('bclhn,bchpn,bhcl->bclhp', Ch, states[:, :-1], jnp.exp(A_cs))
    y = (y_diag + y_off).reshape(Bsz, L, H, P)
    return y, states[:, -1]


def decoder_layer(x, S_h, S_s, buf_s, buf_f, lb, w):
    f32 = jnp.float32
    Bsz, L, _ = x.shape
    dt_ = x.dtype
    h = rmsnorm(x, w['norm1_w'])
    proj = h @ w['w_in']
    splits = np.cumsum(IN_SIZES)[:-1].tolist()
    q_r, f_r, i_r, g_r, z, xbc_r, dt_r = jnp.split(proj, splits, axis=-1)

    f = lb + (1.0 - lb) * jax.nn.sigmoid(f_r.astype(f32))
    log_f = jnp.log(f)
    k = 1.0 - f
    q = jax.nn.silu(q_r.astype(f32))

    def heads(t):
        return t.reshape(Bsz, L, HGRN_HEADS, -1)

    o_a, S_h_new = hgrn2_mix(heads(q), heads(k), heads(i_r.astype(f32)), heads(log_f), S_h.astype(f32))
    o_a = rmsnorm(o_a, w['hgrn_norm_w']) * jax.nn.silu(heads(g_r).astype(f32))
    o_a = o_a.reshape(Bsz, L, D_HGRN).astype(dt_)

    xbc, buf_s_new = causal_dwconv(xbc_r, buf_s, w['ssm_conv_w'], w['ssm_conv_b'])
    xbc = jax.nn.silu(xbc.astype(f32))
    xs, Bm, Cm = jnp.split(xbc, [D_SSM, D_SSM + SSM_GROUPS * SSM_STATE], axis=-1)
    dt = jax.nn.softplus(dt_r.astype(f32) + w['ssm_dt_bias'].astype(f32))
    A = -jnp.exp(w['ssm_a_log'].astype(f32))
    X = xs.reshape(Bsz, L, SSM_HEADS, SSM_HEAD_DIM)
    rep = SSM_HEADS // SSM_GROUPS
    Bh = jnp.repeat(Bm.reshape(Bsz, L, SSM_GROUPS, SSM_STATE), rep, axis=2)
    Ch = jnp.repeat(Cm.reshape(Bsz, L, SSM_GROUPS, SSM_STATE), rep, axis=2)
    y, S_s_new = ssd_scan(X * dt[..., None], A * dt, Bh, Ch, S_s.astype(f32))
    y = y + w['ssm_d'].astype(f32)[:, None] * X
    y = y.reshape(Bsz, L, D_SSM) * jax.nn.silu(z.astype(f32))
    y = rmsnorm(y.reshape(Bsz, L, SSM_GROUPS, -1), w['ssm_norm_w'].reshape(SSM_GROUPS, -1))
    y = y.reshape(Bsz, L, D_SSM).astype(dt_)

    x = x + jnp.concatenate([o_a, y], axis=-1) @ w['w_out']

    h = rmsnorm(x, w['norm2_w'])
    gate, val = jnp.split(h @ w['w_up'], 2, axis=-1)
    gate, buf_f_new = causal_dwconv(gate, buf_f, w['ffn_conv_w'], w['ffn_conv_b'])
    x = x + (jax.nn.silu(gate) * val) @ w['w_down']
    return (x, S_h_new.astype(S_h.dtype), S_s_new.astype(S_s.dtype),
            buf_s_new.astype(buf_s.dtype), buf_f_new.astype(buf_f.dtype))


def run_trunk(x, st_hgrn, st_ssm, st_cs, st_cf, hgrn_lb, layer_w, final_norm_w):
    lb_all = jnp.cumsum(jax.nn.softmax(hgrn_lb.astype(jnp.float32), axis=0), axis=0)
    new_h, new_s, new_cs, new_cf = [], [], [], []
    for l in range(DEPTH):
        w = {name: arr[l] for name, arr in layer_w.items()}
        x, sh, ss, scs, scf = decoder_layer(x, st_hgrn[l], st_ssm[l], st_cs[l], st_cf[l], lb_all[l], w)
        new_h.append(sh)
        new_s.append(ss)
        new_cs.append(scs)
        new_cf.append(scf)
    y = rmsnorm(x, final_norm_w)
    return y, jnp.stack(new_h), jnp.stack(new_s), jnp.stack(new_cs), jnp.stack(new_cf)


def setup_inputs(seed: int = 0) -> dict:
    key = jax.random.key(seed)
    k = jax.random.split(key, 24)
    f32 = jnp.float32

    def nrm(i, shape, scale):
        return jax.random.normal(k[i], shape, f32) * scale

    x_prompt = nrm(0, (BATCH, SEQ, D_MODEL), 1.0)
    x_sample = nrm(1, (DEC_BATCH, DEC_SEQ, D_MODEL), 1.0)
    state_hgrn = nrm(2, (DEPTH, DEC_BATCH, HGRN_HEADS, HGRN_DK, HGRN_DV), 0.5)
    state_ssm = nrm(3, (DEPTH, DEC_BATCH, SSM_HEADS, SSM_HEAD_DIM, SSM_STATE), 0.3)
    state_conv_ssm = nrm(4, (DEPTH, DEC_BATCH, SSM_CONV - 1, CONV_DIM), 1.0)
    state_conv_ffn = nrm(5, (DEPTH, DEC_BATCH, FFN_CONV - 1, D_FF), 1.0)
    norm1_w = 1.0 + nrm(6, (DEPTH, D_MODEL), 0.02)
    w_in = nrm(7, (DEPTH, D_MODEL, D_IN), D_MODEL ** -0.5)
    hgrn_lb = nrm(8, (DEPTH + 1, HGRN_HEADS * HGRN_DK), 0.1)
    hgrn_norm_w = 1.0 + nrm(9, (DEPTH, HGRN_DV), 0.02)
    ssm_conv_w = nrm(10, (DEPTH, SSM_CONV, CONV_DIM), SSM_CONV ** -0.5)
    ssm_conv_b = nrm(11, (DEPTH, CONV_DIM), 0.02)
    dt0 = jnp.exp(jax.random.uniform(k[12], (DEPTH, SSM_HEADS), f32, math.log(DT_MIN), math.log(DT_MAX)))
    ssm_dt_bias = dt0 + jnp.log(-jnp.expm1(-dt0))
    ssm_a_log = jnp.log(jax.random.uniform(k[13], (DEPTH, SSM_HEADS), f32, 1.0, 16.0))
    ssm_d = 1.0 + nrm(14, (DEPTH, SSM_HEADS), 0.02)
    ssm_norm_w = 1.0 + nrm(15, (DEPTH, D_SSM), 0.02)
    w_out = nrm(16, (DEPTH, D_MIX, D_MODEL), D_MIX ** -0.5)
    norm2_w = 1.0 + nrm(17, (DEPTH, D_MODEL), 0.02)
    w_up = nrm(18, (DEPTH, D_MODEL, 2 * D_FF), D_MODEL ** -0.5)
    ffn_conv_w = nrm(19, (DEPTH, FFN_CONV, D_FF), FFN_CONV ** -0.5)
    ffn_conv_b = nrm(20, (DEPTH, D_FF), 0.02)
    w_down = nrm(21, (DEPTH, D_FF, D_MODEL), D_FF ** -0.5)
    final_norm_w = 1.0 + nrm(22, (D_MODEL,), 0.02)
    return {'x_prompt': x_prompt, 'x_sample': x_sample,
            'state_hgrn': state_hgrn, 'state_ssm': state_ssm,
            'state_conv_ssm': state_conv_ssm, 'state_conv_ffn': state_conv_ffn,
            'norm1_w': norm1_w, 'w_in': w_in, 'hgrn_lb': hgrn_lb, 'hgrn_norm_w': hgrn_norm_w,
            'ssm_conv_w': ssm_conv_w, 'ssm_conv_b': ssm_conv_b, 'ssm_dt_bias': ssm_dt_bias,
            'ssm_a_log': ssm_a_log, 'ssm_d': ssm_d, 'ssm_norm_w': ssm_norm_w, 'w_out': w_out,
            'norm2_w': norm2_w, 'w_up': w_up, 'ffn_conv_w': ffn_conv_w, 'ffn_conv_b': ffn_conv_b,
            'w_down': w_down, 'final_norm_w': final_norm_w}


def reference(x_prompt, x_sample, state_hgrn, state_ssm, state_conv_ssm, state_conv_ffn,
              norm1_w, w_in, hgrn_lb, hgrn_norm_w, ssm_conv_w, ssm_conv_b, ssm_dt_bias,
              ssm_a_log, ssm_d, ssm_norm_w, w_out, norm2_w, w_up, ffn_conv_w, ffn_conv_b,
              w_down, final_norm_w):
    layer_w = {'norm1_w': norm1_w, 'w_in': w_in, 'hgrn_norm_w': hgrn_norm_w,
               'ssm_conv_w': ssm_conv_w, 'ssm_conv_b': ssm_conv_b, 'ssm_dt_bias': ssm_dt_bias,
               'ssm_a_log': ssm_a_log, 'ssm_d': ssm_d, 'ssm_norm_w': ssm_norm_w, 'w_out': w_out,
               'norm2_w': norm2_w, 'w_up': w_up, 'ffn_conv_w': ffn_conv_w, 'ffn_conv_b': ffn_conv_b,
               'w_down': w_down}
    bp = x_prompt.shape[0]
    dtp = x_prompt.dtype
    z_hgrn = jnp.zeros((DEPTH, bp, HGRN_HEADS, HGRN_DK, HGRN_DV), dtp)
    z_ssm = jnp.zeros((DEPTH, bp, SSM_HEADS, SSM_HEAD_DIM, SSM_STATE), dtp)
    z_cs = jnp.zeros((DEPTH, bp, SSM_CONV - 1, CONV_DIM), dtp)
    z_cf = jnp.zeros((DEPTH, bp, FFN_CONV - 1, D_FF), dtp)
    y_prompt, hgrn_p, ssm_p, cs_p, cf_p = run_trunk(
        x_prompt, z_hgrn, z_ssm, z_cs, z_cf, hgrn_lb, layer_w, final_norm_w)
    y_sample, hgrn_s, ssm_s, cs_s, cf_s = run_trunk(
        x_sample, state_hgrn, state_ssm, state_conv_ssm, state_conv_ffn, hgrn_lb, layer_w, final_norm_w)
    return (y_prompt, y_sample, hgrn_p, hgrn_s, ssm_p, ssm_s, cs_p, cs_s, cf_p, cf_s)
```

```python
from contextlib import ExitStack

import numpy as np
import concourse.bass as bass
import concourse.mybir as mybir
from concourse.bass_utils import run_bass_kernel_spmd

F32 = mybir.dt.float32
BF16 = mybir.dt.bfloat16
ALU = mybir.AluOpType
AF = mybir.ActivationFunctionType

NCORES = 8
D = 1024
DIN = 6672
DFF = 2816
C_Q, C_F, C_I, C_G, C_Z, C_X, C_DT = 0, 1024, 2048, 3072, 4096, 5120, 6656
EPS = 1e-6
NEG = -30000.0

COMPUTE = ("pe", "act", "dve", "pool")
STREAMS = ("pe", "act", "dve", "pool", "sp")


class Op:
    __slots__ = ("idx", "eng", "fn", "deps", "dma", "signal", "ms", "sem", "val", "prev_val")

    def __init__(self, idx, eng, fn, dma):
        self.idx = idx
        self.eng = eng
        self.fn = fn
        self.dma = dma
        self.deps = set()
        self.signal = False
        self.ms = 0
        self.sem = None
        self.val = 0
        self.prev_val = 0


class Sched:
    def __init__(self):
        self.ops = []
        self.last_w = {}
        self.readers = {}
        self.last_on = {}
        self.pending_dma = []
        self.marks = {}
        self.rec = None

    def add(self, eng, fn, reads=(), writes=(), dma=False):
        if self.rec is not None:
            self.rec.append(lambda: self._add(eng, fn, reads, writes, dma))
            return None
        return self._add(eng, fn, reads, writes, dma)

    def record(self, f, *a):
        assert self.rec is None
        self.rec = []
        f(*a)
        r, self.rec = self.rec, None
        return r

    @staticmethod
    def merge(A, B):
        out, i, j = [], 0, 0
        while i < len(A) or j < len(B):
            if j >= len(B) or (i < len(A) and (i + 0.5) * len(B) <= (j + 0.5) * len(A)):
                out.append(A[i])
                i += 1
            else:
                out.append(B[j])
                j += 1
        return out

    def _add(self, eng, fn, reads=(), writes=(), dma=False):
        op = Op(len(self.ops), eng, fn, dma)
        deps = set()
        for k in reads:
            w = self.last_w.get(k)
            if w is not None:
                deps.add(w)
        for k in writes:
            w = self.last_w.get(k)
            if w is not None:
                deps.add(w)
            for r in self.readers.get(k, ()):
                deps.add(r)
        for k in writes:
            self.last_w[k] = op
            self.readers[k] = []
        for k in reads:
            if self.last_w.get(k) is not op:
                self.readers.setdefault(k, []).append(op)
        deps.discard(op)
        op.deps = {d for d in deps if not (d.eng == "pe" and eng == "pe" and not d.dma and not dma)}
        self.ops.append(op)
        if dma:
            self.pending_dma.append(op)
        else:
            self.last_on[eng] = op
        return op

    def mark(self, name):
        self.marks[name] = len(self.ops)

    def barrier(self):
        lasts = list(self.last_on.values())
        dmas = list(self.pending_dma)
        for st in STREAMS:
            op = Op(len(self.ops), st, None, False)
            op.deps = set(lasts) | set(dmas)
            self.ops.append(op)
        self.pending_dma = []
        self.last_w = {}
        self.readers = {}

    def final_wait(self, ops):
        op = Op(len(self.ops), "sp", None, False)
        op.deps = set(ops)
        self.ops.append(op)

    def emit(self, nc, es, n_dma_sems=20):
        for op in self.ops:
            for d in op.deps:
                if not d.dma:
                    d.signal = True
        cnt = {e: 0 for e in COMPUTE}
        for op in self.ops:
            if op.dma or op.fn is None:
                continue
            if op.signal:
                cnt[op.eng] += 1
                op.ms = cnt[op.eng]
        assert max(cnt.values()) < 60000, cnt
        esem = {e: es.enter_context(nc.semaphore("s_" + e)) for e in COMPUTE}
        dsem = {st: [es.enter_context(nc.semaphore("d_%s_%d" % (st, i))) for i in range(n)]
                for st, n in (("sp", 12), ("pool", 8))}
        dcount = {st: 0 for st in dsem}
        dvals = {}
        for op in self.ops:
            if op.dma:
                pool = dsem[op.eng]
                i = dcount[op.eng] % len(pool)
                dcount[op.eng] += 1
                op.sem = pool[i]
                op.prev_val = dvals.get((op.eng, i), 0)
                op.val = op.prev_val + 16
                assert op.val < 60000
                dvals[(op.eng, i)] = op.val
        ops = self.ops

        def run_stream(st, eng):
            known = {}

            def wait(sem, val):
                key = id(sem)
                if known.get(key, 0) >= val:
                    return
                eng.wait_ge(sem, val)
                known[key] = val

            for op in ops:
                if op.eng != st:
                    continue
                for d in sorted(op.deps, key=lambda o: o.idx):
                    if d.dma:
                        wait(d.sem, d.val)
                    else:
                        wait(esem[d.eng], d.ms)
                if op.fn is None:
                    continue
                if op.dma:
                    if op.prev_val:
                        wait(op.sem, op.prev_val)
                    op.fn(eng).then_inc(op.sem, 16)
                else:
                    ins = op.fn(eng)
                    if op.signal:
                        ins.then_inc(esem[st], 1)

        with nc.Block() as block:
            @block.tensor
            def _(e):
                run_stream("pe", e)

            @block.scalar
            def _(e):
                run_stream("act", e)

            @block.vector
            def _(e):
                run_stream("dve", e)

            @block.gpsimd
            def _(e):
                run_stream("pool", e)

            @block.sync
            def _(e):
                run_stream("sp", e)


DEBUG = False
CUT = None


def build_nc():
    nc = bass.Bass("TRN2", target_bir_lowering=False)
    S = Sched()

    def din(name, shape):
        return nc.dram_tensor(name, shape, F32, kind="ExternalInput").ap()

    def dout(name, shape):
        return nc.dram_tensor(name, shape, F32, kind="ExternalOutput").ap()

    xp = din("xp", [2048, D])
    xs = din("xs", [16, D])
    st_h = din("st_h", [16, 8, 128, 128])
    st_s = din("st_s", [16, 1024, 128])
    st_cs = din("st_cs", [16, 3, 1536])
    st_cf = din("st_cf", [16, 2, DFF])
    norm1_w = din("norm1_w", [D])
    w_in = din("w_in", [D, DIN])
    hgrn_lb = din("hgrn_lb", [2, 1024])
    hgrn_norm_w = din("hgrn_norm_w", [128])
    ssm_conv_w = din("ssm_conv_w", [4, 1536])
    ssm_conv_b = din("ssm_conv_b", [1536])
    ssm_dt_bias = din("ssm_dt_bias", [16])
    ssm_a_log = din("ssm_a_log", [16])
    ssm_d = din("ssm_d", [16])
    ssm_norm_w = din("ssm_norm_w", [1024])
    w_out = din("w_out", [2048, D])
    norm2_w = din("norm2_w", [D])
    w_up = din("w_up", [D, 2 * DFF])
    ffn_conv_w = din("ffn_conv_w", [3, DFF])
    ffn_conv_b = din("ffn_conv_b", [DFF])
    w_down = din("w_down", [DFF, D])
    final_norm_w = din("final_norm_w", [D])

    y_p = dout("y_p", [2048, D])
    y_s = dout("y_s", [16, D])
    hg_p = dout("hg_p", [8, 128, 128])
    hg_s = dout("hg_s", [16, 8, 128, 128])
    sm_p = dout("sm_p", [1024, 128])
    sm_s = dout("sm_s", [16, 1024, 128])
    cs_p = dout("cs_p", [3, 1536])
    cs_s = dout("cs_s", [16, 3, 1536])
    cf_p = dout("cf_p", [2, DFF])
    cf_s = dout("cf_s", [16, 2, DFF])

    out_ops = []
    dbg_mix = nc.dram_tensor("dbg_mix", [2, 128, 9, 2048], BF16, kind="ExternalOutput").ap() if DEBUG else None

    def mm(out, lhsT, rhs, start, stop, r, w):
        S.add("pe", lambda e: e.matmul(out, lhsT=lhsT, rhs=rhs, start=start, stop=stop), r, w)

    def tr(out, in_, ident, r, w):
        S.add("pe", lambda e: e.transpose(out=out, in_=in_, identity=ident), r, w)

    def act(out, in_, func, r, w, bias=None, scale=None, accum=None):
        kw = {}
        if bias is not None:
            kw["bias"] = bias
        if scale is not None:
            kw["scale"] = scale
        if accum is not None:
            kw["accum_out"] = accum
        S.add("act", lambda e: e.activation(out=out, in_=in_, func=func, **kw), r, w)

    def ts(eng, out, in0, s1, s2, op0, op1, r, w):
        if op1 is None:
            S.add(eng, lambda e: e.tensor_scalar(out=out, in0=in0, scalar1=s1, scalar2=None, op0=op0), r, w)
        else:
            S.add(eng, lambda e: e.tensor_scalar(out=out, in0=in0, scalar1=s1, scalar2=s2, op0=op0, op1=op1), r, w)

    def stt(out, in0, scalar, in1, op0, op1, r, w):
        S.add("dve", lambda e: e.scalar_tensor_tensor(out=out, in0=in0, scalar=scalar, in1=in1, op0=op0, op1=op1), r, w)

    def tt(eng, out, in0, in1, op, r, w):
        S.add(eng, lambda e: e.tensor_tensor(out=out, in0=in0, in1=in1, op=op), r, w)

    def cp(eng, out, in_, r, w):
        if eng == "act":
            S.add("act", lambda e: e.copy(out=out, in_=in_), r, w)
        else:
            S.add(eng, lambda e: e.tensor_copy(out=out, in_=in_), r, w)

    def mset(eng, ap, val, w):
        S.add(eng, lambda e: e.memset(ap, val), (), w)

    def dma(q, out, in_, r, w, slow=False):
        if slow:
            return S.add(q, lambda e: e.dma_start(out=out, in_=in_, allow_slow_non_contiguous=True), r, w, dma=True)
        return S.add(q, lambda e: e.dma_start(out=out, in_=in_), r, w, dma=True)

    def store(out, in_, r, slow=False):
        if S.rec is not None:
            if slow:
                fn = lambda e: e.dma_start(out=out, in_=in_, allow_slow_non_contiguous=True)
            else:
                fn = lambda e: e.dma_start(out=out, in_=in_)
            S.rec.append(lambda: out_ops.append(S._add("sp", fn, r, (), True)))
            return
        out_ops.append(dma("sp", out, in_, r, (), slow=slow))

    with ExitStack() as es:
        uniq = {"n": 0}

        def sbt(stack, name, shape, dt):
            uniq["n"] += 1
            return stack.enter_context(nc.sbuf_tensor("%s_%d" % (name, uniq["n"]), shape, dt))

        psb = [es.enter_context(nc.psum_tensor("psb%d" % i, [128, 512], F32)) for i in range(8)]
        bank_state = {"next": 0, "reserved": set()}

        def newbank():
            pool = bank_state.get("pool")
            if pool is not None:
                key = "next_%s" % (pool,)
                while True:
                    b = pool[bank_state.get(key, 0) % len(pool)]
                    bank_state[key] = bank_state.get(key, 0) + 1
                    if b not in bank_state["reserved"]:
                        return b
            while True:
                b = bank_state["next"] % 8
                bank_state["next"] += 1
                if b not in bank_state["reserved"]:
                    return b

        def with_pool(pool, f, *a):
            old = bank_state.get("pool")
            bank_state["pool"] = pool
            try:
                return S.record(f, *a)
            finally:
                bank_state["pool"] = old

        def pk(b):
            return "ps%d" % b

        def hold(*bs):
            bank_state["reserved"].update(bs)

        def release(*bs):
            bank_state["reserved"].difference_update(bs)

        ones_f = sbt(es, "ones_f", [128, 128], F32)
        zeros_f = sbt(es, "zeros_f", [128, 128], F32)
        ident_f = sbt(es, "ident_f", [128, 128], F32)
        ident_bf = sbt(es, "ident_bf", [128, 128], BF16)
        U_f = sbt(es, "U_f", [128, 128], F32)
        maskBD = sbt(es, "maskBD", [128, 128], F32)
        negm4 = sbt(es, "negm4", [128, 4, 128], BF16)
        m01 = sbt(es, "m01", [128, 512], F32)
        R48 = sbt(es, "R48", [48, 16, 128], F32)
        L48 = sbt(es, "L48", [48, 128], F32)
        gwb = sbt(es, "gwb", [128, 128], F32)
        dtb_b = sbt(es, "dtb_b", [128, 16], F32)
        A_b = sbt(es, "A_b", [128, 16], F32)
        D_b = sbt(es, "D_b", [128, 16], F32)
        lbraw = sbt(es, "lbraw", [128, 2, 8], F32)
        lb_col = sbt(es, "lb_col", [128, 8], F32)
        c0_col = sbt(es, "c0_col", [128, 8], F32)
        c1_col = sbt(es, "c1_col", [128, 8], F32)
        gwbh = sbt(es, "gwbh", [128, 128], F32)
        half_f = sbt(es, "half_f", [128, 1], F32)
        tgS = sbt(es, "tgS", [16, 128], F32)
        oml_col = sbt(es, "oml_col", [128, 8], F32)
        cw_col = sbt(es, "cw_col", [128, 12, 4], F32)
        cb_col = sbt(es, "cb_col", [128, 12], F32)
        fw_col = sbt(es, "fw_col", [128, 22, 3], F32)
        fb_col = sbt(es, "fb_col", [128, 22], F32)
        S_h = sbt(es, "S_h", [128, 8, 128], F32)
        ST = sbt(es, "ST", [128, 1024], F32)
        ST_bf = sbt(es, "ST_bf", [128, 1024], BF16)
        halo_s = sbt(es, "halo_s", [128, 12, 3], F32)
        halo_f = sbt(es, "halo_f", [128, 22, 2], F32)
        hT = sbt(es, "hT", [128, 8, 1040], BF16)
        big = sbt(es, "big", [128, 9, 1024], F32)
        mix = big[:].bitcast(BF16)
        x2 = big
        wst = [sbt(es, "wst%d" % i, [128, 8, 512], BF16) for i in range(2)]
        wdt = sbt(es, "wdt", [128, 8, 16], BF16)
        wslot = {"n": 0}
        qS = sbt(es, "qS", [128, 8, 16], F32)
        fS = sbt(es, "fS", [128, 8, 16], F32)
        kS = sbt(es, "kS", [128, 8, 16], F32)
        vS = sbt(es, "vS", [16, 1024], F32)
        gsS = sbt(es, "gsS", [16, 1024], F32)
        tmpS = sbt(es, "tmpS", [128, 32], F32)
        xrawS = sbt(es, "xrawS", [128, 12, 16], F32)
        zS = sbt(es, "zS", [16, 1024], BF16)
        dtS_raw = sbt(es, "dtS_raw", [16, 16], F32)

        def next_wslot():
            i = wslot["n"] % 2
            wslot["n"] += 1
            return i

        def aff(out, in_, pattern, cmp, fill, base, cm, r, w):
            S.add("pool", lambda e: e.affine_select(out=out, in_=in_, pattern=pattern, compare_op=cmp, fill=fill,
                                                    base=base, channel_multiplier=cm), r, w)

        mset("pool", ones_f[:], 1.0, ["ones_f"])
        mset("pool", zeros_f[:], 0.0, ["zeros_f"])
        aff(ident_f[:], ones_f[:], [[-1, 128]], ALU.is_equal, 0.0, 0, 1, ["ones_f"], ["ident_f"])
        cp("pool", ident_bf[:], ident_f[:], ["ident_f"], ["ident_bf"])
        aff(U_f[:], ones_f[:], [[1, 128]], ALU.is_ge, 0.0, 0, -1, ["ones_f"], ["U_f"])
        cp("pool", maskBD[:], U_f[:], ["U_f"], ["maskBD"])
        mset("pool", maskBD[0:64, 64:128], 0.0, ["maskBD"])
        negf = sbt(es, "negf", [128, 128], F32)
        aff(negf[:], zeros_f[:], [[1, 128]], ALU.is_ge, NEG, 0, -1, ["zeros_f"], ["negf"])
        for i in range(4):
            cp("pool", negm4[:, i, :], negf[:], ["negf"], ["negm4"])
        mset("pool", m01[:], 1.0, ["m01"])
        mset("pool", m01[:].rearrange("p (c t) -> p c t", t=64)[:, :, 0:1], 0.0, ["m01"])
        mset("pool", R48[:], 0.0, ["R48c"])
        mset("pool", L48[:], 0.0, ["L48c"])
        mset("pool", L48[0:16, :], 1.0, ["L48c"])
        negones = sbt(es, "negones", [128, 1], F32)
        mset("pool", negones[:], -1.0, ["negones"])
        aff(R48[32:48, :, :], negones[32:48, 0:1].unsqueeze(1).to_broadcast([16, 16, 128]), [[-1, 16], [0, 128]],
            ALU.is_equal, 0.0, 0, 1, ["negones", "R48c"], ["R48c"])
        mset("pool", ST[:], 0.0, ["ST"])
        mset("pool", ST_bf[:], 0.0, ["ST_bf"])
        mset("pool", S_h[:], 0.0, ["S_h"])
        mset("pool", halo_s[:], 0.0, ["halo_s"])
        mset("pool", halo_f[:], 0.0, ["halo_f"])

        dma("sp", gwb[:], hgrn_norm_w.partition_broadcast(128), (), ["gwb"])
        dma("sp", dtb_b[:], ssm_dt_bias.partition_broadcast(128), (), ["dtb_b"])
        dma("sp", A_b[:], ssm_a_log.partition_broadcast(128), (), ["A_b"])
        dma("sp", D_b[:], ssm_d.partition_broadcast(128), (), ["D_b"])
        for r_ in range(2):
            dma("sp", lbraw[:, r_, :], hgrn_lb[r_].rearrange("(h p) -> p h", p=128), (), ["lbraw"], slow=True)
        for j_ in range(4):
            dma("sp", cw_col[:, :, j_], ssm_conv_w[j_].rearrange("(c p) -> p c", p=128), (), ["cw_col"], slow=True)
        dma("sp", cb_col[:], ssm_conv_b.rearrange("(c p) -> p c", p=128), (), ["cb_col"], slow=True)
        for j_ in range(3):
            dma("sp", fw_col[:, :, j_], ffn_conv_w[j_].rearrange("(c p) -> p c", p=128), (), ["fw_col"], slow=True)
        dma("sp", fb_col[:], ffn_conv_b.rearrange("(c p) -> p c", p=128), (), ["fb_col"], slow=True)
        dma("pool", wdt[:], w_in[:, C_DT:C_DT + 16].rearrange("(k p) n -> p k n", p=128), (), ["wdt"])
        act(A_b[:], A_b[:], AF.Exp, ["A_b"], ["A_b"])
        ts("dve", A_b[:], A_b[:], -1.0, None, ALU.mult, None, ["A_b"], ["A_b"])
        tt("dve", lb_col[:], lbraw[:, 0, :], lbraw[:, 1, :], ALU.subtract, ["lbraw"], ["lb_col"])
        act(oml_col[:], lb_col[:], AF.Sigmoid, ["lb_col"], ["oml_col"], scale=-1.0)
        act(lb_col[:], lb_col[:], AF.Sigmoid, ["lb_col"], ["lb_col"])
        ts("dve", c1_col[:], oml_col[:], 0.5, None, ALU.mult, None, ["oml_col"], ["c1_col"])
        tt("dve", c0_col[:], lb_col[:], c1_col[:], ALU.add, ["lb_col", "c1_col"], ["c0_col"])
        ts("dve", gwbh[:], gwb[:], 0.5, None, ALU.mult, None, ["gwb"], ["gwbh"])
        mset("pool", half_f[:], 0.5, ["half_f"])

        def rstd_small(ssq, n_feat, r, w):
            act(ssq, ssq, AF.Ln, r, w, bias=EPS, scale=1.0 / n_feat)
            act(ssq, ssq, AF.Exp, w, w, scale=-0.5)

        def load_w(dst, src, r, w):
            return dma("pool", dst, src, r, w)

        def wq(slot, a, b):
            return ["wst%dq%d" % (slot, j) for j in range(a // 128, (b - 1) // 128 + 1)]

        def wv(src, c0, n):
            return src[:, c0:c0 + n].rearrange("(k p) n -> p k n", p=128)

        job_list = []
        for sb_ in range(2):
            for wb in range(3):
                job_list.append((("ssdx", sb_, wb), [(0, 512, wv(w_in, C_X + wb * 512, 512))]))
            for wb in range(2):
                job_list.append((("ssdz", sb_, wb), [(0, 512, wv(w_in, C_Z + wb * 512, 512))]))
            for h_ in range(8):
                job_list.append((("hgrn", sb_, h_), [(i * 128, 128, wv(w_in, c0 + h_ * 128, 128))
                                                     for i, c0 in enumerate((C_Q, C_F, C_I, C_G))]))
            for fg_ in range(2):
                for pr in range(6):
                    ft0 = fg_ * 11 + pr * 2
                    n = 256 if pr < 5 else 128
                    job_list.append((("ffn", sb_, fg_, pr), [(0, n, wv(w_up, ft0 * 128, n)),
                                                             (256, n, wv(w_up, DFF + ft0 * 128, n))]))
        job_index = {k: i for i, (k, _) in enumerate(job_list)}
        wjobs = {}
        jstate = {"issued": 0}

        def w_use(key):
            idx = job_index[key]
            while jstate["issued"] <= min(idx + 1, len(job_list) - 1):
                k_, parts = job_list[jstate["issued"]]
                slot = next_wslot()
                for (c0, n, src) in parts:
                    load_w(wst[slot][:, :, c0:c0 + n], src, (), wq(slot, c0, c0 + n))
                wjobs[k_] = slot
                jstate["issued"] += 1
            return wjobs[key]

        SAMPLE = True

        def ttr(out, in0, in1, accum, r, w):
            S.add("dve", lambda e: e.scalar_tensor_tensor(out=out, in0=in0, scalar=1.0, in1=in1, op0=ALU.mult,
                                                          op1=ALU.mult, accum_out=accum), r, w)

        def build_selS(stack):
            selS = sbt(stack, "selS", [16, 16, 128], F32)
            aff(selS[:], ones_f[0:16, 0:1].unsqueeze(1).to_broadcast([16, 16, 128]), [[-1, 16], [0, 128]],
                ALU.is_equal, 0.0, 0, 1, ["ones_f"], ["selS"])
            return selS

        def sample_hgrn(ss):
            if True:
                selS = build_selS(ss)
                eyeS = sbt(ss, "eyeS", [128, 16, 16], F32)
                Qm = sbt(ss, "Qm", [128, 8, 256], F32)
                Sin = [sbt(ss, "SinH%d" % i, [128, 8, 128], F32) for i in range(3)]
                tmpH = [sbt(ss, "tmpH%d" % i, [128, 8, 128], F32) for i in range(2)]
                gwS = sbt(ss, "gwS", [16, 1024], F32)
                junkS = sbt(ss, "junkS", [16, 128], F32)
                ssqS = sbt(ss, "ssqS", [16, 8], F32)
                aff(eyeS[:], half_f[:, 0:1].unsqueeze(1).to_broadcast([128, 16, 16]), [[1, 16], [-1, 16]],
                    ALU.is_equal, 0.0, 0, 0, ["half_f"], ["eyeS"])
                for h in range(8):
                    tt("dve", Qm[:, h, :].rearrange("p (j c) -> p j c", c=16),
                       qS[:, h, :].unsqueeze(1).to_broadcast([128, 16, 16]), eyeS[:], ALU.mult, ["qS", "eyeS"], ["Qm"])
                bo_ = [newbank(), newbank()]
                bank_state["reserved"].update(bo_)
                def h_s1(j):
                    sl = j % 3
                    sk = "SinH%d" % sl
                    tj = j % 2
                    dma("sp", Sin[sl][:], st_h[j].rearrange("h k v -> k h v"), (), [sk])
                    bvb = [newbank(), newbank()]
                    for half in range(2):
                        mm(psb[bvb[half]][:, :], selS[:, j, :], vS[:, half * 512:(half + 1) * 512], True, True,
                           ["selS", "vS"], [pk(bvb[half])])
                    for h in range(8):
                        act(tmpH[tj][:, h, :], psb[bvb[h // 4]][:, (h % 4) * 128:(h % 4 + 1) * 128], AF.Identity,
                            [pk(bvb[h // 4]), "kS"], ["tmpH%d" % tj], scale=kS[:, h, j:j + 1])

                def h_s2(j):
                    sl = j % 3
                    sk = "SinH%d" % sl
                    tj = j % 2
                    tt("pool", Sin[sl][:], Sin[sl][:], fS[:, :, j:j + 1].to_broadcast([128, 8, 128]), ALU.mult,
                       [sk, "fS"], [sk])
                    tt("dve", Sin[sl][:], Sin[sl][:], tmpH[tj][:], ALU.add, [sk, "tmpH%d" % tj], [sk])
                    for h in range(8):
                        Sv = Sin[sl][:, h, :]
                        mm(psb[bo_[h // 4]][0:16, (h % 4) * 128:(h % 4 + 1) * 128], Qm[:, h, j * 16:(j + 1) * 16], Sv,
                           j == 0 and h % 4 == 0, j == 15 and h % 4 == 3, ["Qm", sk], [pk(bo_[h // 4])])
                    store(hg_s[j].rearrange("h k v -> k h v"), Sin[sl][:], [sk])

                h_s1(0)
                for j in range(16):
                    if j + 1 < 16:
                        h_s1(j + 1)
                    h_s2(j)
                for h in range(8):
                    oc = psb[bo_[h // 4]][0:16, (h % 4) * 128:(h % 4 + 1) * 128]
                    act(junkS[:], oc, AF.Square, [pk(bo_[h // 4])], ["junkS", "ssqS"], accum=ssqS[:, h:h + 1])
                rstd_small(ssqS[:], 128, ["ssqS"], ["ssqS"])
                tt("dve", gwS[:].rearrange("p (h v) -> p h v", v=128), gsS[:].rearrange("p (h v) -> p h v", v=128),
                   gwbh[0:16, :].unsqueeze(1).to_broadcast([16, 8, 128]), ALU.mult, ["gsS", "gwbh"], ["gwS"])
                for h in range(8):
                    oc = psb[bo_[h // 4]][0:16, (h % 4) * 128:(h % 4 + 1) * 128]
                    stt(mix[0:16, 8, h * 128:(h + 1) * 128], oc, ssqS[:, h:h + 1], gwS[:, h * 128:(h + 1) * 128],
                        ALU.mult, ALU.mult, [pk(bo_[h // 4]), "ssqS", "gwS"], ["mix8"])
                bank_state["reserved"].difference_update(bo_)

        def sample_ssd(ss):
            if True:
                selS = build_selS(ss)
                E = sbt(ss, "E", [16, 8, 128], F32)
                stcs_t = [sbt(ss, "stcs_t%d" % i, [16, 3, 128], F32) for i in range(2)]
                CST = [sbt(ss, "CST%d" % i, [128, 3, 16], F32) for i in range(2)]
                accS = [sbt(ss, "accS%d" % i, [128, 16], F32) for i in range(2)]
                xcS = sbt(ss, "xcS", [128, 12, 16], F32)
                xtokS = [sbt(ss, "xtokS%d" % i, [16, 128], F32) for i in range(2)]
                dtS = sbt(ss, "dtS", [16, 32], F32)
                dtdecT = sbt(ss, "dtdecT", [16, 34], F32)
                dE = sbt(ss, "dE", [128, 8, 34], F32)
                XdtE = sbt(ss, "XdtE", [128, 8, 16], F32)
                BCtok = sbt(ss, "BCtok", [16, 512], F32)
                Sin = [sbt(ss, "SinS%d" % i, [128, 8, 128], F32) for i in range(3)]
                tmpP = [sbt(ss, "tmpP%d" % i, [128, 8, 128], F32) for i in range(2)]
                junkP = sbt(ss, "junkP", [128, 128], F32)
                yS = sbt(ss, "yS", [128, 8, 16], F32)
                dcol = sbt(ss, "dcol", [16, 1], F32)
                ytokS = sbt(ss, "ytokS", [16, 1024], F32)
                junkS = sbt(ss, "junkS2", [16, 512], F32)
                ssqS = sbt(ss, "ssqS2", [16, 2], F32)
                snwS = sbt(ss, "snwS", [16, 1024], F32)
                dma("sp", snwS[:], ssm_norm_w.partition_broadcast(16), (), ["snwS"])
                dma("sp", dcol[:], ssm_d.rearrange("(h o) -> h o", o=1), (), ["dcol"])
                for a_ in range(2):
                    aff(E[:, :, a_ * 64:(a_ + 1) * 64], ones_f[0:16, 0:1].unsqueeze(1).to_broadcast([16, 8, 64]),
                        [[-2, 8], [0, 64]], ALU.is_equal, 0.0, -a_, 1, ["ones_f"], ["E"])
                mset("pool", yS[:], 0.0, ["yS"])
                store(cs_s[:, 0:2, :], st_cs[:, 1:3, :], [])
                for ft in range(12):
                    ri = ft % 2
                    dma("sp", stcs_t[ri][:], st_cs[:, :, ft * 128:(ft + 1) * 128], (), ["stcs_t%d" % ri])
                    b = newbank()
                    for r_ in range(3):
                        tr(psb[b][:, r_ * 16:(r_ + 1) * 16], stcs_t[ri][:, r_, :], ident_f[0:16, 0:16],
                           ["stcs_t%d" % ri, "ident_f"], [pk(b)])
                    cp("act", CST[ri][:].rearrange("p r j -> p (r j)"), psb[b][:, 0:48], [pk(b)], ["CST%d" % ri])
                    ts("dve", accS[ri][:], xrawS[:, ft, :], cw_col[:, ft, 3:4], cb_col[:, ft:ft + 1], ALU.mult, ALU.add,
                       ["xrawS", "cw_col", "cb_col"], ["accS%d" % ri])
                    for jj in (2, 1, 0):
                        stt(accS[ri][:], CST[ri][:, jj, :], cw_col[:, ft, jj:jj + 1], accS[ri][:], ALU.mult, ALU.add,
                            ["CST%d" % ri, "cw_col", "accS%d" % ri], ["accS%d" % ri])
                    act(xcS[:, ft, :], accS[ri][:], AF.Silu, ["accS%d" % ri], ["xcS"])
                    b2 = newbank()
                    tr(psb[b2][0:16, 0:128], xrawS[:, ft, :], ident_f[:], ["xrawS", "ident_f"], [pk(b2)])
                    cp("act", xtokS[ri][:], psb[b2][0:16, 0:128], [pk(b2)], ["xtokS%d" % ri])
                    store(cs_s[:, 2, ft * 128:(ft + 1) * 128], xtokS[ri][:], ["xtokS%d" % ri])
                tt("dve", dtS[:, 0:16], dtS_raw[:], dtb_b[0:16, :], ALU.add, ["dtS_raw", "dtb_b"], ["dtS"])
                act(dtS[:, 0:16], dtS[:, 0:16], AF.Exp, ["dtS"], ["dtS"])
                act(dtS[:, 0:16], dtS[:, 0:16], AF.Ln, ["dtS"], ["dtS"], bias=1.0)
                tt("dve", dtS[:, 16:32], dtS[:, 0:16], A_b[0:16, :], ALU.mult, ["dtS", "A_b"], ["dtS"])
                act(dtS[:, 16:32], dtS[:, 16:32], AF.Exp, ["dtS"], ["dtS"])
                b = newbank()
                for i in range(2):
                    tr(psb[b][0:16, i * 16:(i + 1) * 16], dtS[:, i * 16:(i + 1) * 16], ident_f[0:16, 0:16],
                       ["dtS", "ident_f"], [pk(b)])
                cp("act", dtdecT[:, 0:32], psb[b][0:16, 0:32], [pk(b)], ["dtdecT"])
                cp("dve", dtdecT[:, 32:34], dcol[:, 0:1].to_broadcast([16, 2]), ["dcol", "dtdecT"], ["dtdecT"])
                b = newbank()
                for c in range(8):
                    mm(psb[b][:, c * 34:(c + 1) * 34], E[:, c, :], dtdecT[:], True, True, ["E", "dtdecT"], [pk(b)])
                cp("act", dE[:].rearrange("p c x -> p (c x)"), psb[b][:, 0:272], [pk(b)], ["dE"])
                tt("dve", XdtE[:], xcS[:, 0:8, :], dE[:, :, 0:16], ALU.mult, ["xcS", "dE"], ["XdtE"])
                b = newbank()
                for i in range(4):
                    tr(psb[b][0:16, i * 128:(i + 1) * 128], xcS[:, 8 + i, :], ident_f[:], ["xcS", "ident_f"], [pk(b)])
                cp("act", BCtok[:], psb[b][0:16, :], [pk(b)], ["BCtok"])
                sbank = {}

                def s_s1(j):
                    sl = j % 3
                    sk = "SinS%d" % sl
                    tj = j % 2
                    dma("sp", Sin[sl][:], st_s[j].rearrange("(c q) n -> q c n", q=128), (), [sk])
                    b = newbank()
                    hold(b)
                    sbank[j] = b
                    mm(psb[b][:, :], selS[:, j, :], BCtok[:], True, True, ["selS", "BCtok"], [pk(b)])
                    for c in range(8):
                        g = c // 4
                        act(tmpP[tj][:, c, :], psb[b][:, g * 128:(g + 1) * 128], AF.Identity, [pk(b), "XdtE"],
                            ["tmpP%d" % tj], scale=XdtE[:, c, j:j + 1])

                def s_s2(j):
                    sl = j % 3
                    sk = "SinS%d" % sl
                    tj = j % 2
                    b = sbank[j]
                    tt("pool", Sin[sl][:], Sin[sl][:], dE[:, :, 16 + j:17 + j].to_broadcast([128, 8, 128]), ALU.mult,
                       [sk, "dE"], [sk])
                    tt("dve", Sin[sl][:], Sin[sl][:], tmpP[tj][:], ALU.add, [sk, "tmpP%d" % tj], [sk])
                    tt("dve", tmpP[tj][:].rearrange("p (g i) n -> p g i n", g=2),
                       Sin[sl][:].rearrange("p (g i) n -> p g i n", g=2),
                       psb[b][:, 256:512].rearrange("p (g n) -> p g n", g=2).unsqueeze(2).to_broadcast([128, 2, 4, 128]),
                       ALU.mult, [sk, pk(b), "tmpP%d" % tj], ["tmpP%d" % tj])
                    release(b)
                    S.add("dve", lambda e, o=yS[:, :, j], i_=tmpP[tj][:]: e.tensor_reduce(
                        out=o, in_=i_, axis=mybir.AxisListType.X, op=ALU.add), ["tmpP%d" % tj, "yS"], ["yS"])
                    store(sm_s[j].rearrange("(c q) n -> q c n", q=128), Sin[sl][:], [sk])

                s_s1(0)
                for j in range(16):
                    if j + 1 < 16:
                        s_s1(j + 1)
                    s_s2(j)
                for c in range(8):
                    stt(yS[:, c, :], xcS[:, c, :], dE[:, c, 32:33], yS[:, c, :], ALU.mult, ALU.add,
                        ["xcS", "dE", "yS"], ["yS"])
                for half in range(2):
                    b = newbank()
                    for i in range(4):
                        tr(psb[b][0:16, i * 128:(i + 1) * 128], yS[:, half * 4 + i, :], ident_f[:], ["yS", "ident_f"], [pk(b)])
                    tt("dve", ytokS[:, half * 512:(half + 1) * 512], psb[b][0:16, :], zS[:, half * 512:(half + 1) * 512],
                       ALU.mult, [pk(b), "zS"], ["ytokS"])
                for g in range(2):
                    act(junkS[:], ytokS[:, g * 512:(g + 1) * 512], AF.Square, ["ytokS"], ["junkS2", "ssqS2"],
                        accum=ssqS[:, g:g + 1])
                rstd_small(ssqS[:], 512, ["ssqS2"], ["ssqS2"])
                for g in range(2):
                    stt(mix[0:16, 8, 1024 + g * 512:1024 + (g + 1) * 512], ytokS[:, g * 512:(g + 1) * 512],
                        ssqS[:, g:g + 1], snwS[:, g * 512:(g + 1) * 512], ALU.mult, ALU.mult,
                        ["ytokS", "ssqS2", "snwS"], ["mix8"])

        def ffn_sample_tile(ft, ftl, ri, stcf_t, CFT, grawS, gaccS, gtokS, actT, bvvS):
            if not SAMPLE:
                return
            if ft == 0:
                store(cf_s[:, 0, :], st_cf[:, 1, :], [])
            dma("sp", stcf_t[ri][:], st_cf[:, :, ft * 128:(ft + 1) * 128], (), ["stcf_t%d" % ri])
            b = newbank()
            for r_ in range(2):
                tr(psb[b][:, r_ * 16:(r_ + 1) * 16], stcf_t[ri][:, r_, :], ident_f[0:16, 0:16],
                   ["stcf_t%d" % ri, "ident_f"], [pk(b)])
            cp("act", CFT[ri][:].rearrange("p r j -> p (r j)"), psb[b][:, 0:32], [pk(b)], ["CFT%d" % ri])
            ts("dve", gaccS[ri][:], grawS[ri][:], fw_col[:, ft, 2:3], fb_col[:, ft:ft + 1], ALU.mult, ALU.add,
               ["grawS%d" % ri, "fw_col", "fb_col"], ["gaccS%d" % ri])
            for jj in (1, 0):
                stt(gaccS[ri][:], CFT[ri][:, jj, :], fw_col[:, ft, jj:jj + 1], gaccS[ri][:], ALU.mult, ALU.add,
                    ["CFT%d" % ri, "fw_col", "gaccS%d" % ri], ["gaccS%d" % ri])
            act(gaccS[ri][:], gaccS[ri][:], AF.Silu, ["gaccS%d" % ri], ["gaccS%d" % ri])
            tt("dve", actT[:, ftl, 1024:1040], gaccS[ri][:], psb[bvvS][:, 0:16], ALU.mult,
               ["gaccS%d" % ri, pk(bvvS)], ["actT%d" % ftl])
            b2 = newbank()
            tr(psb[b2][0:16, 0:128], grawS[ri][:], ident_f[:], ["grawS%d" % ri, "ident_f"], [pk(b2)])
            cp("act", gtokS[ri][:], psb[b2][0:16, 0:128], [pk(b2)], ["gtokS%d" % ri])
            store(cf_s[:, 1, ft * 128:(ft + 1) * 128], gtokS[ri][:], ["gtokS%d" % ri])

        for sb in range(2):
            tok0 = sb * 1024
            ntile = 9 if sb == 1 else 8
            ncols = 1040 if sb == 1 else 1024
            blocks = [(0, 512), (512, 512)] + ([(1024, 16)] if sb == 1 else [])

            def rows_of(t):
                return 16 if t == 8 else 128

            S.mark("A%d" % sb)
            with ExitStack() as sa:
                xt = [sbt(sa, "xt%d" % i, [128, D], F32) for i in range(2)]
                sq = sbt(sa, "sq", [128, D], F32)
                hn = [sbt(sa, "hn%d" % i, [128, D], BF16) for i in range(2)]
                ssA = sbt(sa, "ssA", [128, 2], F32)
                w1b = sbt(sa, "w1b", [128, D], F32)
                dma("sp", w1b[:], norm1_w.partition_broadcast(128), (), ["w1b"])
                def a_s1(t):
                    i = t % 2
                    rw = rows_of(t)
                    src = xs if t == 8 else xp[tok0 + t * 128: tok0 + (t + 1) * 128, :]
                    dma("sp", xt[i][0:rw, :], src, (), ["xt%d" % i])
                    act(sq[0:rw, :], xt[i][0:rw, :], AF.Square, ["xt%d" % i], ["sq", "ssA%d" % i], accum=ssA[0:rw, i:i + 1])
                    rstd_small(ssA[0:rw, i:i + 1], D, ["ssA%d" % i], ["ssA%d" % i])

                def a_s2(t):
                    i = t % 2
                    rw = rows_of(t)
                    stt(hn[i][0:rw, :], xt[i][0:rw, :], ssA[0:rw, i:i + 1], w1b[0:rw, :], ALU.mult, ALU.mult,
                        ["xt%d" % i, "ssA%d" % i, "w1b"], ["hn%d" % i])
                    b = newbank()
                    pT = psb[b][:].bitcast(BF16)
                    for k in range(8):
                        tr(pT[:, k * 128:k * 128 + rw], hn[i][0:rw, k * 128:(k + 1) * 128], ident_bf[0:rw, 0:rw],
                           ["hn%d" % i, "ident_bf"], [pk(b)])
                    cp("act", hT[:, :, t * 128:t * 128 + rw],
                       pT[:, 0:1024].rearrange("p (k c) -> p k c", k=8)[:, :, 0:rw], [pk(b)], ["hT"])

                a_s1(0)
                for t in range(ntile):
                    if t + 1 < ntile:
                        a_s1(t + 1)
                    a_s2(t)
            S.barrier()

            def wview(c0, n):
                return w_in[:, c0:c0 + n].rearrange("(k p) n -> p k n", p=128)

            S.mark("B1_%d" % sb)
            with ExitStack() as s1:
                xc = sbt(s1, "xc", [128, 12, 1024], BF16)
                zs = sbt(s1, "zs", [128, 8, 1024], BF16)
                xraw = [sbt(s1, "xraw%d" % i, [128, 3 + 1024], F32) for i in range(2)]
                cacc = [sbt(s1, "cacc%d" % i, [128, 1024], F32) for i in range(2)]
                dtraw = sbt(s1, "dtraw", [128, 8, 16], F32)
                dtv = sbt(s1, "dtv", [128, 8, 16], F32)
                a_ext = sbt(s1, "a_ext", [128, 8, 48], F32)
                Xtok = sbt(s1, "Xtok", [128, 1024], BF16)
                Xdt = sbt(s1, "Xdt", [128, 1024], BF16)
                Xd_ = [sbt(s1, "Xd%d" % i, [128, 1024], BF16) for i in range(2)]
                XD = sbt(s1, "XD", [128, 1024], BF16)
                Btok_ = [sbt(s1, "Btok%d" % i, [128, 256], BF16) for i in range(2)]
                ex3_ = [sbt(s1, "ex3_%d" % i, [128, 48], F32) for i in range(2)]
                AcsT_sb = sbt(s1, "AcsT_sb", [16, 128], F32)
                GT_sb = sbt(s1, "GT_sb", [128, 256], F32)
                Lx = [sbt(s1, "Lx%d" % i, [128, 512], F32) for i in range(2)]
                M_sb = sbt(s1, "M_sb", [128, 16, 128], BF16)
                ytmp = sbt(s1, "ytmp", [128, 1024], F32)
                sqs = sbt(s1, "sqs", [128, 512], F32)
                ssq2 = sbt(s1, "ssq2", [128, 2], F32)
                snwb = sbt(s1, "snwb", [128, 1024], F32)
                dma("sp", snwb[:], ssm_norm_w.partition_broadcast(128), (), ["snwb"])

                xsteps = [(ft, bi, c0b) for ft in range(12) for bi, (c0b, nb) in enumerate(blocks) if nb == 512]
                xst = {}

                def x_s1(n):
                    ft, bi, c0b = xsteps[n]
                    ri = ft % 2
                    xr = xraw[ri]
                    j = ft % 4
                    if bi == 0:
                        if j == 0:
                            xst["slot"] = w_use(("ssdx", sb, ft // 4))
                        cp("act", xr[:, 0:3], halo_s[:, ft, :], ["halo_s%d" % ft], ["xraw%dh" % ri])
                    slot = xst["slot"]
                    todo = [(c0b, 512)] + ([(1024, 16)] if (bi == 1 and sb == 1) else [])
                    for (c0, nb) in todo:
                        b = newbank()
                        for k in range(8):
                            mm(psb[b][:, 0:nb], wst[slot][:, k, j * 128:(j + 1) * 128], hT[:, k, c0:c0 + nb],
                               k == 0, k == 7, wq(slot, j * 128, (j + 1) * 128) + ["hT"], [pk(b)])
                        if nb == 16:
                            cp("act", xrawS[:, ft, :], psb[b][:, 0:16], [pk(b)], ["xrawS"])
                        else:
                            cp("act", xr[:, 3 + c0:3 + c0 + 512], psb[b][:, 0:512], [pk(b)], ["xraw%db%d" % (ri, bi)])
                    if bi == 1:
                        cp("act", halo_s[:, ft, :], xr[:, 1024:1027], ["xraw%db1" % ri], ["halo_s%d" % ft])

                def x_s2(n):
                    ft, bi, c0b = xsteps[n]
                    ri = ft % 2
                    xr = xraw[ri]
                    cb_ = cacc[ri][:, c0b:c0b + 512]
                    ck = "cacc%d_%d" % (ri, bi)
                    xk = ["xraw%dh" % ri, "xraw%db0" % ri] + (["xraw%db1" % ri] if bi == 1 else [])
                    ts("dve", cb_, xr[:, 3 + c0b:3 + c0b + 512], cw_col[:, ft, 3:4], cb_col[:, ft:ft + 1],
                       ALU.mult, ALU.add, xk + ["cw_col", "cb_col"], [ck])
                    for jj in (2, 1, 0):
                        stt(cb_, xr[:, jj + c0b:jj + c0b + 512], cw_col[:, ft, jj:jj + 1], cb_, ALU.mult, ALU.add,
                            xk + ["cw_col", ck], [ck])

                def x_s3(n):
                    ft, bi, c0b = xsteps[n]
                    ri = ft % 2
                    act(xc[:, ft, c0b:c0b + 512], cacc[ri][:, c0b:c0b + 512], AF.Silu, ["cacc%d_%d" % (ri, bi)], ["xc"])

                x_s1(0)
                for n in range(len(xsteps)):
                    if n + 1 < len(xsteps):
                        x_s1(n + 1)
                    x_s2(n)
                    x_s3(n)
                if sb == 1:
                    for ft in range(12):
                        store(cs_p[:, ft * 128:(ft + 1) * 128].rearrange("t p -> p t"), halo_s[:, ft, :], ["halo_s%d" % ft],
                              slow=True)
                for wb in range(2):
                    slot = w_use(("ssdz", sb, wb))
                    for t in range(ntile):
                        rw = rows_of(t)
                        b = newbank()
                        for k in range(8):
                            mm(psb[b][0:rw, :], hT[:, k, t * 128:t * 128 + rw], wst[slot][:, k, :], k == 0, k == 7,
                               wq(slot, 0, 512) + ["hT"], [pk(b)])
                        if t == 8:
                            act(zS[:, wb * 512:(wb + 1) * 512], psb[b][0:16, :], AF.Silu, [pk(b)], ["zS"])
                        else:
                            act(zs[:, t, wb * 512:(wb + 1) * 512], psb[b][:, :], AF.Silu, [pk(b)], ["zs"])
                b = newbank()
                for t in range(ntile):
                    rw = rows_of(t)
                    for k in range(8):
                        mm(psb[b][0:rw, t * 16:(t + 1) * 16], hT[:, k, t * 128:t * 128 + rw], wdt[:, k, :], k == 0, k == 7,
                           ["wdt", "hT"], [pk(b)])
                cp("dve", dtraw[:], psb[b][:, 0:128].rearrange("p (t h) -> p t h", h=16), [pk(b)], ["dtraw"])
                if sb == 1:
                    cp("dve", dtS_raw[:], psb[b][0:16, 128:144], [pk(b)], ["dtS_raw"])
                tt("dve", dtv[:], dtraw[:], dtb_b[:].unsqueeze(1).to_broadcast([128, 8, 16]), ALU.add,
                   ["dtraw", "dtb_b"], ["dtv"])
                act(dtv[:], dtv[:], AF.Exp, ["dtv"], ["dtv"])
                act(dtv[:], dtv[:], AF.Ln, ["dtv"], ["dtv"], bias=1.0)
                mset("pool", a_ext[:], 0.0, ["a_ext"])
                for o_ in (0, 32):
                    tt("dve", a_ext[:, :, o_:o_ + 16], dtv[:], A_b[:].unsqueeze(1).to_broadcast([128, 8, 16]), ALU.mult,
                       ["dtv", "A_b", "a_ext"], ["a_ext"])

                R48f = R48[:].rearrange("p h s -> p (h s)")
                bYs = {}

                def stage1(c):
                    p_ = c % 2
                    ex3, Btok, Xd = ex3_[p_], Btok_[p_], Xd_[p_]
                    kx, kb_, kd_ = "ex3_%d" % p_, "Btok%d" % p_, "Xd%d" % p_
                    cols = slice(c * 128, (c + 1) * 128)
                    bX = newbank()
                    pX = psb[bX][:].bitcast(BF16)
                    for ft in range(8):
                        tr(pX[:, ft * 128:(ft + 1) * 128], xc[:, ft, cols], ident_bf[:], ["xc", "ident_bf"], [pk(bX)])
                    cp("act", Xtok[:], pX[:, 0:1024], [pk(bX)], ["Xtok"])
                    bB = newbank()
                    pB = psb[bB][:].bitcast(BF16)
                    for g in range(2):
                        tr(pB[:, g * 128:(g + 1) * 128], xc[:, 8 + g, cols], ident_bf[:], ["xc", "ident_bf"], [pk(bB)])
                    cp("act", Btok[:], pB[:, 0:256], [pk(bB)], [kb_])
                    bC = newbank()
                    mm(psb[bC][:, 0:16], U_f[:], a_ext[:, c, 0:16], True, True, ["U_f", "a_ext"], [pk(bC)])
                    mm(psb[bC][:, 16:32], ones_f[:], a_ext[:, c, 0:16], True, True, ["ones_f", "a_ext"], [pk(bC)])
                    mm(psb[bC][0:48, 128:256], a_ext[:, c, :], U_f[:], True, True, ["U_f", "a_ext"], [pk(bC)])
                    cp("dve", ex3[:, 0:32], psb[bC][:, 0:32], [pk(bC)], [kx])
                    tt("dve", ex3[:, 32:48], ex3[:, 16:32], ex3[:, 0:16], ALU.subtract, [kx], [kx])
                    act(ex3[:], ex3[:], AF.Exp, [kx], [kx])
                    cp("act", AcsT_sb[:], psb[bC][0:16, 128:256], [pk(bC)], ["AcsT_sb"])
                    cp("dve", L48[32:48, :], psb[bC][32:48, 128:256], [pk(bC)], ["L48"])
                    aff(R48[0:16, :, :], AcsT_sb[:].unsqueeze(1).to_broadcast([16, 16, 128]), [[-1, 16], [0, 128]],
                        ALU.is_equal, 0.0, 0, 1, ["AcsT_sb"], ["R48"])
                    X3 = Xtok[:].rearrange("p (h q) -> p h q", q=64)
                    tt("dve", Xdt[:].rearrange("p (h q) -> p h q", q=64), X3,
                       dtv[:, c, :].unsqueeze(2).to_broadcast([128, 16, 64]), ALU.mult, ["Xtok", "dtv"], ["Xdt"])
                    tt("pool", Xd[:].rearrange("p (h q) -> p h q", q=64), Xdt[:].rearrange("p (h q) -> p h q", q=64),
                       ex3[:, 32:48].unsqueeze(2).to_broadcast([128, 16, 64]), ALU.mult, ["Xdt", kx], [kd_])
                    tt("pool", XD[:].rearrange("p (h q) -> p h q", q=64), X3,
                       D_b[:].unsqueeze(2).to_broadcast([128, 16, 64]), ALU.mult, ["Xtok", "D_b"], ["XD"])
                    bG = newbank()
                    for g in range(2):
                        mm(psb[bG][:, g * 128:(g + 1) * 128], xc[:, 8 + g, cols], xc[:, 10 + g, cols], True, True,
                           ["xc"], [pk(bG)])
                    cp("act", GT_sb[:], psb[bG][:, 0:256], [pk(bG)], ["GT_sb"])
                    for hg in range(4):
                        b = newbank()
                        g = hg // 2
                        mm(psb[b][:, :], L48[:, :], R48f[:, hg * 512:(hg + 1) * 512], True, False,
                           ["L48", "L48c", "R48", "R48c"], [pk(b)])
                        mm(psb[b][:, :], ident_bf[:], negm4[:].rearrange("p h s -> p (h s)"), False, True,
                           ["ident_bf", "negm4"], [pk(b)])
                        li = hg % 2
                        act(Lx[li][:], psb[b][:, :], AF.Exp, [pk(b)], ["Lx%d" % li])
                        tt("dve", M_sb[:, hg * 4:(hg + 1) * 4, :], Lx[li][:].rearrange("p (h s) -> p h s", s=128),
                           GT_sb[:, g * 128:(g + 1) * 128].unsqueeze(1).to_broadcast([128, 4, 128]), ALU.mult,
                           ["Lx%d" % li, "GT_sb"], ["M_sb"])
                    bY = [newbank(), newbank()]
                    hold(*bY)
                    bYs[c] = bY
                    for h in range(16):
                        bk_ = bY[h // 8]
                        col = (h % 8) * 64
                        mm(psb[bk_][:, col:col + 64], M_sb[:, h, :], Xdt[:, h * 64:(h + 1) * 64], h % 8 == 0, False,
                           ["M_sb", "Xdt"], [pk(bk_)])
                    for half in range(2):
                        mm(psb[bY[half]][:, :], ident_bf[:], XD[:, half * 512:(half + 1) * 512], False, True,
                           ["ident_bf", "XD"], [pk(bY[half])])

                def stage2(c):
                    p_ = c % 2
                    ex3, Btok, Xd = ex3_[p_], Btok_[p_], Xd_[p_]
                    kx, kb_, kd_ = "ex3_%d" % p_, "Btok%d" % p_, "Xd%d" % p_
                    cols = slice(c * 128, (c + 1) * 128)
                    bY = bYs[c]
                    bO = [newbank(), newbank()]
                    for g in range(2):
                        mm(psb[bO[g]][:, :], xc[:, 10 + g, cols], ST_bf[:, g * 512:(g + 1) * 512], True, True,
                           ["xc", "ST_bf"], [pk(bO[g])])
                    for half in range(2):
                        yh = ytmp[:, half * 512:(half + 1) * 512]
                        tt("dve", yh.rearrange("p (h q) -> p h q", q=64),
                           psb[bO[half]][:, :].rearrange("p (h q) -> p h q", q=64),
                           ex3[:, half * 8:(half + 1) * 8].unsqueeze(2).to_broadcast([128, 8, 64]), ALU.mult,
                           [pk(bO[half]), kx], ["ytmp"])
                        tt("dve", yh, yh, psb[bY[half]][:, :], ALU.add, ["ytmp", pk(bY[half])], ["ytmp"])
                    release(*bY)
                    tt("dve", ytmp[:], ytmp[:], zs[:, c, :], ALU.mult, ["ytmp", "zs"], ["ytmp"])
                    for g in range(2):
                        act(sqs[:], ytmp[:, g * 512:(g + 1) * 512], AF.Square, ["ytmp"], ["sqs", "ssq2"],
                            accum=ssq2[:, g:g + 1])
                    rstd_small(ssq2[:], 512, ["ssq2"], ["ssq2"])
                    for g in range(2):
                        stt(mix[:, c, 1024 + g * 512:1024 + (g + 1) * 512], ytmp[:, g * 512:(g + 1) * 512],
                            ssq2[:, g:g + 1], snwb[:, g * 512:(g + 1) * 512], ALU.mult, ALU.mult,
                            ["ytmp", "ssq2", "snwb"], ["mix%d" % c])
                    bS = [newbank(), newbank()]
                    for g in range(2):
                        mm(psb[bS[g]][:, :], Btok[:, g * 128:(g + 1) * 128], Xd[:, g * 512:(g + 1) * 512], True, True,
                           [kb_, kd_], [pk(bS[g])])
                    ST3 = ST[:].rearrange("p (h q) -> p h q", q=64)
                    tt("dve", ST3, ST3, ex3[:, 16:32].unsqueeze(2).to_broadcast([128, 16, 64]), ALU.mult,
                       ["ST", kx], ["ST"])
                    for g in range(2):
                        tt("dve", ST[:, g * 512:(g + 1) * 512], ST[:, g * 512:(g + 1) * 512], psb[bS[g]][:, :], ALU.add,
                           ["ST", pk(bS[g])], ["ST"])
                    cp("act", ST_bf[:], ST[:], ["ST"], ["ST_bf"])

                P1, P2 = (0, 1, 2, 3, 4, 5), (6, 7)
                for th in with_pool(P1, stage1, 0):
                    th()
                for c in range(8):
                    A_ = with_pool(P1, stage1, c + 1) if c + 1 < 8 else []
                    B_ = with_pool(P2, stage2, c)
                    for th in Sched.merge(A_, B_):
                        th()
                if sb == 1:
                    STo = sbt(s1, "STo", [128, 8, 128], F32)
                    for half in range(2):
                        b = newbank()
                        for i in range(4):
                            cc = half * 4 + i
                            tr(psb[b][:, i * 128:(i + 1) * 128], ST[:, cc * 128:(cc + 1) * 128], ident_f[:],
                               ["ST", "ident_f"], [pk(b)])
                        cp("act", STo[:, half * 4:(half + 1) * 4, :], psb[b][:, :].rearrange("p (c n) -> p c n", n=128),
                           [pk(b)], ["STo"])
                    store(sm_p.rearrange("(c q) n -> q c n", q=128), STo[:], ["STo"])
            S.barrier()

            S.mark("B2_%d" % sb)
            with ExitStack() as s2:
                def F32s(name, n):
                    return [sbt(s2, "%s%d" % (name, i), [128, 512], F32) for i in range(n)]

                def B16s(name, n, shape):
                    return [sbt(s2, "%s%d" % (name, i), shape, BF16) for i in range(n)]
                tq1 = F32s("tq", 1)[0]
                qs_ = F32s("qs", 3)
                ff_ = F32s("ff", 3)
                kk_ = F32s("kk", 3)
                lf1 = F32s("lf", 1)[0]
                bb1 = F32s("bb", 1)[0]
                enb1 = F32s("enb", 1)[0]
                eb_ = F32s("eb", 2)
                qb_ = B16s("qb", 2, [128, 512])
                kb_ = B16s("kb", 2, [128, 512])
                kd1 = B16s("kd", 1, [128, 512])[0]
                kdT_ = B16s("kdT", 2, [128, 512])
                qbz_ = B16s("qbz", 2, [128, 8, 128])
                v_ = B16s("v", 4, [128, 4, 128])
                tg1 = sbt(s2, "tg", [128, 4, 128], F32)
                gw_ = [sbt(s2, "gw%d" % i, [128, 4, 128], F32) for i in range(4)]
                ATm_all = B16s("ATm", 2, [128, 4, 128])
                Sall_f = [sbt(s2, "Sall_f%d" % i, [128, 9, 128], F32) for i in range(2)]
                Sall_b = B16s("Sall_b", 2, [128, 9, 128])
                Sbf0 = sbt(s2, "Sbf0", [128, 128], BF16)
                junk = sbt(s2, "junk", [128, 128], F32)
                ssq4 = [sbt(s2, "ssq4%d" % i, [128, 4], F32) for i in range(2)]
                for i in range(2):
                    mset("pool", qbz_[i][:], 0.0, ["qbz%d" % i])
                items = [(h, bi, c0b) for h in range(8) for bi, (c0b, nb) in enumerate(blocks) if nb == 512]
                NI = len(items)
                slots = {}
                SEG = lambda: S.rec.append(None)

                def stageA(k):
                    h, bi, c0b = items[k]
                    s2_, s3_ = "q%d" % (k % 3), "v%d" % (k % 4)
                    qs, ff, kk, v_sb, gw = qs_[k % 3], ff_[k % 3], kk_[k % 3], v_[k % 4], gw_[k % 4]
                    if bi == 0:
                        slots[h] = w_use(("hgrn", sb, h))
                    slot = slots[h]
                    wkq, wkf, wkvg = wq(slot, 0, 128), wq(slot, 128, 256), wq(slot, 256, 512)
                    bv = [newbank(), newbank()]
                    hold(*bv)
                    for t4 in range(4):
                        t = c0b // 128 + t4
                        bk_ = bv[t4 // 2]
                        col = (t4 % 2) * 256
                        for kc in range(8):
                            mm(psb[bk_][:, col:col + 256], hT[:, kc, t * 128:(t + 1) * 128], wst[slot][:, kc, 256:512],
                               kc == 0, kc == 7, wkvg + ["hT"], [pk(bk_)])
                    bq = newbank()
                    hold(bq)
                    for kc in range(8):
                        mm(psb[bq][:, :], wst[slot][:, kc, 0:128], hT[:, kc, c0b:c0b + 512], kc == 0, kc == 7,
                           wkq + ["hT"], [pk(bq)])
                    SEG()
                    bf_ = newbank()
                    for kc in range(8):
                        mm(psb[bf_][:, :], wst[slot][:, kc, 128:256], hT[:, kc, c0b:c0b + 512], kc == 0, kc == 7,
                           wkf + ["hT"], [pk(bf_)])
                    SEG()
                    for half in range(2):
                        pv = psb[bv[half]][:, :].rearrange("p (t c) -> p t c", c=256)
                        act(tg1[:, half * 2:(half + 1) * 2, :], pv[:, :, 128:256], AF.Tanh, [pk(bv[half])], ["tg"],
                            scale=0.5)
                    act(tq1[:], psb[bq][:, :], AF.Tanh, [pk(bq)], ["tq"], scale=0.5)
                    SEG()
                    act(ff[:], psb[bf_][:, :], AF.Tanh, [pk(bf_)], ["ff" + s2_], scale=0.5)
                    for half in range(2):
                        pv = psb[bv[half]][:, :].rearrange("p (t c) -> p t c", c=256)
                        cp("act", v_sb[:, half * 2:(half + 1) * 2, :], pv[:, :, 0:128], [pk(bv[half])], ["v" + s3_])
                    SEG()
                    for half in range(2):
                        pv = psb[bv[half]][:, :].rearrange("p (t c) -> p t c", c=256)
                        stt(gw[:, half * 2:(half + 1) * 2, :], tg1[:, half * 2:(half + 1) * 2, :], 1.0, pv[:, :, 128:256],
                            ALU.add, ALU.mult, ["tg", pk(bv[half])], ["gw" + s3_])
                    release(*bv)
                    stt(qs[:], tq1[:], 1.0, psb[bq][:, :], ALU.add, ALU.mult, ["tq", pk(bq)], ["qs" + s2_])
                    release(bq)
                    ts("dve", ff[:], ff[:], c1_col[:, h:h + 1], c0_col[:, h:h + 1], ALU.mult, ALU.add,
                       ["ff" + s2_, "c1_col", "c0_col"], ["ff" + s2_])
                    ts("dve", kk[:], ff[:], -1.0, 1.0, ALU.mult, ALU.add, ["ff" + s2_], ["kk" + s2_])
                    tt("pool", gw[:], gw[:], gwbh[:].unsqueeze(1).to_broadcast([128, 4, 128]), ALU.mult,
                       ["gw" + s3_, "gwbh"], ["gw" + s3_])
                    if bi == 0 and sb == 1:
                        b = newbank()
                        for kc in range(8):
                            mm(psb[b][:, 0:16], wst[slot][:, kc, 0:128], hT[:, kc, 1024:1040], kc == 0, kc == 7,
                               wkq + ["hT"], [pk(b)])
                        for kc in range(8):
                            mm(psb[b][:, 16:32], wst[slot][:, kc, 128:256], hT[:, kc, 1024:1040], kc == 0, kc == 7,
                               wkf + ["hT"], [pk(b)])
                        act(tmpS[:, 0:32], psb[b][:, 0:32], AF.Tanh, [pk(b)], ["tmpS"], scale=0.5)
                        stt(qS[:, h, :], tmpS[:, 0:16], 1.0, psb[b][:, 0:16], ALU.add, ALU.mult, ["tmpS", pk(b)], ["qS"])
                        ts("dve", fS[:, h, :], tmpS[:, 16:32], c1_col[:, h:h + 1], c0_col[:, h:h + 1], ALU.mult, ALU.add,
                           ["tmpS", "c1_col", "c0_col"], ["fS"])
                        ts("dve", kS[:, h, :], fS[:, h, :], -1.0, 1.0, ALU.mult, ALU.add, ["fS"], ["kS"])
                        b = newbank()
                        for kc in range(8):
                            mm(psb[b][0:16, 0:256], hT[:, kc, 1024:1040], wst[slot][:, kc, 256:512], kc == 0, kc == 7,
                               wkvg + ["hT"], [pk(b)])
                        cp("act", vS[:, h * 128:(h + 1) * 128], psb[b][0:16, 0:128], [pk(b)], ["vS"])
                        act(tgS[:], psb[b][0:16, 128:256], AF.Tanh, [pk(b)], ["tgS"], scale=0.5)
                        stt(gsS[:, h * 128:(h + 1) * 128], tgS[:], 1.0, psb[b][0:16, 128:256], ALU.add, ALU.mult,
                            ["tgS", pk(b)], ["gsS"])

                def stageB(k):
                    h, bi, c0b = items[k]
                    s2_ = str(k % 2)
                    sq_ = "q%d" % (k % 3)
                    qs, ff, kk = qs_[k % 3], ff_[k % 3], kk_[k % 3]
                    eb, qb, kb, kdT, qbz = eb_[k % 2], qb_[k % 2], kb_[k % 2], kdT_[k % 2], qbz_[k % 2]
                    act(lf1[:], ff[:], AF.Ln, ["ff" + sq_], ["lf"])
                    SEG()
                    S.add("dve", lambda e, o=bb1[:], d0=m01[:], d1=lf1[:]: e.tensor_tensor_scan(
                        out=o, data0=d0, data1=d1, initial=0.0, op0=ALU.mult, op1=ALU.add), ["m01", "lf"], ["bb"])
                    SEG()
                    act(eb[:], bb1[:], AF.Exp, ["bb"], ["eb" + s2_])
                    act(enb1[:], bb1[:], AF.Exp, ["bb"], ["enb"], scale=-1.0)
                    SEG()
                    stt(qb[:], qs[:], 0.5, eb[:], ALU.mult, ALU.mult, ["qs" + sq_, "eb" + s2_], ["qb" + s2_])
                    tt("dve", kb[:], kk[:], enb1[:], ALU.mult, ["kk" + sq_, "enb"], ["kb" + s2_])
                    tt("pool", kd1[:].rearrange("p (c t) -> p c t", t=64), kb[:].rearrange("p (c t) -> p c t", t=64),
                       eb[:].rearrange("p (c t) -> p c t", t=64)[:, :, 63:64].to_broadcast([128, 8, 64]), ALU.mult,
                       ["kb" + s2_, "eb" + s2_], ["kd"])
                    qbz_view = qbz[:].rearrange("p c x -> p (c x)").rearrange(
                        "p (pr j i) -> p pr j i", j=4, i=64)[:, :, 0:4:3, :]
                    cp("pool", qbz_view, qb[:].rearrange("p (pr two i) -> p pr two i", two=2, i=64),
                       ["qb" + s2_], ["qbz" + s2_])
                    SEG()
                    pK = psb[4][:].bitcast(BF16)
                    for t4 in range(4):
                        tr(pK[:, t4 * 128:(t4 + 1) * 128], kd1[:, t4 * 128:(t4 + 1) * 128], ident_bf[:],
                           ["kd", "ident_bf"], [pk(4)])
                    cp("act", kdT[:], pK[:, 0:512], [pk(4)], ["kdT" + s2_])

                def stageC(k):
                    h, bi, c0b = items[k]
                    x_ = k % 2
                    s2_, s3_ = str(k % 2), "v%d" % (k % 4)
                    eb, qb, kb, qbz, kdT, v_sb, gw, ssq = (eb_[x_], qb_[x_], kb_[x_], qbz_[x_], kdT_[x_], v_[k % 4],
                                                           gw_[k % 4], ssq4[x_])
                    Sf, Sb16, ATm = Sall_f[x_], Sall_b[x_], ATm_all[x_]
                    bA, bSa, bSb, bo = 4, 5, 6, 7
                    for t4 in range(4):
                        mm(psb[bA][:, t4 * 128:(t4 + 1) * 128], kb[:, t4 * 128:(t4 + 1) * 128],
                           qb[:, t4 * 128:(t4 + 1) * 128], True, True, ["kb" + s2_, "qb" + s2_], [pk(bA)])
                    for t4 in range(4):
                        mm(psb[bSa][:, t4 * 128:(t4 + 1) * 128], kdT[0:64, t4 * 128:(t4 + 1) * 128], v_sb[0:64, t4, :],
                           True, True, ["kdT" + s2_, "v" + s3_], [pk(bSa)])
                        mm(psb[bSb][:, t4 * 128:(t4 + 1) * 128], kdT[64:128, t4 * 128:(t4 + 1) * 128], v_sb[64:128, t4, :],
                           True, True, ["kdT" + s2_, "v" + s3_], [pk(bSb)])
                    SEG()
                    tt("dve", ATm[:], psb[bA][:, :].rearrange("p (t s) -> p t s", s=128),
                       maskBD[:].unsqueeze(1).to_broadcast([128, 4, 128]), ALU.mult, [pk(bA), "maskBD"], ["ATm" + s2_])
                    if bi == 0:
                        prev_f, prev_b, pk_f, pk_b = S_h[:, h, :], Sbf0[:], "S_h%d" % h, "Sbf0"
                        cp("pool", Sbf0[:], S_h[:, h, :], ["S_h%d" % h], ["Sbf0"])
                    else:
                        prev_f, prev_b = Sall_f[1 - x_][:, 8, :], Sall_b[1 - x_][:, 8, :]
                        pk_f, pk_b = "Sf%d" % (1 - x_), "Sb%d" % (1 - x_)
                    for c in range(8):
                        src_ = prev_f if c == 0 else Sf[:, c, :]
                        bank = bSa if c % 2 == 0 else bSb
                        stt(Sf[:, c + 1, :], src_, eb[:, c * 64 + 63:c * 64 + 64],
                            psb[bank][:, (c // 2) * 128:(c // 2 + 1) * 128], ALU.mult, ALU.add,
                            ["Sf" + s2_, pk_f, "eb" + s2_, pk(bank)], ["Sf" + s2_])
                    SEG()
                    cp("act", Sb16[:, 1:5, :], Sf[:, 1:5, :], ["Sf" + s2_], ["Sb" + s2_])
                    cp("act", Sb16[:, 5:9, :], Sf[:, 5:9, :], ["Sf" + s2_], ["Sb" + s2_])
                    SEG()
                    for t4 in range(4):
                        ca_, cb_ = 2 * t4, 2 * t4 + 1
                        oc = psb[bo][:, t4 * 128:(t4 + 1) * 128]
                        mm(oc, ATm[:, t4, :], v_sb[:, t4, :], True, False, ["ATm" + s2_, "v" + s3_], [pk(bo)])
                        before = prev_b if ca_ == 0 else Sb16[:, ca_, :]
                        mm(oc, qbz[:, ca_, :], before, False, False, ["qbz" + s2_, "Sb" + s2_, pk_b], [pk(bo)])
                        mm(oc, qbz[:, cb_, :], Sb16[:, cb_, :], False, True, ["qbz" + s2_, "Sb" + s2_], [pk(bo)])
                    SEG()
                    for t4 in range(4):
                        act(junk[:], psb[bo][:, t4 * 128:(t4 + 1) * 128], AF.Square, [pk(bo)], ["junk", "ssq" + s2_],
                            accum=ssq[:, t4:t4 + 1])
                    rstd_small(ssq[:], 128, ["ssq" + s2_], ["ssq" + s2_])
                    SEG()
                    for t4 in range(4):
                        t = c0b // 128 + t4
                        stt(mix[:, t, h * 128:(h + 1) * 128], psb[bo][:, t4 * 128:(t4 + 1) * 128], ssq[:, t4:t4 + 1],
                            gw[:, t4, :], ALU.mult, ALU.mult, [pk(bo), "ssq" + s2_, "gw" + s3_], ["mix%d" % t])
                    if bi == 1:
                        cp("dve", S_h[:, h, :], Sf[:, 8, :], ["Sf" + s2_], ["S_h%d" % h])
                        if sb == 1:
                            store(hg_p[h], S_h[:, h, :], ["S_h%d" % h])

                def segs(lst):
                    out, cur = [], []
                    for th in lst:
                        if th is None:
                            out.append(cur)
                            cur = []
                        else:
                            cur.append(th)
                    out.append(cur)
                    return out

                def emit_iter(kc, ka, kb2):
                    cs = segs(with_pool((0, 1, 2, 3), stageC, kc)) if kc is not None else [[]] * 6
                    as_ = segs(with_pool((0, 1, 2, 3), stageA, ka)) if ka is not None else [[]] * 5
                    bs = segs(with_pool((0, 1, 2, 3), stageB, kb2)) if kb2 is not None else [[]] * 5
                    c1, c2, c3, c4, c5, c6 = cs
                    a1a, a1b, a2a, a2b, a3 = as_
                    b1, b2, b3, b4, b5 = bs
                    for seg in (c1, b1, b2, a1a, c2, b3, c3, b4, c4, a1b, c5, b5, c6, a2a, a2b, a3):
                        for th in seg:
                            th()

                emit_iter(None, 0, None)
                emit_iter(None, 1, None)
                emit_iter(None, 2, 0)
                for i in range(NI):
                    emit_iter(i, i + 3 if i + 3 < NI else None, i + 1 if i + 1 < NI else None)
            S.barrier()
            if sb == 1:
                with ExitStack() as ssm:
                    A_ = with_pool((0, 1, 2, 3), sample_hgrn, ssm)
                    B_ = with_pool((4, 5, 6, 7), sample_ssd, ssm)
                    for th in Sched.merge(A_, B_):
                        th()
                S.barrier()

            S.mark("C%d" % sb)
            if DEBUG:
                store(dbg_mix[sb][:, 0:8, :], mix[:, 0:8, :], ["mix%d" % t for t in range(9)])
                S.barrier()
            with ExitStack() as s3:
                Wout = sbt(s3, "Wout", [128, 16, 1024], BF16)
                mixT = [sbt(s3, "mixT%d" % i, [128, 16, 128], BF16) for i in range(2)]
                xtc = [sbt(s3, "xtc%d" % i, [128, D], F32) for i in range(2)]
                hnc = [sbt(s3, "hnc%d" % i, [128, D], BF16) for i in range(2)]
                sqc = sbt(s3, "sqc", [128, D], F32)
                ssC = sbt(s3, "ssC", [128, 2], F32)
                w2b = sbt(s3, "w2b", [128, D], F32)
                dma("sp", w2b[:], norm2_w.partition_broadcast(128), (), ["w2b"])
                for q4 in range(4):
                    load_w(Wout[:, q4 * 4:(q4 + 1) * 4, :],
                           w_out[q4 * 512:(q4 + 1) * 512, :].rearrange("(k p) n -> p k n", p=128), (), ["Wout%d" % q4])
                def c_s1(t):
                    i = t % 2
                    rw = rows_of(t)
                    src = xs if t == 8 else xp[tok0 + t * 128: tok0 + (t + 1) * 128, :]
                    dma("sp", xtc[i][0:rw, :], src, (), ["xtc%d" % i])
                    for half in range(2):
                        b = newbank()
                        pT = psb[b][:].bitcast(BF16)
                        for j in range(8):
                            fc = half * 8 + j
                            tr(pT[:, j * 128:j * 128 + rw], mix[0:rw, t, fc * 128:(fc + 1) * 128], ident_bf[0:rw, 0:rw],
                               ["mix%d" % t, "ident_bf"], [pk(b)])
                        cp("act" if half == 0 else "dve", mixT[i][:, half * 8:(half + 1) * 8, 0:rw],
                           pT[:, 0:1024].rearrange("p (k c) -> p k c", k=8)[:, :, 0:rw], [pk(b)], ["mixT%d" % i])

                def c_s2(t):
                    i = t % 2
                    rw = rows_of(t)
                    bo2 = [newbank(), newbank()]
                    for nh in range(2):
                        for fc in range(16):
                            mm(psb[bo2[nh]][0:rw, :], mixT[i][:, fc, 0:rw], Wout[:, fc, nh * 512:(nh + 1) * 512],
                               fc == 0, fc == 15, ["mixT%d" % i, "Wout%d" % (fc // 4)], [pk(bo2[nh])])
                    for nh in range(2):
                        tt("dve", x2[0:rw, t, nh * 512:(nh + 1) * 512], xtc[i][0:rw, nh * 512:(nh + 1) * 512],
                           psb[bo2[nh]][0:rw, :], ALU.add, ["xtc%d" % i, pk(bo2[nh])], ["x2_%d" % t])
                    act(sqc[0:rw, :], x2[0:rw, t, :], AF.Square, ["x2_%d" % t], ["sqc", "ssC%d" % i],
                        accum=ssC[0:rw, i:i + 1])
                    rstd_small(ssC[0:rw, i:i + 1], D, ["ssC%d" % i], ["ssC%d" % i])
                    stt(hnc[i][0:rw, :], x2[0:rw, t, :], ssC[0:rw, i:i + 1], w2b[0:rw, :], ALU.mult, ALU.mult,
                        ["x2_%d" % t, "ssC%d" % i, "w2b"], ["hnc%d" % i])

                def c_s3(t):
                    i = t % 2
                    rw = rows_of(t)
                    b = newbank()
                    pT = psb[b][:].bitcast(BF16)
                    for k in range(8):
                        tr(pT[:, k * 128:k * 128 + rw], hnc[i][0:rw, k * 128:(k + 1) * 128], ident_bf[0:rw, 0:rw],
                           ["hnc%d" % i, "ident_bf"], [pk(b)])
                    cp("act", hT[:, :, t * 128:t * 128 + rw],
                       pT[:, 0:1024].rearrange("p (k c) -> p k c", k=8)[:, :, 0:rw], [pk(b)], ["hT"])

                c_s1(0)
                for t in range(ntile):
                    if t + 1 < ntile:
                        c_s1(t + 1)
                    c_s2(t)
                    if t >= 1:
                        c_s3(t - 1)
                c_s3(ntile - 1)
            S.barrier()

            S.mark("D%d" % sb)
            with ExitStack() as s4:
                actT = sbt(s4, "actT", [128, 11, 1040], BF16)
                Wd = sbt(s4, "Wd", [128, 11, 1024], BF16)
                graw = [sbt(s4, "graw%d" % i, [128, 2 + 1024], F32) for i in range(2)]
                gacc = [sbt(s4, "gacc%d" % i, [128, 1024], F32) for i in range(2)]
                if sb == 1:
                    stcf_t = [sbt(s4, "stcf_t%d" % i, [16, 2, 128], F32) for i in range(2)]
                    CFT = [sbt(s4, "CFT%d" % i, [128, 2, 16], F32) for i in range(2)]
                    grawS = [sbt(s4, "grawS%d" % i, [128, 16], F32) for i in range(2)]
                    gaccS = [sbt(s4, "gaccS%d" % i, [128, 16], F32) for i in range(2)]
                    gtokS = [sbt(s4, "gtokS%d" % i, [16, 128], F32) for i in range(2)]
                yt = [sbt(s4, "yt%d" % i, [128, D], F32) for i in range(2)]
                sqe = sbt(s4, "sqe", [128, D], F32)
                ssE = sbt(s4, "ssE", [128, 2], F32)
                wfb = sbt(s4, "wfb", [128, D], F32)
                dma("sp", wfb[:], final_norm_w.partition_broadcast(128), (), ["wfb"])

                def phase_e(t):
                    i = t % 2
                    rw = rows_of(t)
                    act(sqe[0:rw, :], x2[0:rw, t, :], AF.Square, ["x2_%d" % t], ["sqe", "ssE%d" % i],
                        accum=ssE[0:rw, i:i + 1])
                    rstd_small(ssE[0:rw, i:i + 1], D, ["ssE%d" % i], ["ssE%d" % i])
                    stt(yt[i][0:rw, :], x2[0:rw, t, :], ssE[0:rw, i:i + 1], wfb[0:rw, :], ALU.mult, ALU.mult,
                        ["x2_%d" % t, "ssE%d" % i, "wfb"], ["yt%d" % i])
                    dst = y_s if t == 8 else y_p[tok0 + t * 128: tok0 + (t + 1) * 128, :]
                    store(dst, yt[i][0:rw, :], ["yt%d" % i])

                for fg in range(2):
                    steps = [(ftl, bi, c0b) for ftl in range(11) for bi, (c0b, nb) in enumerate(blocks) if nb == 512]
                    fst = {}

                    def f_s1(n):
                        ftl, bi, c0b = steps[n]
                        ft = fg * 11 + ftl
                        ri = ft % 2
                        gr = graw[ri]
                        if bi == 0:
                            fst[("slot", ftl)] = w_use(("ffn", sb, fg, ftl // 2))
                            if ftl == 0:
                                for f2 in range(11):
                                    load_w(Wd[:, f2, :], w_down[(fg * 11 + f2) * 128:(fg * 11 + f2 + 1) * 128, :], (),
                                           ["Wd%d" % f2])
                            cp("act", gr[:, 0:2], halo_f[:, ft, :], ["halo_f%d" % ft], ["graw%dh" % ri])
                        slot = fst[("slot", ftl)]
                        gc0 = (ftl % 2) * 128
                        vc0 = 256 + (ftl % 2) * 128
                        todo = [(c0b, 512)] + ([(1024, 16)] if (bi == 1 and sb == 1) else [])
                        for (c0, nb) in todo:
                            bg = newbank()
                            for k in range(8):
                                mm(psb[bg][:, 0:nb], wst[slot][:, k, gc0:gc0 + 128], hT[:, k, c0:c0 + nb], k == 0, k == 7,
                                   wq(slot, gc0, gc0 + 128) + ["hT"], [pk(bg)])
                            bvv = newbank()
                            hold(bvv)
                            for k in range(8):
                                mm(psb[bvv][:, 0:nb], wst[slot][:, k, vc0:vc0 + 128], hT[:, k, c0:c0 + nb], k == 0, k == 7,
                                   wq(slot, vc0, vc0 + 128) + ["hT"], [pk(bvv)])
                            if nb == 16:
                                fst[("vS", ftl)] = bvv
                                cp("act", grawS[ri][:], psb[bg][:, 0:16], [pk(bg)], ["grawS%d" % ri])
                            else:
                                fst[("v", n)] = bvv
                                cp("act", gr[:, 2 + c0:2 + c0 + 512], psb[bg][:, 0:512], [pk(bg)], ["graw%db%d" % (ri, bi)])
                        if bi == 1:
                            cp("act", halo_f[:, ft, :], gr[:, 1024:1026], ["graw%db1" % ri], ["halo_f%d" % ft])

                    def f_s2(n):
                        ftl, bi, c0b = steps[n]
                        ft = fg * 11 + ftl
                        ri = ft % 2
                        gr = graw[ri]
                        gb = gacc[ri][:, c0b:c0b + 512]
                        ak = "gacc%db%d" % (ri, bi)
                        gk = ["graw%dh" % ri, "graw%db0" % ri] + (["graw%db1" % ri] if bi == 1 else [])
                        ts("dve", gb, gr[:, 2 + c0b:2 + c0b + 512], fw_col[:, ft, 2:3], fb_col[:, ft:ft + 1],
                           ALU.mult, ALU.add, gk + ["fw_col", "fb_col"], [ak])
                        for jj in (1, 0):
                            stt(gb, gr[:, jj + c0b:jj + c0b + 512], fw_col[:, ft, jj:jj + 1], gb, ALU.mult, ALU.add,
                                gk + ["fw_col", ak], [ak])

                    def f_s3(n):
                        ftl, bi, c0b = steps[n]
                        ri = (fg * 11 + ftl) % 2
                        gb = gacc[ri][:, c0b:c0b + 512]
                        ak = "gacc%db%d" % (ri, bi)
                        act(gb, gb, AF.Silu, [ak], [ak])

                    def f_s4(n):
                        ftl, bi, c0b = steps[n]
                        ft = fg * 11 + ftl
                        ri = ft % 2
                        gb = gacc[ri][:, c0b:c0b + 512]
                        ak = "gacc%db%d" % (ri, bi)
                        bvv = fst[("v", n)]
                        tt("dve", actT[:, ftl, c0b:c0b + 512], gb, psb[bvv][:, :], ALU.mult, [ak, pk(bvv)],
                           ["actT%d" % ftl])
                        release(bvv)
                        if bi == 1 and sb == 1:
                            ffn_sample_tile(ft, ftl, ri, stcf_t, CFT, grawS, gaccS, gtokS, actT, fst[("vS", ftl)])
                            release(fst[("vS", ftl)])

                    NS = len(steps)
                    f_s1(0)
                    for n in range(NS):
                        if n + 1 < NS:
                            f_s1(n + 1)
                        f_s2(n)
                        if n >= 1:
                            f_s4(n - 1)
                        f_s3(n)
                    f_s4(NS - 1)
                    if sb == 1:
                        for ftl2 in range(11):
                            ft2 = fg * 11 + ftl2
                            store(cf_p[:, ft2 * 128:(ft2 + 1) * 128].rearrange("t p -> p t"), halo_f[:, ft2, :],
                                  ["halo_f%d" % ft2], slow=True)
                    for t in range(ntile):
                        rw = rows_of(t)
                        b2 = [newbank(), newbank()]
                        for nh in range(2):
                            for ftl in range(11):
                                mm(psb[b2[nh]][0:rw, :], actT[:, ftl, t * 128:t * 128 + rw], Wd[:, ftl, nh * 512:(nh + 1) * 512],
                                   ftl == 0, ftl == 10, ["actT%d" % ftl, "Wd%d" % ftl], [pk(b2[nh])])
                        if fg == 1 and t >= 1:
                            phase_e(t - 1)
                        for nh in range(2):
                            tt("dve", x2[0:rw, t, nh * 512:(nh + 1) * 512], x2[0:rw, t, nh * 512:(nh + 1) * 512],
                               psb[b2[nh]][0:rw, :], ALU.add, ["x2_%d" % t, pk(b2[nh])], ["x2_%d" % t])
                    if fg == 1:
                        phase_e(ntile - 1)
            S.barrier()

        if CUT is not None:
            cut_at = S.marks[CUT]
            S.ops = S.ops[:cut_at]
            out_ops[:] = [o for o in out_ops if o.idx < cut_at]
        S.final_wait(out_ops)
        S.emit(nc, es)
    return nc


_NC_CACHE = {}


def kernel(**inputs):
    f32 = np.float32
    g = {k: np.ascontiguousarray(np.asarray(v, dtype=f32)) for k, v in inputs.items()}
    if "nc" not in _NC_CACHE:
        _NC_CACHE["nc"] = build_nc()
    nc = _NC_CACHE["nc"]
    shared = {
        "norm1_w": g["norm1_w"][0], "w_in": g["w_in"][0], "hgrn_lb": g["hgrn_lb"],
        "hgrn_norm_w": g["hgrn_norm_w"][0], "ssm_conv_w": g["ssm_conv_w"][0], "ssm_conv_b": g["ssm_conv_b"][0],
        "ssm_dt_bias": g["ssm_dt_bias"][0], "ssm_a_log": g["ssm_a_log"][0], "ssm_d": g["ssm_d"][0],
        "ssm_norm_w": g["ssm_norm_w"][0], "w_out": g["w_out"][0], "norm2_w": g["norm2_w"][0],
        "w_up": g["w_up"][0], "ffn_conv_w": g["ffn_conv_w"][0], "ffn_conv_b": g["ffn_conv_b"][0],
        "w_down": g["w_down"][0], "final_norm_w": g["final_norm_w"],
    }
    in_maps = []
    for c in range(NCORES):
        sl = slice(16 * c, 16 * c + 16)
        m = dict(shared)
        m["xp"] = g["x_prompt"][c]
        m["xs"] = g["x_sample"][sl, 0, :]
        m["st_h"] = g["state_hgrn"][0, sl]
        m["st_s"] = g["state_ssm"][0, sl].reshape(16, 1024, 128)
        m["st_cs"] = g["state_conv_ssm"][0, sl]
        m["st_cf"] = g["state_conv_ffn"][0, sl]
        in_maps.append({k: np.ascontiguousarray(v) for k, v in m.items()})
    res = run_bass_kernel_spmd(nc, in_maps, core_ids=list(range(NCORES)))
    R = res.results
    y_prompt = np.stack([R[c]["y_p"] for c in range(NCORES)], 0)
    y_sample = np.concatenate([R[c]["y_s"] for c in range(NCORES)], 0)[:, None, :]
    hgrn_p = np.stack([R[c]["hg_p"] for c in range(NCORES)], 0)[None]
    hgrn_s = np.concatenate([R[c]["hg_s"] for c in range(NCORES)], 0)[None]
    ssm_p = np.stack([R[c]["sm_p"].reshape(16, 64, 128) for c in range(NCORES)], 0)[None]
    ssm_s = np.concatenate([R[c]["sm_s"].reshape(16, 16, 64, 128) for c in range(NCORES)], 0)[None]
    cs_p_ = np.stack([R[c]["cs_p"] for c in range(NCORES)], 0)[None]
    cs_s_ = np.concatenate([R[c]["cs_s"] for c in range(NCORES)], 0)[None]
    cf_p_ = np.stack([R[c]["cf_p"] for c in range(NCORES)], 0)[None]
    cf_s_ = np.concatenate([R[c]["cf_s"] for c in range(NCORES)], 0)[None]
    outs = (y_prompt, y_sample, hgrn_p, hgrn_s, ssm_p, ssm_s, cs_p_, cs_s_, cf_p_, cf_s_)
    return tuple(np.ascontiguousarray(o, dtype=f32) for o in outs)
```

```python
from contextlib import ExitStack

import numpy as np
import concourse.bass as bass
import concourse.mybir as mybir
from concourse.bass_utils import run_bass_kernel_spmd

F32 = mybir.dt.float32
BF16 = mybir.dt.bfloat16
ALU = mybir.AluOpType
AF = mybir.ActivationFunctionType

NCORES = 8
D = 1024
DIN = 6672
DFF = 2816
C_Q, C_F, C_I, C_G, C_Z, C_X, C_DT = 0, 1024, 2048, 3072, 4096, 5120, 6656
EPS = 1e-6
NEG = -30000.0

COMPUTE = ("pe", "act", "dve", "pool")
STREAMS = ("pe", "act", "dve", "pool", "sp")


class Op:
    __slots__ = ("idx", "eng", "fn", "deps", "dma", "signal", "ms", "sem", "val", "prev_val")

    def __init__(self, idx, eng, fn, dma):
        self.idx = idx
        self.eng = eng
        self.fn = fn
        self.dma = dma
        self.deps = set()
        self.signal = False
        self.ms = 0
        self.sem = None
        self.val = 0
        self.prev_val = 0


class Sched:
    def __init__(self):
        self.ops = []
        self.last_w = {}
        self.readers = {}
        self.last_on = {}
        self.pending_dma = []
        self.marks = {}
        self.rec = None

    def add(self, eng, fn, reads=(), writes=(), dma=False):
        if self.rec is not None:
            self.rec.append(lambda: self._add(eng, fn, reads, writes, dma))
            return None
        return self._add(eng, fn, reads, writes, dma)

    def record(self, f, *a):
        assert self.rec is None
        self.rec = []
        f(*a)
        r, self.rec = self.rec, None
        return r

    @staticmethod
    def merge(A, B):
        out, i, j = [], 0, 0
        while i < len(A) or j < len(B):
            if j >= len(B) or (i < len(A) and (i + 0.5) * len(B) <= (j + 0.5) * len(A)):
                out.append(A[i])
                i += 1
            else:
                out.append(B[j])
                j += 1
        return out

    def _add(self, eng, fn, reads=(), writes=(), dma=False):
        op = Op(len(self.ops), eng, fn, dma)
        deps = set()
        for k in reads:
            w = self.last_w.get(k)
            if w is not None:
                deps.add(w)
        for k in writes:
            w = self.last_w.get(k)
            if w is not None:
                deps.add(w)
            for r in self.readers.get(k, ()):
                deps.add(r)
        for k in writes:
            self.last_w[k] = op
            self.readers[k] = []
        for k in reads:
            if self.last_w.get(k) is not op:
                self.readers.setdefault(k, []).append(op)
        deps.discard(op)
        op.deps = {d for d in deps if not (d.eng == "pe" and eng == "pe" and not d.dma and not dma)}
        self.ops.append(op)
        if dma:
            self.pending_dma.append(op)
        else:
            self.last_on[eng] = op
        return op

    def mark(self, name):
        self.marks[name] = len(self.ops)

    def barrier(self):
        lasts = list(self.last_on.values())
        dmas = list(self.pending_dma)
        for st in STREAMS:
            op = Op(len(self.ops), st, None, False)
            op.deps = set(lasts) | set(dmas)
            self.ops.append(op)
        self.pending_dma = []
        self.last_w = {}
        self.readers = {}

    def final_wait(self, ops):
        op = Op(len(self.ops), "sp", None, False)
        op.deps = set(ops)
        self.ops.append(op)

    def emit(self, nc, es, n_dma_sems=20):
        for op in self.ops:
            for d in op.deps:
                if not d.dma:
                    d.signal = True
        cnt = {e: 0 for e in COMPUTE}
        for op in self.ops:
            if op.dma or op.fn is None:
                continue
            if op.signal:
                cnt[op.eng] += 1
                op.ms = cnt[op.eng]
        assert max(cnt.values()) < 60000, cnt
        esem = {e: es.enter_context(nc.semaphore("s_" + e)) for e in COMPUTE}
        dsem = {st: [es.enter_context(nc.semaphore("d_%s_%d" % (st, i))) for i in range(n)]
                for st, n in (("sp", 12), ("pool", 8))}
        dcount = {st: 0 for st in dsem}
        dvals = {}
        for op in self.ops:
            if op.dma:
                pool = dsem[op.eng]
                i = dcount[op.eng] % len(pool)
                dcount[op.eng] += 1
                op.sem = pool[i]
                op.prev_val = dvals.get((op.eng, i), 0)
                op.val = op.prev_val + 16
                assert op.val < 60000
                dvals[(op.eng, i)] = op.val
        ops = self.ops

        def run_stream(st, eng):
            known = {}

            def wait(sem, val):
                key = id(sem)
                if known.get(key, 0) >= val:
                    return
                eng.wait_ge(sem, val)
                known[key] = val

            for op in ops:
                if op.eng != st:
                    continue
                for d in sorted(op.deps, key=lambda o: o.idx):
                    if d.dma:
                        wait(d.sem, d.val)
                    else:
                        wait(esem[d.eng], d.ms)
                if op.fn is None:
                    continue
                if op.dma:
                    if op.prev_val:
                        wait(op.sem, op.prev_val)
                    op.fn(eng).then_inc(op.sem, 16)
                else:
                    ins = op.fn(eng)
                    if op.signal:
                        ins.then_inc(esem[st], 1)

        with nc.Block() as block:
            @block.tensor
            def _(e):
                run_stream("pe", e)

            @block.scalar
            def _(e):
                run_stream("act", e)

            @block.vector
            def _(e):
                run_stream("dve", e)

            @block.gpsimd
            def _(e):
                run_stream("pool", e)

            @block.sync
            def _(e):
                run_stream("sp", e)


DEBUG = False
CUT = None


def build_nc():
    nc = bass.Bass("TRN2", target_bir_lowering=False)
    S = Sched()

    def din(name, shape):
        return nc.dram_tensor(name, shape, F32, kind="ExternalInput").ap()

    def dout(name, shape):
        return nc.dram_tensor(name, shape, F32, kind="ExternalOutput").ap()

    xp = din("xp", [2048, D])
    xs = din("xs", [16, D])
    st_h = din("st_h", [16, 8, 128, 128])
    st_s = din("st_s", [16, 1024, 128])
    st_cs = din("st_cs", [16, 3, 1536])
    st_cf = din("st_cf", [16, 2, DFF])
    norm1_w = din("norm1_w", [D])
    w_in = din("w_in", [D, DIN])
    hgrn_lb = din("hgrn_lb", [2, 1024])
    hgrn_norm_w = din("hgrn_norm_w", [128])
    ssm_conv_w = din("ssm_conv_w", [4, 1536])
    ssm_conv_b = din("ssm_conv_b", [1536])
    ssm_dt_bias = din("ssm_dt_bias", [16])
    ssm_a_log = din("ssm_a_log", [16])
    ssm_d = din("ssm_d", [16])
    ssm_norm_w = din("ssm_norm_w", [1024])
    w_out = din("w_out", [2048, D])
    norm2_w = din("norm2_w", [D])
    w_up = din("w_up", [D, 2 * DFF])
    ffn_conv_w = din("ffn_conv_w", [3, DFF])
    ffn_conv_b = din("ffn_conv_b", [DFF])
    w_down = din("w_down", [DFF, D])
    final_norm_w = din("final_norm_w", [D])

    y_p = dout("y_p", [2048, D])
    y_s = dout("y_s", [16, D])
    hg_p = dout("hg_p", [8, 128, 128])
    hg_s = dout("hg_s", [16, 8, 128, 128])
    sm_p = dout("sm_p", [1024, 128])
    sm_s = dout("sm_s", [16, 1024, 128])
    cs_p = dout("cs_p", [3, 1536])
    cs_s = dout("cs_s", [16, 3, 1536])
    cf_p = dout("cf_p", [2, DFF])
    cf_s = dout("cf_s", [16, 2, DFF])

    out_ops = []
    dbg_mix = nc.dram_tensor("dbg_mix", [2, 128, 9, 2048], BF16, kind="ExternalOutput").ap() if DEBUG else None

    def mm(out, lhsT, rhs, start, stop, r, w):
        S.add("pe", lambda e: e.matmul(out, lhsT=lhsT, rhs=rhs, start=start, stop=stop), r, w)

    def tr(out, in_, ident, r, w):
        S.add("pe", lambda e: e.transpose(out=out, in_=in_, identity=ident), r, w)

    def act(out, in_, func, r, w, bias=None, scale=None, accum=None):
        kw = {}
        if bias is not None:
            kw["bias"] = bias
        if scale is not None:
            kw["scale"] = scale
        if accum is not None:
            kw["accum_out"] = accum
        S.add("act", lambda e: e.activation(out=out, in_=in_, func=func, **kw), r, w)

    def ts(eng, out, in0, s1, s2, op0, op1, r, w):
        if op1 is None:
            S.add(eng, lambda e: e.tensor_scalar(out=out, in0=in0, scalar1=s1, scalar2=None, op0=op0), r, w)
        else:
            S.add(eng, lambda e: e.tensor_scalar(out=out, in0=in0, scalar1=s1, scalar2=s2, op0=op0, op1=op1), r, w)

    def stt(out, in0, scalar, in1, op0, op1, r, w):
        S.add("dve", lambda e: e.scalar_tensor_tensor(out=out, in0=in0, scalar=scalar, in1=in1, op0=op0, op1=op1), r, w)

    def tt(eng, out, in0, in1, op, r, w):
        S.add(eng, lambda e: e.tensor_tensor(out=out, in0=in0, in1=in1, op=op), r, w)

    def cp(eng, out, in_, r, w):
        if eng == "act":
            S.add("act", lambda e: e.copy(out=out, in_=in_), r, w)
        else:
            S.add(eng, lambda e: e.tensor_copy(out=out, in_=in_), r, w)

    def mset(eng, ap, val, w):
        S.add(eng, lambda e: e.memset(ap, val), (), w)

    def dma(q, out, in_, r, w, slow=False):
        if slow:
            return S.add(q, lambda e: e.dma_start(out=out, in_=in_, allow_slow_non_contiguous=True), r, w, dma=True)
        return S.add(q, lambda e: e.dma_start(out=out, in_=in_), r, w, dma=True)

    def store(out, in_, r, slow=False):
        if S.rec is not None:
            if slow:
                fn = lambda e: e.dma_start(out=out, in_=in_, allow_slow_non_contiguous=True)
            else:
                fn = lambda e: e.dma_start(out=out, in_=in_)
            S.rec.append(lambda: out_ops.append(S._add("sp", fn, r, (), True)))
            return
        out_ops.append(dma("sp", out, in_, r, (), slow=slow))

    with ExitStack() as es:
        uniq = {"n": 0}

        def sbt(stack, name, shape, dt):
            uniq["n"] += 1
            return stack.enter_context(nc.sbuf_tensor("%s_%d" % (name, uniq["n"]), shape, dt))

        psb = [es.enter_context(nc.psum_tensor("psb%d" % i, [128, 512], F32)) for i in range(8)]
        bank_state = {"next": 0, "reserved": set()}

        def newbank():
            pool = bank_state.get("pool")
            if pool is not None:
                key = "next_%s" % (pool,)
                while True:
                    b = pool[bank_state.get(key, 0) % len(pool)]
                    bank_state[key] = bank_state.get(key, 0) + 1
                    if b not in bank_state["reserved"]:
                        return b
            while True:
                b = bank_state["next"] % 8
                bank_state["next"] += 1
                if b not in bank_state["reserved"]:
                    return b

        def with_pool(pool, f, *a):
            old = bank_state.get("pool")
            bank_state["pool"] = pool
            try:
                return S.record(f, *a)
            finally:
                bank_state["pool"] = old

        def pk(b):
            return "ps%d" % b

        def hold(*bs):
            bank_state["reserved"].update(bs)

        def release(*bs):
            bank_state["reserved"].difference_update(bs)

        ones_f = sbt(es, "ones_f", [128, 128], F32)
        zeros_f = sbt(es, "zeros_f", [128, 128], F32)
        ident_f = sbt(es, "ident_f", [128, 128], F32)
        ident_bf = sbt(es, "ident_bf", [128, 128], BF16)
        U_f = sbt(es, "U_f", [128, 128], F32)
        maskBD = sbt(es, "maskBD", [128, 128], F32)
        negm4 = sbt(es, "negm4", [128, 4, 128], BF16)
        m01 = sbt(es, "m01", [128, 512], F32)
        R48 = sbt(es, "R48", [48, 16, 128], F32)
        L48 = sbt(es, "L48", [48, 128], F32)
        gwb = sbt(es, "gwb", [128, 128], F32)
        dtb_b = sbt(es, "dtb_b", [128, 16], F32)
        A_b = sbt(es, "A_b", [128, 16], F32)
        D_b = sbt(es, "D_b", [128, 16], F32)
        lbraw = sbt(es, "lbraw", [128, 2, 8], F32)
        lb_col = sbt(es, "lb_col", [128, 8], F32)
        c0_col = sbt(es, "c0_col", [128, 8], F32)
        c1_col = sbt(es, "c1_col", [128, 8], F32)
        gwbh = sbt(es, "gwbh", [128, 128], F32)
        half_f = sbt(es, "half_f", [128, 1], F32)
        tgS = sbt(es, "tgS", [16, 128], F32)
        oml_col = sbt(es, "oml_col", [128, 8], F32)
        cw_col = sbt(es, "cw_col", [128, 12, 4], F32)
        cb_col = sbt(es, "cb_col", [128, 12], F32)
        fw_col = sbt(es, "fw_col", [128, 22, 3], F32)
        fb_col = sbt(es, "fb_col", [128, 22], F32)
        S_h = sbt(es, "S_h", [128, 8, 128], F32)
        ST = sbt(es, "ST", [128, 1024], F32)
        ST_bf = sbt(es, "ST_bf", [128, 1024], BF16)
        halo_s = sbt(es, "halo_s", [128, 12, 3], F32)
        halo_f = sbt(es, "halo_f", [128, 22, 2], F32)
        hT = sbt(es, "hT", [128, 8, 1040], BF16)
        big = sbt(es, "big", [128, 9, 1024], F32)
        mix = big[:].bitcast(BF16)
        x2 = big
        wst = [sbt(es, "wst%d" % i, [128, 8, 512], BF16) for i in range(2)]
        wdt = sbt(es, "wdt", [128, 8, 16], BF16)
        wslot = {"n": 0}
        qS = sbt(es, "qS", [128, 8, 16], F32)
        fS = sbt(es, "fS", [128, 8, 16], F32)
        kS = sbt(es, "kS", [128, 8, 16], F32)
        vS = sbt(es, "vS", [16, 1024], F32)
        gsS = sbt(es, "gsS", [16, 1024], F32)
        tmpS = sbt(es, "tmpS", [128, 32], F32)
        xrawS = sbt(es, "xrawS", [128, 12, 16], F32)
        zS = sbt(es, "zS", [16, 1024], BF16)
        dtS_raw = sbt(es, "dtS_raw", [16, 16], F32)

        def next_wslot():
            i = wslot["n"] % 2
            wslot["n"] += 1
            return i

        def aff(out, in_, pattern, cmp, fill, base, cm, r, w):
            S.add("pool", lambda e: e.affine_select(out=out, in_=in_, pattern=pattern, compare_op=cmp, fill=fill,
                                                    base=base, channel_multiplier=cm), r, w)

        mset("pool", ones_f[:], 1.0, ["ones_f"])
        mset("pool", zeros_f[:], 0.0, ["zeros_f"])
        aff(ident_f[:], ones_f[:], [[-1, 128]], ALU.is_equal, 0.0, 0, 1, ["ones_f"], ["ident_f"])
        cp("pool", ident_bf[:], ident_f[:], ["ident_f"], ["ident_bf"])
        aff(U_f[:], ones_f[:], [[1, 128]], ALU.is_ge, 0.0, 0, -1, ["ones_f"], ["U_f"])
        cp("pool", maskBD[:], U_f[:], ["U_f"], ["maskBD"])
        mset("pool", maskBD[0:64, 64:128], 0.0, ["maskBD"])
        negf = sbt(es, "negf", [128, 128], F32)
        aff(negf[:], zeros_f[:], [[1, 128]], ALU.is_ge, NEG, 0, -1, ["zeros_f"], ["negf"])
        for i in range(4):
            cp("pool", negm4[:, i, :], negf[:], ["negf"], ["negm4"])
        mset("pool", m01[:], 1.0, ["m01"])
        mset("pool", m01[:].rearrange("p (c t) -> p c t", t=64)[:, :, 0:1], 0.0, ["m01"])
        mset("pool", R48[:], 0.0, ["R48c"])
        mset("pool", L48[:], 0.0, ["L48c"])
        mset("pool", L48[0:16, :], 1.0, ["L48c"])
        negones = sbt(es, "negones", [128, 1], F32)
        mset("pool", negones[:], -1.0, ["negones"])
        aff(R48[32:48, :, :], negones[32:48, 0:1].unsqueeze(1).to_broadcast([16, 16, 128]), [[-1, 16], [0, 128]],
            ALU.is_equal, 0.0, 0, 1, ["negones", "R48c"], ["R48c"])
        mset("pool", ST[:], 0.0, ["ST"])
        mset("pool", ST_bf[:], 0.0, ["ST_bf"])
        mset("pool", S_h[:], 0.0, ["S_h"])
        mset("pool", halo_s[:], 0.0, ["halo_s"])
        mset("pool", halo_f[:], 0.0, ["halo_f"])

        dma("sp", gwb[:], hgrn_norm_w.partition_broadcast(128), (), ["gwb"])
        dma("sp", dtb_b[:], ssm_dt_bias.partition_broadcast(128), (), ["dtb_b"])
        dma("sp", A_b[:], ssm_a_log.partition_broadcast(128), (), ["A_b"])
        dma("sp", D_b[:], ssm_d.partition_broadcast(128), (), ["D_b"])
        for r_ in range(2):
            dma("sp", lbraw[:, r_, :], hgrn_lb[r_].rearrange("(h p) -> p h", p=128), (), ["lbraw"], slow=True)
        def load_conv_cols():
            for j_ in range(4):
                dma("sp", cw_col[:, :, j_], ssm_conv_w[j_].rearrange("(c p) -> p c", p=128), (), ["cw_col"], slow=True)
            dma("sp", cb_col[:], ssm_conv_b.rearrange("(c p) -> p c", p=128), (), ["cb_col"], slow=True)
            for j_ in range(3):
                dma("sp", fw_col[:, :, j_], ffn_conv_w[j_].rearrange("(c p) -> p c", p=128), (), ["fw_col"], slow=True)
            dma("sp", fb_col[:], ffn_conv_b.rearrange("(c p) -> p c", p=128), (), ["fb_col"], slow=True)
        dma("pool", wdt[:], w_in[:, C_DT:C_DT + 16].rearrange("(k p) n -> p k n", p=128), (), ["wdt"])
        act(A_b[:], A_b[:], AF.Exp, ["A_b"], ["A_b"])
        ts("dve", A_b[:], A_b[:], -1.0, None, ALU.mult, None, ["A_b"], ["A_b"])
        tt("dve", lb_col[:], lbraw[:, 0, :], lbraw[:, 1, :], ALU.subtract, ["lbraw"], ["lb_col"])
        act(oml_col[:], lb_col[:], AF.Sigmoid, ["lb_col"], ["oml_col"], scale=-1.0)
        act(lb_col[:], lb_col[:], AF.Sigmoid, ["lb_col"], ["lb_col"])
        ts("dve", c1_col[:], oml_col[:], 0.5, None, ALU.mult, None, ["oml_col"], ["c1_col"])
        tt("dve", c0_col[:], lb_col[:], c1_col[:], ALU.add, ["lb_col", "c1_col"], ["c0_col"])
        ts("dve", gwbh[:], gwb[:], 0.5, None, ALU.mult, None, ["gwb"], ["gwbh"])
        mset("pool", half_f[:], 0.5, ["half_f"])

        def rstd_small(ssq, n_feat, r, w):
            act(ssq, ssq, AF.Ln, r, w, bias=EPS, scale=1.0 / n_feat)
            act(ssq, ssq, AF.Exp, w, w, scale=-0.5)

        def load_w(dst, src, r, w):
            return dma("pool", dst, src, r, w)

        def wq(slot, a, b):
            return ["wst%dq%d" % (slot, j) for j in range(a // 128, (b - 1) // 128 + 1)]

        def wv(src, c0, n):
            return src[:, c0:c0 + n].rearrange("(k p) n -> p k n", p=128)

        job_list = []
        for sb_ in range(2):
            for wb in range(3):
                job_list.append((("ssdx", sb_, wb), [(0, 512, wv(w_in, C_X + wb * 512, 512))]))
            for wb in range(2):
                job_list.append((("ssdz", sb_, wb), [(0, 512, wv(w_in, C_Z + wb * 512, 512))]))
            for h_ in range(8):
                job_list.append((("hgrn", sb_, h_), [(i * 128, 128, wv(w_in, c0 + h_ * 128, 128))
                                                     for i, c0 in enumerate((C_Q, C_F, C_I, C_G))]))
            for fg_ in range(2):
                for pr in range(6):
                    ft0 = fg_ * 11 + pr * 2
                    n = 256 if pr < 5 else 128
                    job_list.append((("ffn", sb_, fg_, pr), [(0, n, wv(w_up, ft0 * 128, n)),
                                                             (256, n, wv(w_up, DFF + ft0 * 128, n))]))
        job_index = {k: i for i, (k, _) in enumerate(job_list)}
        wjobs = {}
        jstate = {"issued": 0}

        def w_use(key):
            idx = job_index[key]
            while jstate["issued"] <= min(idx + 1, len(job_list) - 1):
                k_, parts = job_list[jstate["issued"]]
                slot = next_wslot()
                for (c0, n, src) in parts:
                    load_w(wst[slot][:, :, c0:c0 + n], src, (), wq(slot, c0, c0 + n))
                wjobs[k_] = slot
                jstate["issued"] += 1
            return wjobs[key]

        SAMPLE = True

        def ttr(out, in0, in1, accum, r, w):
            S.add("dve", lambda e: e.scalar_tensor_tensor(out=out, in0=in0, scalar=1.0, in1=in1, op0=ALU.mult,
                                                          op1=ALU.mult, accum_out=accum), r, w)

        def build_selS(stack):
            selS = sbt(stack, "selS", [16, 16, 128], F32)
            aff(selS[:], ones_f[0:16, 0:1].unsqueeze(1).to_broadcast([16, 16, 128]), [[-1, 16], [0, 128]],
                ALU.is_equal, 0.0, 0, 1, ["ones_f"], ["selS"])
            return selS

        def sample_hgrn(ss):
            if True:
                selS = build_selS(ss)
                eyeS = sbt(ss, "eyeS", [128, 16, 16], F32)
                Qm = sbt(ss, "Qm", [128, 8, 256], F32)
                Sin = [sbt(ss, "SinH%d" % i, [128, 8, 128], F32) for i in range(3)]
                tmpH = [sbt(ss, "tmpH%d" % i, [128, 8, 128], F32) for i in range(2)]
                gwS = sbt(ss, "gwS", [16, 1024], F32)
                junkS = sbt(ss, "junkS", [16, 128], F32)
                ssqS = sbt(ss, "ssqS", [16, 8], F32)
                aff(eyeS[:], half_f[:, 0:1].unsqueeze(1).to_broadcast([128, 16, 16]), [[1, 16], [-1, 16]],
                    ALU.is_equal, 0.0, 0, 0, ["half_f"], ["eyeS"])
                for h in range(8):
                    tt("dve", Qm[:, h, :].rearrange("p (j c) -> p j c", c=16),
                       qS[:, h, :].unsqueeze(1).to_broadcast([128, 16, 16]), eyeS[:], ALU.mult, ["qS", "eyeS"], ["Qm"])
                bo_ = [newbank(), newbank()]
                bank_state["reserved"].update(bo_)
                def h_s1(j):
                    sl = j % 3
                    sk = "SinH%d" % sl
                    tj = j % 2
                    dma("sp", Sin[sl][:], st_h[j].rearrange("h k v -> k h v"), (), [sk])
                    bvb = [newbank(), newbank()]
                    for half in range(2):
                        mm(psb[bvb[half]][:, :], selS[:, j, :], vS[:, half * 512:(half + 1) * 512], True, True,
                           ["selS", "vS"], [pk(bvb[half])])
                    for h in range(8):
                        act(tmpH[tj][:, h, :], psb[bvb[h // 4]][:, (h % 4) * 128:(h % 4 + 1) * 128], AF.Identity,
                            [pk(bvb[h // 4]), "kS"], ["tmpH%d" % tj], scale=kS[:, h, j:j + 1])

                def h_s2(j):
                    sl = j % 3
                    sk = "SinH%d" % sl
                    tj = j % 2
                    tt("pool", Sin[sl][:], Sin[sl][:], fS[:, :, j:j + 1].to_broadcast([128, 8, 128]), ALU.mult,
                       [sk, "fS"], [sk])
                    tt("dve", Sin[sl][:], Sin[sl][:], tmpH[tj][:], ALU.add, [sk, "tmpH%d" % tj], [sk])
                    for h in range(8):
                        Sv = Sin[sl][:, h, :]
                        mm(psb[bo_[h // 4]][0:16, (h % 4) * 128:(h % 4 + 1) * 128], Qm[:, h, j * 16:(j + 1) * 16], Sv,
                           j == 0 and h % 4 == 0, j == 15 and h % 4 == 3, ["Qm", sk], [pk(bo_[h // 4])])
                    store(hg_s[j].rearrange("h k v -> k h v"), Sin[sl][:], [sk])

                h_s1(0)
                for j in range(16):
                    if j + 1 < 16:
                        h_s1(j + 1)
                    h_s2(j)
                for h in range(8):
                    oc = psb[bo_[h // 4]][0:16, (h % 4) * 128:(h % 4 + 1) * 128]
                    act(junkS[:], oc, AF.Square, [pk(bo_[h // 4])], ["junkS", "ssqS"], accum=ssqS[:, h:h + 1])
                rstd_small(ssqS[:], 128, ["ssqS"], ["ssqS"])
                tt("dve", gwS[:].rearrange("p (h v) -> p h v", v=128), gsS[:].rearrange("p (h v) -> p h v", v=128),
                   gwbh[0:16, :].unsqueeze(1).to_broadcast([16, 8, 128]), ALU.mult, ["gsS", "gwbh"], ["gwS"])
                for h in range(8):
                    oc = psb[bo_[h // 4]][0:16, (h % 4) * 128:(h % 4 + 1) * 128]
                    stt(mix[0:16, 8, h * 128:(h + 1) * 128], oc, ssqS[:, h:h + 1], gwS[:, h * 128:(h + 1) * 128],
                        ALU.mult, ALU.mult, [pk(bo_[h // 4]), "ssqS", "gwS"], ["mix8"])
                bank_state["reserved"].difference_update(bo_)

        def sample_ssd(ss):
            if True:
                selS = build_selS(ss)
                E = sbt(ss, "E", [16, 8, 128], F32)
                stcs_t = [sbt(ss, "stcs_t%d" % i, [16, 3, 128], F32) for i in range(2)]
                CST = [sbt(ss, "CST%d" % i, [128, 3, 16], F32) for i in range(2)]
                accS = [sbt(ss, "accS%d" % i, [128, 16], F32) for i in range(2)]
                xcS = sbt(ss, "xcS", [128, 12, 16], F32)
                xtokS = [sbt(ss, "xtokS%d" % i, [16, 128], F32) for i in range(2)]
                dtS = sbt(ss, "dtS", [16, 32], F32)
                dtdecT = sbt(ss, "dtdecT", [16, 34], F32)
                dE = sbt(ss, "dE", [128, 8, 34], F32)
                XdtE = sbt(ss, "XdtE", [128, 8, 16], F32)
                BCtok = sbt(ss, "BCtok", [16, 512], F32)
                Sin = [sbt(ss, "SinS%d" % i, [128, 8, 128], F32) for i in range(3)]
                tmpP = [sbt(ss, "tmpP%d" % i, [128, 8, 128], F32) for i in range(2)]
                junkP = sbt(ss, "junkP", [128, 128], F32)
                yS = sbt(ss, "yS", [128, 8, 16], F32)
                dcol = sbt(ss, "dcol", [16, 1], F32)
                ytokS = sbt(ss, "ytokS", [16, 1024], F32)
                junkS = sbt(ss, "junkS2", [16, 512], F32)
                ssqS = sbt(ss, "ssqS2", [16, 2], F32)
                snwS = sbt(ss, "snwS", [16, 1024], F32)
                dma("sp", snwS[:], ssm_norm_w.partition_broadcast(16), (), ["snwS"])
                dma("sp", dcol[:], ssm_d.rearrange("(h o) -> h o", o=1), (), ["dcol"])
                for a_ in range(2):
                    aff(E[:, :, a_ * 64:(a_ + 1) * 64], ones_f[0:16, 0:1].unsqueeze(1).to_broadcast([16, 8, 64]),
                        [[-2, 8], [0, 64]], ALU.is_equal, 0.0, -a_, 1, ["ones_f"], ["E"])
                mset("pool", yS[:], 0.0, ["yS"])
                store(cs_s[:, 0:2, :], st_cs[:, 1:3, :], [])
                for ft in range(12):
                    ri = ft % 2
                    dma("sp", stcs_t[ri][:], st_cs[:, :, ft * 128:(ft + 1) * 128], (), ["stcs_t%d" % ri])
                    b = newbank()
                    for r_ in range(3):
                        tr(psb[b][:, r_ * 16:(r_ + 1) * 16], stcs_t[ri][:, r_, :], ident_f[0:16, 0:16],
                           ["stcs_t%d" % ri, "ident_f"], [pk(b)])
                    cp("act", CST[ri][:].rearrange("p r j -> p (r j)"), psb[b][:, 0:48], [pk(b)], ["CST%d" % ri])
                    ts("dve", accS[ri][:], xrawS[:, ft, :], cw_col[:, ft, 3:4], cb_col[:, ft:ft + 1], ALU.mult, ALU.add,
                       ["xrawS", "cw_col", "cb_col"], ["accS%d" % ri])
                    for jj in (2, 1, 0):
                        stt(accS[ri][:], CST[ri][:, jj, :], cw_col[:, ft, jj:jj + 1], accS[ri][:], ALU.mult, ALU.add,
                            ["CST%d" % ri, "cw_col", "accS%d" % ri], ["accS%d" % ri])
                    act(xcS[:, ft, :], accS[ri][:], AF.Silu, ["accS%d" % ri], ["xcS"])
                    b2 = newbank()
                    tr(psb[b2][0:16, 0:128], xrawS[:, ft, :], ident_f[:], ["xrawS", "ident_f"], [pk(b2)])
                    cp("act", xtokS[ri][:], psb[b2][0:16, 0:128], [pk(b2)], ["xtokS%d" % ri])
                    store(cs_s[:, 2, ft * 128:(ft + 1) * 128], xtokS[ri][:], ["xtokS%d" % ri])
                tt("dve", dtS[:, 0:16], dtS_raw[:], dtb_b[0:16, :], ALU.add, ["dtS_raw", "dtb_b"], ["dtS"])
                act(dtS[:, 0:16], dtS[:, 0:16], AF.Exp, ["dtS"], ["dtS"])
                act(dtS[:, 0:16], dtS[:, 0:16], AF.Ln, ["dtS"], ["dtS"], bias=1.0)
                tt("dve", dtS[:, 16:32], dtS[:, 0:16], A_b[0:16, :], ALU.mult, ["dtS", "A_b"], ["dtS"])
                act(dtS[:, 16:32], dtS[:, 16:32], AF.Exp, ["dtS"], ["dtS"])
                b = newbank()
                for i in range(2):
                    tr(psb[b][0:16, i * 16:(i + 1) * 16], dtS[:, i * 16:(i + 1) * 16], ident_f[0:16, 0:16],
                       ["dtS", "ident_f"], [pk(b)])
                cp("act", dtdecT[:, 0:32], psb[b][0:16, 0:32], [pk(b)], ["dtdecT"])
                cp("dve", dtdecT[:, 32:34], dcol[:, 0:1].to_broadcast([16, 2]), ["dcol", "dtdecT"], ["dtdecT"])
                b = newbank()
                for c in range(8):
                    mm(psb[b][:, c * 34:(c + 1) * 34], E[:, c, :], dtdecT[:], True, True, ["E", "dtdecT"], [pk(b)])
                cp("act", dE[:].rearrange("p c x -> p (c x)"), psb[b][:, 0:272], [pk(b)], ["dE"])
                tt("dve", XdtE[:], xcS[:, 0:8, :], dE[:, :, 0:16], ALU.mult, ["xcS", "dE"], ["XdtE"])
                b = newbank()
                for i in range(4):
                    tr(psb[b][0:16, i * 128:(i + 1) * 128], xcS[:, 8 + i, :], ident_f[:], ["xcS", "ident_f"], [pk(b)])
                cp("act", BCtok[:], psb[b][0:16, :], [pk(b)], ["BCtok"])
                sbank = {}

                def s_s1(j):
                    sl = j % 3
                    sk = "SinS%d" % sl
                    tj = j % 2
                    dma("sp", Sin[sl][:], st_s[j].rearrange("(c q) n -> q c n", q=128), (), [sk])
                    b = newbank()
                    hold(b)
                    sbank[j] = b
                    mm(psb[b][:, :], selS[:, j, :], BCtok[:], True, True, ["selS", "BCtok"], [pk(b)])
                    for c in range(8):
                        g = c // 4
                        act(tmpP[tj][:, c, :], psb[b][:, g * 128:(g + 1) * 128], AF.Identity, [pk(b), "XdtE"],
                            ["tmpP%d" % tj], scale=XdtE[:, c, j:j + 1])

                def s_s2(j):
                    sl = j % 3
                    sk = "SinS%d" % sl
                    tj = j % 2
                    b = sbank[j]
                    tt("pool", Sin[sl][:], Sin[sl][:], dE[:, :, 16 + j:17 + j].to_broadcast([128, 8, 128]), ALU.mult,
                       [sk, "dE"], [sk])
                    tt("dve", Sin[sl][:], Sin[sl][:], tmpP[tj][:], ALU.add, [sk, "tmpP%d" % tj], [sk])
                    tt("dve", tmpP[tj][:].rearrange("p (g i) n -> p g i n", g=2),
                       Sin[sl][:].rearrange("p (g i) n -> p g i n", g=2),
                       psb[b][:, 256:512].rearrange("p (g n) -> p g n", g=2).unsqueeze(2).to_broadcast([128, 2, 4, 128]),
                       ALU.mult, [sk, pk(b), "tmpP%d" % tj], ["tmpP%d" % tj])
                    release(b)
                    S.add("dve", lambda e, o=yS[:, :, j], i_=tmpP[tj][:]: e.tensor_reduce(
                        out=o, in_=i_, axis=mybir.AxisListType.X, op=ALU.add), ["tmpP%d" % tj, "yS"], ["yS"])
                    store(sm_s[j].rearrange("(c q) n -> q c n", q=128), Sin[sl][:], [sk])

                s_s1(0)
                for j in range(16):
                    if j + 1 < 16:
                        s_s1(j + 1)
                    s_s2(j)
                for c in range(8):
                    stt(yS[:, c, :], xcS[:, c, :], dE[:, c, 32:33], yS[:, c, :], ALU.mult, ALU.add,
                        ["xcS", "dE", "yS"], ["yS"])
                for half in range(2):
                    b = newbank()
                    for i in range(4):
                        tr(psb[b][0:16, i * 128:(i + 1) * 128], yS[:, half * 4 + i, :], ident_f[:], ["yS", "ident_f"], [pk(b)])
                    tt("dve", ytokS[:, half * 512:(half + 1) * 512], psb[b][0:16, :], zS[:, half * 512:(half + 1) * 512],
                       ALU.mult, [pk(b), "zS"], ["ytokS"])
                for g in range(2):
                    act(junkS[:], ytokS[:, g * 512:(g + 1) * 512], AF.Square, ["ytokS"], ["junkS2", "ssqS2"],
                        accum=ssqS[:, g:g + 1])
                rstd_small(ssqS[:], 512, ["ssqS2"], ["ssqS2"])
                for g in range(2):
                    stt(mix[0:16, 8, 1024 + g * 512:1024 + (g + 1) * 512], ytokS[:, g * 512:(g + 1) * 512],
                        ssqS[:, g:g + 1], snwS[:, g * 512:(g + 1) * 512], ALU.mult, ALU.mult,
                        ["ytokS", "ssqS2", "snwS"], ["mix8"])

        def ffn_sample_tile(ft, ftl, ri, stcf_t, CFT, grawS, gaccS, gtokS, actT, bvvS):
            if not SAMPLE:
                return
            if ft == 0:
                store(cf_s[:, 0, :], st_cf[:, 1, :], [])
            dma("sp", stcf_t[ri][:], st_cf[:, :, ft * 128:(ft + 1) * 128], (), ["stcf_t%d" % ri])
            b = newbank()
            for r_ in range(2):
                tr(psb[b][:, r_ * 16:(r_ + 1) * 16], stcf_t[ri][:, r_, :], ident_f[0:16, 0:16],
                   ["stcf_t%d" % ri, "ident_f"], [pk(b)])
            cp("act", CFT[ri][:].rearrange("p r j -> p (r j)"), psb[b][:, 0:32], [pk(b)], ["CFT%d" % ri])
            ts("dve", gaccS[ri][:], grawS[ri][:], fw_col[:, ft, 2:3], fb_col[:, ft:ft + 1], ALU.mult, ALU.add,
               ["grawS%d" % ri, "fw_col", "fb_col"], ["gaccS%d" % ri])
            for jj in (1, 0):
                stt(gaccS[ri][:], CFT[ri][:, jj, :], fw_col[:, ft, jj:jj + 1], gaccS[ri][:], ALU.mult, ALU.add,
                    ["CFT%d" % ri, "fw_col", "gaccS%d" % ri], ["gaccS%d" % ri])
            act(gaccS[ri][:], gaccS[ri][:], AF.Silu, ["gaccS%d" % ri], ["gaccS%d" % ri])
            tt("dve", actT[:, ftl, 1024:1040], gaccS[ri][:], psb[bvvS][:, 0:16], ALU.mult,
               ["gaccS%d" % ri, pk(bvvS)], ["actT%d" % ftl])
            b2 = newbank()
            tr(psb[b2][0:16, 0:128], grawS[ri][:], ident_f[:], ["grawS%d" % ri, "ident_f"], [pk(b2)])
            cp("act", gtokS[ri][:], psb[b2][0:16, 0:128], [pk(b2)], ["gtokS%d" % ri])
            store(cf_s[:, 1, ft * 128:(ft + 1) * 128], gtokS[ri][:], ["gtokS%d" % ri])

        for sb in range(2):
            tok0 = sb * 1024
            ntile = 9 if sb == 1 else 8
            ncols = 1040 if sb == 1 else 1024
            blocks = [(0, 512), (512, 512)] + ([(1024, 16)] if sb == 1 else [])

            def rows_of(t):
                return 16 if t == 8 else 128

            S.mark("A%d" % sb)
            with ExitStack() as sa:
                xt = [sbt(sa, "xt%d" % i, [128, D], F32) for i in range(2)]
                sq = sbt(sa, "sq", [128, D], F32)
                hn = [sbt(sa, "hn%d" % i, [128, D], BF16) for i in range(2)]
                ssA = sbt(sa, "ssA", [128, 2], F32)
                w1b = sbt(sa, "w1b", [128, D], F32)
                dma("sp", w1b[:], norm1_w.partition_broadcast(128), (), ["w1b"])
                def a_s1(t):
                    i = t % 2
                    rw = rows_of(t)
                    src = xs if t == 8 else xp[tok0 + t * 128: tok0 + (t + 1) * 128, :]
                    dma("sp", xt[i][0:rw, :], src, (), ["xt%d" % i])
                    act(sq[0:rw, :], xt[i][0:rw, :], AF.Square, ["xt%d" % i], ["sq", "ssA%d" % i], accum=ssA[0:rw, i:i + 1])
                    rstd_small(ssA[0:rw, i:i + 1], D, ["ssA%d" % i], ["ssA%d" % i])

                def a_s2(t):
                    i = t % 2
                    rw = rows_of(t)
                    stt(hn[i][0:rw, :], xt[i][0:rw, :], ssA[0:rw, i:i + 1], w1b[0:rw, :], ALU.mult, ALU.mult,
                        ["xt%d" % i, "ssA%d" % i, "w1b"], ["hn%d" % i])
                    b = newbank()
                    pT = psb[b][:].bitcast(BF16)
                    for k in range(8):
                        tr(pT[:, k * 128:k * 128 + rw], hn[i][0:rw, k * 128:(k + 1) * 128], ident_bf[0:rw, 0:rw],
                           ["hn%d" % i, "ident_bf"], [pk(b)])
                    cp("act", hT[:, :, t * 128:t * 128 + rw],
                       pT[:, 0:1024].rearrange("p (k c) -> p k c", k=8)[:, :, 0:rw], [pk(b)], ["hT"])

                a_s1(0)
                for t in range(ntile):
                    if t + 1 < ntile:
                        a_s1(t + 1)
                    if t == 1 and sb == 0:
                        load_conv_cols()
                    a_s2(t)
            S.barrier()

            def wview(c0, n):
                return w_in[:, c0:c0 + n].rearrange("(k p) n -> p k n", p=128)

            S.mark("B1_%d" % sb)
            with ExitStack() as s1:
                xc = sbt(s1, "xc", [128, 12, 1024], BF16)
                zs = sbt(s1, "zs", [128, 8, 1024], BF16)
                xraw = [sbt(s1, "xraw%d" % i, [128, 3 + 1024], F32) for i in range(2)]
                cacc = [sbt(s1, "cacc%d" % i, [128, 1024], F32) for i in range(2)]
                dtraw = sbt(s1, "dtraw", [128, 8, 16], F32)
                dtv = sbt(s1, "dtv", [128, 8, 16], F32)
                a_ext = sbt(s1, "a_ext", [128, 8, 48], F32)
                Xtok = sbt(s1, "Xtok", [128, 1024], BF16)
                Xdt = sbt(s1, "Xdt", [128, 1024], BF16)
                Xd_ = [sbt(s1, "Xd%d" % i, [128, 1024], BF16) for i in range(2)]
                XD = sbt(s1, "XD", [128, 1024], BF16)
                Btok_ = [sbt(s1, "Btok%d" % i, [128, 256], BF16) for i in range(2)]
                ex3_ = [sbt(s1, "ex3_%d" % i, [128, 48], F32) for i in range(2)]
                AcsT_sb = sbt(s1, "AcsT_sb", [16, 128], F32)
                GT_sb = sbt(s1, "GT_sb", [128, 256], F32)
                Lx = [sbt(s1, "Lx%d" % i, [128, 512], F32) for i in range(2)]
                M_sb = sbt(s1, "M_sb", [128, 16, 128], BF16)
                ytmp = sbt(s1, "ytmp", [128, 1024], F32)
                sqs = sbt(s1, "sqs", [128, 512], F32)
                ssq2 = sbt(s1, "ssq2", [128, 2], F32)
                snwb = sbt(s1, "snwb", [128, 1024], F32)
                dma("sp", snwb[:], ssm_norm_w.partition_broadcast(128), (), ["snwb"])

                xsteps = [(ft, bi, c0b) for ft in range(12) for bi, (c0b, nb) in enumerate(blocks) if nb == 512]
                xst = {}

                def x_s1(n):
                    ft, bi, c0b = xsteps[n]
                    ri = ft % 2
                    xr = xraw[ri]
                    j = ft % 4
                    if bi == 0:
                        if j == 0:
                            xst["slot"] = w_use(("ssdx", sb, ft // 4))
                        cp("act", xr[:, 0:3], halo_s[:, ft, :], ["halo_s%d" % ft], ["xraw%dh" % ri])
                    slot = xst["slot"]
                    todo = [(c0b, 512)] + ([(1024, 16)] if (bi == 1 and sb == 1) else [])
                    for (c0, nb) in todo:
                        b = newbank()
                        for k in range(8):
                            mm(psb[b][:, 0:nb], wst[slot][:, k, j * 128:(j + 1) * 128], hT[:, k, c0:c0 + nb],
                               k == 0, k == 7, wq(slot, j * 128, (j + 1) * 128) + ["hT"], [pk(b)])
                        if nb == 16:
                            cp("act", xrawS[:, ft, :], psb[b][:, 0:16], [pk(b)], ["xrawS"])
                        else:
                            cp("act", xr[:, 3 + c0:3 + c0 + 512], psb[b][:, 0:512], [pk(b)], ["xraw%db%d" % (ri, bi)])
                    if bi == 1:
                        cp("act", halo_s[:, ft, :], xr[:, 1024:1027], ["xraw%db1" % ri], ["halo_s%d" % ft])

                def x_s2(n):
                    ft, bi, c0b = xsteps[n]
                    ri = ft % 2
                    xr = xraw[ri]
                    cb_ = cacc[ri][:, c0b:c0b + 512]
                    ck = "cacc%d_%d" % (ri, bi)
                    xk = ["xraw%dh" % ri, "xraw%db0" % ri] + (["xraw%db1" % ri] if bi == 1 else [])
                    ts("dve", cb_, xr[:, 3 + c0b:3 + c0b + 512], cw_col[:, ft, 3:4], cb_col[:, ft:ft + 1],
                       ALU.mult, ALU.add, xk + ["cw_col", "cb_col"], [ck])
                    for jj in (2, 1, 0):
                        stt(cb_, xr[:, jj + c0b:jj + c0b + 512], cw_col[:, ft, jj:jj + 1], cb_, ALU.mult, ALU.add,
                            xk + ["cw_col", ck], [ck])

                def x_s3(n):
                    ft, bi, c0b = xsteps[n]
                    ri = ft % 2
                    act(xc[:, ft, c0b:c0b + 512], cacc[ri][:, c0b:c0b + 512], AF.Silu, ["cacc%d_%d" % (ri, bi)], ["xc"])

                x_s1(0)
                for n in range(len(xsteps)):
                    if n + 1 < len(xsteps):
                        x_s1(n + 1)
                    x_s2(n)
                    x_s3(n)
                if sb == 1:
                    for ft in range(12):
                        store(cs_p[:, ft * 128:(ft + 1) * 128].rearrange("t p -> p t"), halo_s[:, ft, :], ["halo_s%d" % ft],
                              slow=True)
                for wb in range(2):
                    slot = w_use(("ssdz", sb, wb))
                    for t in range(ntile):
                        rw = rows_of(t)
                        b = newbank()
                        for k in range(8):
                            mm(psb[b][0:rw, :], hT[:, k, t * 128:t * 128 + rw], wst[slot][:, k, :], k == 0, k == 7,
                               wq(slot, 0, 512) + ["hT"], [pk(b)])
                        if t == 8:
                            act(zS[:, wb * 512:(wb + 1) * 512], psb[b][0:16, :], AF.Silu, [pk(b)], ["zS"])
                        else:
                            act(zs[:, t, wb * 512:(wb + 1) * 512], psb[b][:, :], AF.Silu, [pk(b)], ["zs"])
                b = newbank()
                for t in range(ntile):
                    rw = rows_of(t)
                    for k in range(8):
                        mm(psb[b][0:rw, t * 16:(t + 1) * 16], hT[:, k, t * 128:t * 128 + rw], wdt[:, k, :], k == 0, k == 7,
                           ["wdt", "hT"], [pk(b)])
                cp("dve", dtraw[:], psb[b][:, 0:128].rearrange("p (t h) -> p t h", h=16), [pk(b)], ["dtraw"])
                if sb == 1:
                    cp("dve", dtS_raw[:], psb[b][0:16, 128:144], [pk(b)], ["dtS_raw"])
                tt("dve", dtv[:], dtraw[:], dtb_b[:].unsqueeze(1).to_broadcast([128, 8, 16]), ALU.add,
                   ["dtraw", "dtb_b"], ["dtv"])
                act(dtv[:], dtv[:], AF.Exp, ["dtv"], ["dtv"])
                act(dtv[:], dtv[:], AF.Ln, ["dtv"], ["dtv"], bias=1.0)
                mset("pool", a_ext[:], 0.0, ["a_ext"])
                for o_ in (0, 32):
                    tt("dve", a_ext[:, :, o_:o_ + 16], dtv[:], A_b[:].unsqueeze(1).to_broadcast([128, 8, 16]), ALU.mult,
                       ["dtv", "A_b", "a_ext"], ["a_ext"])

                R48f = R48[:].rearrange("p h s -> p (h s)")
                bYs = {}

                def stage1(c):
                    p_ = c % 2
                    ex3, Btok, Xd = ex3_[p_], Btok_[p_], Xd_[p_]
                    kx, kb_, kd_ = "ex3_%d" % p_, "Btok%d" % p_, "Xd%d" % p_
                    cols = slice(c * 128, (c + 1) * 128)
                    bX = newbank()
                    pX = psb[bX][:].bitcast(BF16)
                    for ft in range(8):
                        tr(pX[:, ft * 128:(ft + 1) * 128], xc[:, ft, cols], ident_bf[:], ["xc", "ident_bf"], [pk(bX)])
                    cp("act", Xtok[:], pX[:, 0:1024], [pk(bX)], ["Xtok"])
                    bB = newbank()
                    pB = psb[bB][:].bitcast(BF16)
                    for g in range(2):
                        tr(pB[:, g * 128:(g + 1) * 128], xc[:, 8 + g, cols], ident_bf[:], ["xc", "ident_bf"], [pk(bB)])
                    cp("act", Btok[:], pB[:, 0:256], [pk(bB)], [kb_])
                    bC = newbank()
                    mm(psb[bC][:, 0:16], U_f[:], a_ext[:, c, 0:16], True, True, ["U_f", "a_ext"], [pk(bC)])
                    mm(psb[bC][:, 16:32], ones_f[:], a_ext[:, c, 0:16], True, True, ["ones_f", "a_ext"], [pk(bC)])
                    mm(psb[bC][0:48, 128:256], a_ext[:, c, :], U_f[:], True, True, ["U_f", "a_ext"], [pk(bC)])
                    cp("dve", ex3[:, 0:32], psb[bC][:, 0:32], [pk(bC)], [kx])
                    tt("dve", ex3[:, 32:48], ex3[:, 16:32], ex3[:, 0:16], ALU.subtract, [kx], [kx])
                    act(ex3[:], ex3[:], AF.Exp, [kx], [kx])
                    cp("act", AcsT_sb[:], psb[bC][0:16, 128:256], [pk(bC)], ["AcsT_sb"])
                    cp("dve", L48[32:48, :], psb[bC][32:48, 128:256], [pk(bC)], ["L48"])
                    aff(R48[0:16, :, :], AcsT_sb[:].unsqueeze(1).to_broadcast([16, 16, 128]), [[-1, 16], [0, 128]],
                        ALU.is_equal, 0.0, 0, 1, ["AcsT_sb"], ["R48"])
                    X3 = Xtok[:].rearrange("p (h q) -> p h q", q=64)
                    tt("dve", Xdt[:].rearrange("p (h q) -> p h q", q=64), X3,
                       dtv[:, c, :].unsqueeze(2).to_broadcast([128, 16, 64]), ALU.mult, ["Xtok", "dtv"], ["Xdt"])
                    tt("pool", Xd[:].rearrange("p (h q) -> p h q", q=64), Xdt[:].rearrange("p (h q) -> p h q", q=64),
                       ex3[:, 32:48].unsqueeze(2).to_broadcast([128, 16, 64]), ALU.mult, ["Xdt", kx], [kd_])
                    tt("pool", XD[:].rearrange("p (h q) -> p h q", q=64), X3,
                       D_b[:].unsqueeze(2).to_broadcast([128, 16, 64]), ALU.mult, ["Xtok", "D_b"], ["XD"])
                    bG = newbank()
                    for g in range(2):
                        mm(psb[bG][:, g * 128:(g + 1) * 128], xc[:, 8 + g, cols], xc[:, 10 + g, cols], True, True,
                           ["xc"], [pk(bG)])
                    cp("act", GT_sb[:], psb[bG][:, 0:256], [pk(bG)], ["GT_sb"])
                    for hg in range(4):
                        b = newbank()
                        g = hg // 2
                        mm(psb[b][:, :], L48[:, :], R48f[:, hg * 512:(hg + 1) * 512], True, False,
                           ["L48", "L48c", "R48", "R48c"], [pk(b)])
                        mm(psb[b][:, :], ident_bf[:], negm4[:].rearrange("p h s -> p (h s)"), False, True,
                           ["ident_bf", "negm4"], [pk(b)])
                        li = hg % 2
                        act(Lx[li][:], psb[b][:, :], AF.Exp, [pk(b)], ["Lx%d" % li])
                        tt("dve", M_sb[:, hg * 4:(hg + 1) * 4, :], Lx[li][:].rearrange("p (h s) -> p h s", s=128),
                           GT_sb[:, g * 128:(g + 1) * 128].unsqueeze(1).to_broadcast([128, 4, 128]), ALU.mult,
                           ["Lx%d" % li, "GT_sb"], ["M_sb"])
                    bY = [newbank(), newbank()]
                    hold(*bY)
                    bYs[c] = bY
                    for h in range(16):
                        bk_ = bY[h // 8]
                        col = (h % 8) * 64
                        mm(psb[bk_][:, col:col + 64], M_sb[:, h, :], Xdt[:, h * 64:(h + 1) * 64], h % 8 == 0, False,
                           ["M_sb", "Xdt"], [pk(bk_)])
                    for half in range(2):
                        mm(psb[bY[half]][:, :], ident_bf[:], XD[:, half * 512:(half + 1) * 512], False, True,
                           ["ident_bf", "XD"], [pk(bY[half])])

                def stage2(c):
                    p_ = c % 2
                    ex3, Btok, Xd = ex3_[p_], Btok_[p_], Xd_[p_]
                    kx, kb_, kd_ = "ex3_%d" % p_, "Btok%d" % p_, "Xd%d" % p_
                    cols = slice(c * 128, (c + 1) * 128)
                    bY = bYs[c]
                    bO = [newbank(), newbank()]
                    for g in range(2):
                        mm(psb[bO[g]][:, :], xc[:, 10 + g, cols], ST_bf[:, g * 512:(g + 1) * 512], True, True,
                           ["xc", "ST_bf"], [pk(bO[g])])
                    for half in range(2):
                        yh = ytmp[:, half * 512:(half + 1) * 512]
                        tt("dve", yh.rearrange("p (h q) -> p h q", q=64),
                           psb[bO[half]][:, :].rearrange("p (h q) -> p h q", q=64),
                           ex3[:, half * 8:(half + 1) * 8].unsqueeze(2).to_broadcast([128, 8, 64]), ALU.mult,
                           [pk(bO[half]), kx], ["ytmp"])
                        tt("dve", yh, yh, psb[bY[half]][:, :], ALU.add, ["ytmp", pk(bY[half])], ["ytmp"])
                    release(*bY)
                    tt("dve", ytmp[:], ytmp[:], zs[:, c, :], ALU.mult, ["ytmp", "zs"], ["ytmp"])
                    for g in range(2):
                        act(sqs[:], ytmp[:, g * 512:(g + 1) * 512], AF.Square, ["ytmp"], ["sqs", "ssq2"],
                            accum=ssq2[:, g:g + 1])
                    rstd_small(ssq2[:], 512, ["ssq2"], ["ssq2"])
                    for g in range(2):
                        stt(mix[:, c, 1024 + g * 512:1024 + (g + 1) * 512], ytmp[:, g * 512:(g + 1) * 512],
                            ssq2[:, g:g + 1], snwb[:, g * 512:(g + 1) * 512], ALU.mult, ALU.mult,
                            ["ytmp", "ssq2", "snwb"], ["mix%d" % c])
                    bS = [newbank(), newbank()]
                    for g in range(2):
                        mm(psb[bS[g]][:, :], Btok[:, g * 128:(g + 1) * 128], Xd[:, g * 512:(g + 1) * 512], True, True,
                           [kb_, kd_], [pk(bS[g])])
                    ST3 = ST[:].rearrange("p (h q) -> p h q", q=64)
                    tt("dve", ST3, ST3, ex3[:, 16:32].unsqueeze(2).to_broadcast([128, 16, 64]), ALU.mult,
                       ["ST", kx], ["ST"])
                    for g in range(2):
                        tt("dve", ST[:, g * 512:(g + 1) * 512], ST[:, g * 512:(g + 1) * 512], psb[bS[g]][:, :], ALU.add,
                           ["ST", pk(bS[g])], ["ST"])
                    cp("act", ST_bf[:], ST[:], ["ST"], ["ST_bf"])

                P1, P2 = (0, 1, 2, 3, 4, 5), (6, 7)
                for th in with_pool(P1, stage1, 0):
                    th()
                for c in range(8):
                    A_ = with_pool(P1, stage1, c + 1) if c + 1 < 8 else []
                    B_ = with_pool(P2, stage2, c)
                    for th in Sched.merge(A_, B_):
                        th()
                if sb == 1:
                    STo = sbt(s1, "STo", [128, 8, 128], F32)
                    for half in range(2):
                        b = newbank()
                        for i in range(4):
                            cc = half * 4 + i
                            tr(psb[b][:, i * 128:(i + 1) * 128], ST[:, cc * 128:(cc + 1) * 128], ident_f[:],
                               ["ST", "ident_f"], [pk(b)])
                        cp("act", STo[:, half * 4:(half + 1) * 4, :], psb[b][:, :].rearrange("p (c n) -> p c n", n=128),
                           [pk(b)], ["STo"])
                    store(sm_p.rearrange("(c q) n -> q c n", q=128), STo[:], ["STo"])
            S.barrier()

            S.mark("B2_%d" % sb)
            with ExitStack() as s2:
                def F32s(name, n):
                    return [sbt(s2, "%s%d" % (name, i), [128, 512], F32) for i in range(n)]

                def B16s(name, n, shape):
                    return [sbt(s2, "%s%d" % (name, i), shape, BF16) for i in range(n)]
                tq1 = F32s("tq", 1)[0]
                qs_ = F32s("qs", 3)
                ff_ = F32s("ff", 3)
                kk_ = F32s("kk", 3)
                lf1 = F32s("lf", 1)[0]
                bb1 = F32s("bb", 1)[0]
                enb1 = F32s("enb", 1)[0]
                eb_ = F32s("eb", 2)
                qb_ = B16s("qb", 2, [128, 512])
                kb_ = B16s("kb", 2, [128, 512])
                kd1 = B16s("kd", 1, [128, 512])[0]
                kdT_ = B16s("kdT", 2, [128, 512])
                qbz_ = B16s("qbz", 2, [128, 8, 128])
                v_ = B16s("v", 4, [128, 4, 128])
                tg1 = sbt(s2, "tg", [128, 4, 128], F32)
                gw_ = [sbt(s2, "gw%d" % i, [128, 4, 128], F32) for i in range(4)]
                ATm_all = B16s("ATm", 2, [128, 4, 128])
                Sall_f = [sbt(s2, "Sall_f%d" % i, [128, 9, 128], F32) for i in range(2)]
                Sall_b = B16s("Sall_b", 2, [128, 9, 128])
                Sbf0 = sbt(s2, "Sbf0", [128, 128], BF16)
                junk = sbt(s2, "junk", [128, 128], F32)
                ssq4 = [sbt(s2, "ssq4%d" % i, [128, 4], F32) for i in range(2)]
                for i in range(2):
                    mset("pool", qbz_[i][:], 0.0, ["qbz%d" % i])
                items = [(h, bi, c0b) for h in range(8) for bi, (c0b, nb) in enumerate(blocks) if nb == 512]
                NI = len(items)
                slots = {}
                SEG = lambda: S.rec.append(None)

                def stageA(k):
                    h, bi, c0b = items[k]
                    s2_, s3_ = "q%d" % (k % 3), "v%d" % (k % 4)
                    qs, ff, kk, v_sb, gw = qs_[k % 3], ff_[k % 3], kk_[k % 3], v_[k % 4], gw_[k % 4]
                    if bi == 0:
                        slots[h] = w_use(("hgrn", sb, h))
                    slot = slots[h]
                    wkq, wkf, wkvg = wq(slot, 0, 128), wq(slot, 128, 256), wq(slot, 256, 512)
                    bv = [newbank(), newbank()]
                    hold(*bv)
                    for t4 in range(4):
                        t = c0b // 128 + t4
                        bk_ = bv[t4 // 2]
                        col = (t4 % 2) * 256
                        for kc in range(8):
                            mm(psb[bk_][:, col:col + 256], hT[:, kc, t * 128:(t + 1) * 128], wst[slot][:, kc, 256:512],
                               kc == 0, kc == 7, wkvg + ["hT"], [pk(bk_)])
                    bq = newbank()
                    hold(bq)
                    for kc in range(8):
                        mm(psb[bq][:, :], wst[slot][:, kc, 0:128], hT[:, kc, c0b:c0b + 512], kc == 0, kc == 7,
                           wkq + ["hT"], [pk(bq)])
                    SEG()
                    bf_ = newbank()
                    for kc in range(8):
                        mm(psb[bf_][:, :], wst[slot][:, kc, 128:256], hT[:, kc, c0b:c0b + 512], kc == 0, kc == 7,
                           wkf + ["hT"], [pk(bf_)])
                    SEG()
                    for half in range(2):
                        pv = psb[bv[half]][:, :].rearrange("p (t c) -> p t c", c=256)
                        act(tg1[:, half * 2:(half + 1) * 2, :], pv[:, :, 128:256], AF.Tanh, [pk(bv[half])], ["tg"],
                            scale=0.5)
                    act(tq1[:], psb[bq][:, :], AF.Tanh, [pk(bq)], ["tq"], scale=0.5)
                    SEG()
                    act(ff[:], psb[bf_][:, :], AF.Tanh, [pk(bf_)], ["ff" + s2_], scale=0.5)
                    for half in range(2):
                        pv = psb[bv[half]][:, :].rearrange("p (t c) -> p t c", c=256)
                        cp("act", v_sb[:, half * 2:(half + 1) * 2, :], pv[:, :, 0:128], [pk(bv[half])], ["v" + s3_])
                    SEG()
                    for half in range(2):
                        pv = psb[bv[half]][:, :].rearrange("p (t c) -> p t c", c=256)
                        stt(gw[:, half * 2:(half + 1) * 2, :], tg1[:, half * 2:(half + 1) * 2, :], 1.0, pv[:, :, 128:256],
                            ALU.add, ALU.mult, ["tg", pk(bv[half])], ["gw" + s3_])
                    release(*bv)
                    stt(qs[:], tq1[:], 1.0, psb[bq][:, :], ALU.add, ALU.mult, ["tq", pk(bq)], ["qs" + s2_])
                    release(bq)
                    ts("dve", ff[:], ff[:], c1_col[:, h:h + 1], c0_col[:, h:h + 1], ALU.mult, ALU.add,
                       ["ff" + s2_, "c1_col", "c0_col"], ["ff" + s2_])
                    ts("dve", kk[:], ff[:], -1.0, 1.0, ALU.mult, ALU.add, ["ff" + s2_], ["kk" + s2_])
                    tt("pool", gw[:], gw[:], gwbh[:].unsqueeze(1).to_broadcast([128, 4, 128]), ALU.mult,
                       ["gw" + s3_, "gwbh"], ["gw" + s3_])
                    if bi == 0 and sb == 1:
                        b = newbank()
                        for kc in range(8):
                            mm(psb[b][:, 0:16], wst[slot][:, kc, 0:128], hT[:, kc, 1024:1040], kc == 0, kc == 7,
                               wkq + ["hT"], [pk(b)])
                        for kc in range(8):
                            mm(psb[b][:, 16:32], wst[slot][:, kc, 128:256], hT[:, kc, 1024:1040], kc == 0, kc == 7,
                               wkf + ["hT"], [pk(b)])
                        act(tmpS[:, 0:32], psb[b][:, 0:32], AF.Tanh, [pk(b)], ["tmpS"], scale=0.5)
                        stt(qS[:, h, :], tmpS[:, 0:16], 1.0, psb[b][:, 0:16], ALU.add, ALU.mult, ["tmpS", pk(b)], ["qS"])
                        ts("dve", fS[:, h, :], tmpS[:, 16:32], c1_col[:, h:h + 1], c0_col[:, h:h + 1], ALU.mult, ALU.add,
                           ["tmpS", "c1_col", "c0_col"], ["fS"])
                        ts("dve", kS[:, h, :], fS[:, h, :], -1.0, 1.0, ALU.mult, ALU.add, ["fS"], ["kS"])
                        b = newbank()
                        for kc in range(8):
                            mm(psb[b][0:16, 0:256], hT[:, kc, 1024:1040], wst[slot][:, kc, 256:512], kc == 0, kc == 7,
                               wkvg + ["hT"], [pk(b)])
                        cp("act", vS[:, h * 128:(h + 1) * 128], psb[b][0:16, 0:128], [pk(b)], ["vS"])
                        act(tgS[:], psb[b][0:16, 128:256], AF.Tanh, [pk(b)], ["tgS"], scale=0.5)
                        stt(gsS[:, h * 128:(h + 1) * 128], tgS[:], 1.0, psb[b][0:16, 128:256], ALU.add, ALU.mult,
                            ["tgS", pk(b)], ["gsS"])

                def stageB(k):
                    h, bi, c0b = items[k]
                    s2_ = str(k % 2)
                    sq_ = "q%d" % (k % 3)
                    qs, ff, kk = qs_[k % 3], ff_[k % 3], kk_[k % 3]
                    eb, qb, kb, kdT, qbz = eb_[k % 2], qb_[k % 2], kb_[k % 2], kdT_[k % 2], qbz_[k % 2]
                    act(lf1[:], ff[:], AF.Ln, ["ff" + sq_], ["lf"])
                    SEG()
                    S.add("dve", lambda e, o=bb1[:], d0=m01[:], d1=lf1[:]: e.tensor_tensor_scan(
                        out=o, data0=d0, data1=d1, initial=0.0, op0=ALU.mult, op1=ALU.add), ["m01", "lf"], ["bb"])
                    SEG()
                    act(eb[:], bb1[:], AF.Exp, ["bb"], ["eb" + s2_])
                    act(enb1[:], bb1[:], AF.Exp, ["bb"], ["enb"], scale=-1.0)
                    SEG()
                    stt(qb[:], qs[:], 0.5, eb[:], ALU.mult, ALU.mult, ["qs" + sq_, "eb" + s2_], ["qb" + s2_])
                    tt("dve", kb[:], kk[:], enb1[:], ALU.mult, ["kk" + sq_, "enb"], ["kb" + s2_])
                    tt("pool", kd1[:].rearrange("p (c t) -> p c t", t=64), kb[:].rearrange("p (c t) -> p c t", t=64),
                       eb[:].rearrange("p (c t) -> p c t", t=64)[:, :, 63:64].to_broadcast([128, 8, 64]), ALU.mult,
                       ["kb" + s2_, "eb" + s2_], ["kd"])
                    qbz_view = qbz[:].rearrange("p c x -> p (c x)").rearrange(
                        "p (pr j i) -> p pr j i", j=4, i=64)[:, :, 0:4:3, :]
                    cp("pool", qbz_view, qb[:].rearrange("p (pr two i) -> p pr two i", two=2, i=64),
                       ["qb" + s2_], ["qbz" + s2_])
                    SEG()
                    pK = psb[4][:].bitcast(BF16)
                    for t4 in range(4):
                        tr(pK[:, t4 * 128:(t4 + 1) * 128], kd1[:, t4 * 128:(t4 + 1) * 128], ident_bf[:],
                           ["kd", "ident_bf"], [pk(4)])
                    cp("act", kdT[:], pK[:, 0:512], [pk(4)], ["kdT" + s2_])

                def stageC(k):
                    h, bi, c0b = items[k]
                    x_ = k % 2
                    s2_, s3_ = str(k % 2), "v%d" % (k % 4)
                    eb, qb, kb, qbz, kdT, v_sb, gw, ssq = (eb_[x_], qb_[x_], kb_[x_], qbz_[x_], kdT_[x_], v_[k % 4],
                                                           gw_[k % 4], ssq4[x_])
                    Sf, Sb16, ATm = Sall_f[x_], Sall_b[x_], ATm_all[x_]
                    bA, bSa, bSb, bo = 4, 5, 6, 7
                    for t4 in range(4):
                        mm(psb[bA][:, t4 * 128:(t4 + 1) * 128], kb[:, t4 * 128:(t4 + 1) * 128],
                           qb[:, t4 * 128:(t4 + 1) * 128], True, True, ["kb" + s2_, "qb" + s2_], [pk(bA)])
                    for t4 in range(4):
                        mm(psb[bSa][:, t4 * 128:(t4 + 1) * 128], kdT[0:64, t4 * 128:(t4 + 1) * 128], v_sb[0:64, t4, :],
                           True, True, ["kdT" + s2_, "v" + s3_], [pk(bSa)])
                        mm(psb[bSb][:, t4 * 128:(t4 + 1) * 128], kdT[64:128, t4 * 128:(t4 + 1) * 128], v_sb[64:128, t4, :],
                           True, True, ["kdT" + s2_, "v" + s3_], [pk(bSb)])
                    SEG()
                    tt("dve", ATm[:], psb[bA][:, :].rearrange("p (t s) -> p t s", s=128),
                       maskBD[:].unsqueeze(1).to_broadcast([128, 4, 128]), ALU.mult, [pk(bA), "maskBD"], ["ATm" + s2_])
                    if bi == 0:
                        prev_f, prev_b, pk_f, pk_b = S_h[:, h, :], Sbf0[:], "S_h%d" % h, "Sbf0"
                        cp("pool", Sbf0[:], S_h[:, h, :], ["S_h%d" % h], ["Sbf0"])
                    else:
                        prev_f, prev_b = Sall_f[1 - x_][:, 8, :], Sall_b[1 - x_][:, 8, :]
                        pk_f, pk_b = "Sf%d" % (1 - x_), "Sb%d" % (1 - x_)
                    for c in range(8):
                        src_ = prev_f if c == 0 else Sf[:, c, :]
                        bank = bSa if c % 2 == 0 else bSb
                        stt(Sf[:, c + 1, :], src_, eb[:, c * 64 + 63:c * 64 + 64],
                            psb[bank][:, (c // 2) * 128:(c // 2 + 1) * 128], ALU.mult, ALU.add,
                            ["Sf" + s2_, pk_f, "eb" + s2_, pk(bank)], ["Sf" + s2_])
                    SEG()
                    cp("act", Sb16[:, 1:5, :], Sf[:, 1:5, :], ["Sf" + s2_], ["Sb" + s2_])
                    cp("act", Sb16[:, 5:9, :], Sf[:, 5:9, :], ["Sf" + s2_], ["Sb" + s2_])
                    SEG()
                    for t4 in range(4):
                        ca_, cb_ = 2 * t4, 2 * t4 + 1
                        oc = psb[bo][:, t4 * 128:(t4 + 1) * 128]
                        mm(oc, ATm[:, t4, :], v_sb[:, t4, :], True, False, ["ATm" + s2_, "v" + s3_], [pk(bo)])
                        before = prev_b if ca_ == 0 else Sb16[:, ca_, :]
                        mm(oc, qbz[:, ca_, :], before, False, False, ["qbz" + s2_, "Sb" + s2_, pk_b], [pk(bo)])
                        mm(oc, qbz[:, cb_, :], Sb16[:, cb_, :], False, True, ["qbz" + s2_, "Sb" + s2_], [pk(bo)])
                    SEG()
                    for t4 in range(4):
                        act(junk[:], psb[bo][:, t4 * 128:(t4 + 1) * 128], AF.Square, [pk(bo)], ["junk", "ssq" + s2_],
                            accum=ssq[:, t4:t4 + 1])
                    rstd_small(ssq[:], 128, ["ssq" + s2_], ["ssq" + s2_])
                    SEG()
                    for t4 in range(4):
                        t = c0b // 128 + t4
                        stt(mix[:, t, h * 128:(h + 1) * 128], psb[bo][:, t4 * 128:(t4 + 1) * 128], ssq[:, t4:t4 + 1],
                            gw[:, t4, :], ALU.mult, ALU.mult, [pk(bo), "ssq" + s2_, "gw" + s3_], ["mix%d" % t])
                    if bi == 1:
                        cp("dve", S_h[:, h, :], Sf[:, 8, :], ["Sf" + s2_], ["S_h%d" % h])
                        if sb == 1:
                            store(hg_p[h], S_h[:, h, :], ["S_h%d" % h])

                def segs(lst):
                    out, cur = [], []
                    for th in lst:
                        if th is None:
                            out.append(cur)
                            cur = []
                        else:
                            cur.append(th)
                    out.append(cur)
                    return out

                def emit_iter(kc, ka, kb2):
                    cs = segs(with_pool((0, 1, 2, 3), stageC, kc)) if kc is not None else [[]] * 6
                    as_ = segs(with_pool((0, 1, 2, 3), stageA, ka)) if ka is not None else [[]] * 5
                    bs = segs(with_pool((0, 1, 2, 3), stageB, kb2)) if kb2 is not None else [[]] * 5
                    c1, c2, c3, c4, c5, c6 = cs
                    a1a, a1b, a2a, a2b, a3 = as_
                    b1, b2, b3, b4, b5 = bs
                    for seg in (c1, b1, b2, a1a, c2, b3, c3, b4, c4, a1b, c5, b5, c6, a2a, a2b, a3):
                        for th in seg:
                            th()

                emit_iter(None, 0, None)
                emit_iter(None, 1, None)
                emit_iter(None, 2, 0)
                for i in range(NI):
                    emit_iter(i, i + 3 if i + 3 < NI else None, i + 1 if i + 1 < NI else None)
            S.barrier()
            if sb == 1:
                with ExitStack() as ssm:
                    A_ = with_pool((0, 1, 2, 3), sample_hgrn, ssm)
                    B_ = with_pool((4, 5, 6, 7), sample_ssd, ssm)
                    for th in Sched.merge(A_, B_):
                        th()
                S.barrier()

            S.mark("C%d" % sb)
            if DEBUG:
                store(dbg_mix[sb][:, 0:8, :], mix[:, 0:8, :], ["mix%d" % t for t in range(9)])
                S.barrier()
            with ExitStack() as s3:
                Wout = sbt(s3, "Wout", [128, 16, 1024], BF16)
                mixT = [sbt(s3, "mixT%d" % i, [128, 16, 128], BF16) for i in range(2)]
                xtc = [sbt(s3, "xtc%d" % i, [128, D], F32) for i in range(2)]
                hnc = [sbt(s3, "hnc%d" % i, [128, D], BF16) for i in range(2)]
                sqc = sbt(s3, "sqc", [128, D], F32)
                ssC = sbt(s3, "ssC", [128, 2], F32)
                w2b = sbt(s3, "w2b", [128, D], F32)
                dma("sp", w2b[:], norm2_w.partition_broadcast(128), (), ["w2b"])
                for q4 in range(4):
                    load_w(Wout[:, q4 * 4:(q4 + 1) * 4, :],
                           w_out[q4 * 512:(q4 + 1) * 512, :].rearrange("(k p) n -> p k n", p=128), (), ["Wout%d" % q4])
                def c_s1(t):
                    i = t % 2
                    rw = rows_of(t)
                    src = xs if t == 8 else xp[tok0 + t * 128: tok0 + (t + 1) * 128, :]
                    dma("sp", xtc[i][0:rw, :], src, (), ["xtc%d" % i])
                    for half in range(2):
                        b = newbank()
                        pT = psb[b][:].bitcast(BF16)
                        for j in range(8):
                            fc = half * 8 + j
                            tr(pT[:, j * 128:j * 128 + rw], mix[0:rw, t, fc * 128:(fc + 1) * 128], ident_bf[0:rw, 0:rw],
                               ["mix%d" % t, "ident_bf"], [pk(b)])
                        cp("act" if half == 0 else "dve", mixT[i][:, half * 8:(half + 1) * 8, 0:rw],
                           pT[:, 0:1024].rearrange("p (k c) -> p k c", k=8)[:, :, 0:rw], [pk(b)], ["mixT%d" % i])

                def c_s2(t):
                    i = t % 2
                    rw = rows_of(t)
                    bo2 = [newbank(), newbank()]
                    for nh in range(2):
                        for fc in range(16):
                            mm(psb[bo2[nh]][0:rw, :], mixT[i][:, fc, 0:rw], Wout[:, fc, nh * 512:(nh + 1) * 512],
                               fc == 0, fc == 15, ["mixT%d" % i, "Wout%d" % (fc // 4)], [pk(bo2[nh])])
                    for nh in range(2):
                        tt("dve", x2[0:rw, t, nh * 512:(nh + 1) * 512], xtc[i][0:rw, nh * 512:(nh + 1) * 512],
                           psb[bo2[nh]][0:rw, :], ALU.add, ["xtc%d" % i, pk(bo2[nh])], ["x2_%d" % t])
                    act(sqc[0:rw, :], x2[0:rw, t, :], AF.Square, ["x2_%d" % t], ["sqc", "ssC%d" % i],
                        accum=ssC[0:rw, i:i + 1])
                    rstd_small(ssC[0:rw, i:i + 1], D, ["ssC%d" % i], ["ssC%d" % i])
                    stt(hnc[i][0:rw, :], x2[0:rw, t, :], ssC[0:rw, i:i + 1], w2b[0:rw, :], ALU.mult, ALU.mult,
                        ["x2_%d" % t, "ssC%d" % i, "w2b"], ["hnc%d" % i])

                def c_s3(t):
                    i = t % 2
                    rw = rows_of(t)
                    b = newbank()
                    pT = psb[b][:].bitcast(BF16)
                    for k in range(8):
                        tr(pT[:, k * 128:k * 128 + rw], hnc[i][0:rw, k * 128:(k + 1) * 128], ident_bf[0:rw, 0:rw],
                           ["hnc%d" % i, "ident_bf"], [pk(b)])
                    cp("act", hT[:, :, t * 128:t * 128 + rw],
                       pT[:, 0:1024].rearrange("p (k c) -> p k c", k=8)[:, :, 0:rw], [pk(b)], ["hT"])

                c_s1(0)
                for t in range(ntile):
                    if t + 1 < ntile:
                        c_s1(t + 1)
                    c_s2(t)
                    if t >= 1:
                        c_s3(t - 1)
                c_s3(ntile - 1)
            S.barrier()

            S.mark("D%d" % sb)
            with ExitStack() as s4:
                actT = sbt(s4, "actT", [128, 11, 1040], BF16)
                Wd = sbt(s4, "Wd", [128, 11, 1024], BF16)
                graw = [sbt(s4, "graw%d" % i, [128, 2 + 1024], F32) for i in range(2)]
                gacc = [sbt(s4, "gacc%d" % i, [128, 1024], F32) for i in range(2)]
                if sb == 1:
                    stcf_t = [sbt(s4, "stcf_t%d" % i, [16, 2, 128], F32) for i in range(2)]
                    CFT = [sbt(s4, "CFT%d" % i, [128, 2, 16], F32) for i in range(2)]
                    grawS = [sbt(s4, "grawS%d" % i, [128, 16], F32) for i in range(2)]
                    gaccS = [sbt(s4, "gaccS%d" % i, [128, 16], F32) for i in range(2)]
                    gtokS = [sbt(s4, "gtokS%d" % i, [16, 128], F32) for i in range(2)]
                yt = [sbt(s4, "yt%d" % i, [128, D], F32) for i in range(2)]
                sqe = sbt(s4, "sqe", [128, D], F32)
                ssE = sbt(s4, "ssE", [128, 2], F32)
                wfb = sbt(s4, "wfb", [128, D], F32)
                dma("sp", wfb[:], final_norm_w.partition_broadcast(128), (), ["wfb"])

                def phase_e(t):
                    i = t % 2
                    rw = rows_of(t)
                    act(sqe[0:rw, :], x2[0:rw, t, :], AF.Square, ["x2_%d" % t], ["sqe", "ssE%d" % i],
                        accum=ssE[0:rw, i:i + 1])
                    rstd_small(ssE[0:rw, i:i + 1], D, ["ssE%d" % i], ["ssE%d" % i])
                    stt(yt[i][0:rw, :], x2[0:rw, t, :], ssE[0:rw, i:i + 1], wfb[0:rw, :], ALU.mult, ALU.mult,
                        ["x2_%d" % t, "ssE%d" % i, "wfb"], ["yt%d" % i])
                    dst = y_s if t == 8 else y_p[tok0 + t * 128: tok0 + (t + 1) * 128, :]
                    store(dst, yt[i][0:rw, :], ["yt%d" % i])

                for fg in range(2):
                    steps = [(ftl, bi, c0b) for ftl in range(11) for bi, (c0b, nb) in enumerate(blocks) if nb == 512]
                    fst = {}

                    def f_s1(n):
                        ftl, bi, c0b = steps[n]
                        ft = fg * 11 + ftl
                        ri = ft % 2
                        gr = graw[ri]
                        if bi == 0:
                            fst[("slot", ftl)] = w_use(("ffn", sb, fg, ftl // 2))
                            if ftl == 0:
                                for f2 in range(11):
                                    load_w(Wd[:, f2, :], w_down[(fg * 11 + f2) * 128:(fg * 11 + f2 + 1) * 128, :], (),
                                           ["Wd%d" % f2])
                            cp("act", gr[:, 0:2], halo_f[:, ft, :], ["halo_f%d" % ft], ["graw%dh" % ri])
                        slot = fst[("slot", ftl)]
                        gc0 = (ftl % 2) * 128
                        vc0 = 256 + (ftl % 2) * 128
                        todo = [(c0b, 512)] + ([(1024, 16)] if (bi == 1 and sb == 1) else [])
                        for (c0, nb) in todo:
                            bg = newbank()
                            for k in range(8):
                                mm(psb[bg][:, 0:nb], wst[slot][:, k, gc0:gc0 + 128], hT[:, k, c0:c0 + nb], k == 0, k == 7,
                                   wq(slot, gc0, gc0 + 128) + ["hT"], [pk(bg)])
                            bvv = newbank()
                            hold(bvv)
                            for k in range(8):
                                mm(psb[bvv][:, 0:nb], wst[slot][:, k, vc0:vc0 + 128], hT[:, k, c0:c0 + nb], k == 0, k == 7,
                                   wq(slot, vc0, vc0 + 128) + ["hT"], [pk(bvv)])
                            if nb == 16:
                                fst[("vS", ftl)] = bvv
                                cp("act", grawS[ri][:], psb[bg][:, 0:16], [pk(bg)], ["grawS%d" % ri])
                            else:
                                fst[("v", n)] = bvv
                                cp("act", gr[:, 2 + c0:2 + c0 + 512], psb[bg][:, 0:512], [pk(bg)], ["graw%db%d" % (ri, bi)])
                        if bi == 1:
                            cp("act", halo_f[:, ft, :], gr[:, 1024:1026], ["graw%db1" % ri], ["halo_f%d" % ft])

                    def f_s2(n):
                        ftl, bi, c0b = steps[n]
                        ft = fg * 11 + ftl
                        ri = ft % 2
                        gr = graw[ri]
                        gb = gacc[ri][:, c0b:c0b + 512]
                        ak = "gacc%db%d" % (ri, bi)
                        gk = ["graw%dh" % ri, "graw%db0" % ri] + (["graw%db1" % ri] if bi == 1 else [])
                        ts("dve", gb, gr[:, 2 + c0b:2 + c0b + 512], fw_col[:, ft, 2:3], fb_col[:, ft:ft + 1],
                           ALU.mult, ALU.add, gk + ["fw_col", "fb_col"], [ak])
                        for jj in (1, 0):
                            stt(gb, gr[:, jj + c0b:jj + c0b + 512], fw_col[:, ft, jj:jj + 1], gb, ALU.mult, ALU.add,
                                gk + ["fw_col", ak], [ak])

                    def f_s3(n):
                        ftl, bi, c0b = steps[n]
                        ri = (fg * 11 + ftl) % 2
                        gb = gacc[ri][:, c0b:c0b + 512]
                        ak = "gacc%db%d" % (ri, bi)
                        act(gb, gb, AF.Silu, [ak], [ak])

                    def f_s4(n):
                        ftl, bi, c0b = steps[n]
                        ft = fg * 11 + ftl
                        ri = ft % 2
                        gb = gacc[ri][:, c0b:c0b + 512]
                        ak = "gacc%db%d" % (ri, bi)
                        bvv = fst[("v", n)]
                        tt("dve", actT[:, ftl, c0b:c0b + 512], gb, psb[bvv][:, :], ALU.mult, [ak, pk(bvv)],
                           ["actT%d" % ftl])
                        release(bvv)
                        if bi == 1 and sb == 1:
                            ffn_sample_tile(ft, ftl, ri, stcf_t, CFT, grawS, gaccS, gtokS, actT, fst[("vS", ftl)])
                            release(fst[("vS", ftl)])

                    NS = len(steps)
                    f_s1(0)
                    for n in range(NS):
                        if n + 1 < NS:
                            f_s1(n + 1)
                        f_s2(n)
                        if n >= 1:
                            f_s4(n - 1)
                        f_s3(n)
                    f_s4(NS - 1)
                    if sb == 1:
                        for ftl2 in range(11):
                            ft2 = fg * 11 + ftl2
                            store(cf_p[:, ft2 * 128:(ft2 + 1) * 128].rearrange("t p -> p t"), halo_f[:, ft2, :],
                                  ["halo_f%d" % ft2], slow=True)
                    for t in range(ntile):
                        rw = rows_of(t)
                        b2 = [newbank(), newbank()]
                        for nh in range(2):
                            for ftl in range(11):
                                mm(psb[b2[nh]][0:rw, :], actT[:, ftl, t * 128:t * 128 + rw], Wd[:, ftl, nh * 512:(nh + 1) * 512],
                                   ftl == 0, ftl == 10, ["actT%d" % ftl, "Wd%d" % ftl], [pk(b2[nh])])
                        if fg == 1 and t >= 1:
                            phase_e(t - 1)
                        for nh in range(2):
                            tt("dve", x2[0:rw, t, nh * 512:(nh + 1) * 512], x2[0:rw, t, nh * 512:(nh + 1) * 512],
                               psb[b2[nh]][0:rw, :], ALU.add, ["x2_%d" % t, pk(b2[nh])], ["x2_%d" % t])
                    if fg == 1:
                        phase_e(ntile - 1)
            S.barrier()

        if CUT is not None:
            cut_at = S.marks[CUT]
            S.ops = S.ops[:cut_at]
            out_ops[:] = [o for o in out_ops if o.idx < cut_at]
        S.final_wait(out_ops)
        S.emit(nc, es)
    return nc


_NC_CACHE = {}


def kernel(**inputs):
    f32 = np.float32
    g = {k: np.ascontiguousarray(np.asarray(v, dtype=f32)) for k, v in inputs.items()}
    if "nc" not in _NC_CACHE:
        _NC_CACHE["nc"] = build_nc()
    nc = _NC_CACHE["nc"]
    shared = {
        "norm1_w": g["norm1_w"][0], "w_in": g["w_in"][0], "hgrn_lb": g["hgrn_lb"],
        "hgrn_norm_w": g["hgrn_norm_w"][0], "ssm_conv_w": g["ssm_conv_w"][0], "ssm_conv_b": g["ssm_conv_b"][0],
        "ssm_dt_bias": g["ssm_dt_bias"][0], "ssm_a_log": g["ssm_a_log"][0], "ssm_d": g["ssm_d"][0],
        "ssm_norm_w": g["ssm_norm_w"][0], "w_out": g["w_out"][0], "norm2_w": g["norm2_w"][0],
        "w_up": g["w_up"][0], "ffn_conv_w": g["ffn_conv_w"][0], "ffn_conv_b": g["ffn_conv_b"][0],
        "w_down": g["w_down"][0], "final_norm_w": g["final_norm_w"],
    }
    in_maps = []
    for c in range(NCORES):
        sl = slice(16 * c, 16 * c + 16)
        m = dict(shared)
        m["xp"] = g["x_prompt"][c]
        m["xs"] = g["x_sample"][sl, 0, :]
        m["st_h"] = g["state_hgrn"][0, sl]
        m["st_s"] = g["state_ssm"][0, sl].reshape(16, 1024, 128)
        m["st_cs"] = g["state_conv_ssm"][0, sl]
        m["st_cf"] = g["state_conv_ffn"][0, sl]
        in_maps.append({k: np.ascontiguousarray(v) for k, v in m.items()})
    res = run_bass_kernel_spmd(nc, in_maps, core_ids=list(range(NCORES)))
    R = res.results
    y_prompt = np.stack([R[c]["y_p"] for c in range(NCORES)], 0)
    y_sample = np.concatenate([R[c]["y_s"] for c in range(NCORES)], 0)[:, None, :]
    hgrn_p = np.stack([R[c]["hg_p"] for c in range(NCORES)], 0)[None]
    hgrn_s = np.concatenate([R[c]["hg_s"] for c in range(NCORES)], 0)[None]
    ssm_p = np.stack([R[c]["sm_p"].reshape(16, 64, 128) for c in range(NCORES)], 0)[None]
    ssm_s = np.concatenate([R[c]["sm_s"].reshape(16, 16, 64, 128) for c in range(NCORES)], 0)[None]
    cs_p_ = np.stack([R[c]["cs_p"] for c in range(NCORES)], 0)[None]
    cs_s_ = np.concatenate([R[c]["cs_s"] for c in range(NCORES)], 0)[None]
    cf_p_ = np.stack([R[c]["cf_p"] for c in range(NCORES)], 0)[None]
    cf_s_ = np.concatenate([R[c]["cf_s"] for c in range(NCORES)], 0)[None]
    outs = (y_prompt, y_sample, hgrn_p, hgrn_s, ssm_p, ssm_s, cs_p_, cs_s_, cf_p_, cf_s_)
    return tuple(np.ascontiguousarray(o, dtype=f32) for o in outs)
```

```python
from contextlib import ExitStack

import numpy as np
import concourse.bass as bass
import concourse.mybir as mybir
from concourse.bass_utils import run_bass_kernel_spmd

F32 = mybir.dt.float32
BF16 = mybir.dt.bfloat16
ALU = mybir.AluOpType
AF = mybir.ActivationFunctionType

NCORES = 8
D = 1024
DIN = 6672
DFF = 2816
C_Q, C_F, C_I, C_G, C_Z, C_X, C_DT = 0, 1024, 2048, 3072, 4096, 5120, 6656
EPS = 1e-6
NEG = -30000.0

COMPUTE = ("pe", "act", "dve", "pool")
STREAMS = ("pe", "act", "dve", "pool", "sp")


class Op:
    __slots__ = ("idx", "eng", "fn", "deps", "dma", "signal", "ms", "sem", "val", "prev_val")

    def __init__(self, idx, eng, fn, dma):
        self.idx = idx
        self.eng = eng
        self.fn = fn
        self.dma = dma
        self.deps = set()
        self.signal = False
        self.ms = 0
        self.sem = None
        self.val = 0
        self.prev_val = 0


class Sched:
    def __init__(self):
        self.ops = []
        self.last_w = {}
        self.readers = {}
        self.last_on = {}
        self.pending_dma = []
        self.marks = {}
        self.rec = None

    def add(self, eng, fn, reads=(), writes=(), dma=False):
        if self.rec is not None:
            self.rec.append(lambda: self._add(eng, fn, reads, writes, dma))
            return None
        return self._add(eng, fn, reads, writes, dma)

    def record(self, f, *a):
        assert self.rec is None
        self.rec = []
        f(*a)
        r, self.rec = self.rec, None
        return r

    @staticmethod
    def merge(A, B):
        out, i, j = [], 0, 0
        while i < len(A) or j < len(B):
            if j >= len(B) or (i < len(A) and (i + 0.5) * len(B) <= (j + 0.5) * len(A)):
                out.append(A[i])
                i += 1
            else:
                out.append(B[j])
                j += 1
        return out

    def _add(self, eng, fn, reads=(), writes=(), dma=False):
        op = Op(len(self.ops), eng, fn, dma)
        deps = set()
        for k in reads:
            w = self.last_w.get(k)
            if w is not None:
                deps.add(w)
        for k in writes:
            w = self.last_w.get(k)
            if w is not None:
                deps.add(w)
            for r in self.readers.get(k, ()):
                deps.add(r)
        for k in writes:
            self.last_w[k] = op
            self.readers[k] = []
        for k in reads:
            if self.last_w.get(k) is not op:
                self.readers.setdefault(k, []).append(op)
        deps.discard(op)
        op.deps = {d for d in deps if not (d.eng == "pe" and eng == "pe" and not d.dma and not dma)}
        self.ops.append(op)
        if dma:
            self.pending_dma.append(op)
        else:
            self.last_on[eng] = op
        return op

    def mark(self, name):
        self.marks[name] = len(self.ops)

    def barrier(self):
        lasts = list(self.last_on.values())
        dmas = list(self.pending_dma)
        for st in STREAMS:
            op = Op(len(self.ops), st, None, False)
            op.deps = set(lasts) | set(dmas)
            self.ops.append(op)
        self.pending_dma = []
        self.last_w = {}
        self.readers = {}

    def final_wait(self, ops):
        op = Op(len(self.ops), "sp", None, False)
        op.deps = set(ops)
        self.ops.append(op)

    def emit(self, nc, es, n_dma_sems=20):
        for op in self.ops:
            for d in op.deps:
                if not d.dma:
                    d.signal = True
        cnt = {e: 0 for e in COMPUTE}
        for op in self.ops:
            if op.dma or op.fn is None:
                continue
            if op.signal:
                cnt[op.eng] += 1
                op.ms = cnt[op.eng]
        assert max(cnt.values()) < 60000, cnt
        esem = {e: es.enter_context(nc.semaphore("s_" + e)) for e in COMPUTE}
        dsem = {st: [es.enter_context(nc.semaphore("d_%s_%d" % (st, i))) for i in range(n)]
                for st, n in (("sp", 12), ("pool", 8))}
        dcount = {st: 0 for st in dsem}
        dvals = {}
        for op in self.ops:
            if op.dma:
                pool = dsem[op.eng]
                i = dcount[op.eng] % len(pool)
                dcount[op.eng] += 1
                op.sem = pool[i]
                op.prev_val = dvals.get((op.eng, i), 0)
                op.val = op.prev_val + 16
                assert op.val < 60000
                dvals[(op.eng, i)] = op.val
        ops = self.ops

        def run_stream(st, eng):
            known = {}

            def wait(sem, val):
                key = id(sem)
                if known.get(key, 0) >= val:
                    return
                eng.wait_ge(sem, val)
                known[key] = val

            for op in ops:
                if op.eng != st:
                    continue
                for d in sorted(op.deps, key=lambda o: o.idx):
                    if d.dma:
                        wait(d.sem, d.val)
                    else:
                        wait(esem[d.eng], d.ms)
                if op.fn is None:
                    continue
                if op.dma:
                    if op.prev_val:
                        wait(op.sem, op.prev_val)
                    op.fn(eng).then_inc(op.sem, 16)
                else:
                    ins = op.fn(eng)
                    if op.signal:
                        ins.then_inc(esem[st], 1)

        with nc.Block() as block:
            @block.tensor
            def _(e):
                run_stream("pe", e)

            @block.scalar
            def _(e):
                run_stream("act", e)

            @block.vector
            def _(e):
                run_stream("dve", e)

            @block.gpsimd
            def _(e):
                run_stream("pool", e)

            @block.sync
            def _(e):
                run_stream("sp", e)


DEBUG = False
CUT = None


def build_nc():
    nc = bass.Bass("TRN2", target_bir_lowering=False)
    S = Sched()

    def din(name, shape):
        return nc.dram_tensor(name, shape, F32, kind="ExternalInput").ap()

    def dout(name, shape):
        return nc.dram_tensor(name, shape, F32, kind="ExternalOutput").ap()

    xp = din("xp", [2048, D])
    xs = din("xs", [16, D])
    st_h = din("st_h", [16, 8, 128, 128])
    st_s = din("st_s", [16, 1024, 128])
    st_cs = din("st_cs", [16, 3, 1536])
    st_cf = din("st_cf", [16, 2, DFF])
    norm1_w = din("norm1_w", [D])
    w_in = din("w_in", [D, DIN])
    hgrn_lb = din("hgrn_lb", [2, 1024])
    hgrn_norm_w = din("hgrn_norm_w", [128])
    ssm_conv_w = din("ssm_conv_w", [4, 1536])
    ssm_conv_b = din("ssm_conv_b", [1536])
    ssm_dt_bias = din("ssm_dt_bias", [16])
    ssm_a_log = din("ssm_a_log", [16])
    ssm_d = din("ssm_d", [16])
    ssm_norm_w = din("ssm_norm_w", [1024])
    w_out = din("w_out", [2048, D])
    norm2_w = din("norm2_w", [D])
    w_up = din("w_up", [D, 2 * DFF])
    ffn_conv_w = din("ffn_conv_w", [3, DFF])
    ffn_conv_b = din("ffn_conv_b", [DFF])
    w_down = din("w_down", [DFF, D])
    final_norm_w = din("final_norm_w", [D])

    y_p = dout("y_p", [2048, D])
    y_s = dout("y_s", [16, D])
    hg_p = dout("hg_p", [8, 128, 128])
    hg_s = dout("hg_s", [16, 8, 128, 128])
    sm_p = dout("sm_p", [1024, 128])
    sm_s = dout("sm_s", [16, 1024, 128])
    cs_p = dout("cs_p", [3, 1536])
    cs_s = dout("cs_s", [16, 3, 1536])
    cf_p = dout("cf_p", [2, DFF])
    cf_s = dout("cf_s", [16, 2, DFF])

    out_ops = []
    dbg_mix = nc.dram_tensor("dbg_mix", [2, 128, 9, 2048], BF16, kind="ExternalOutput").ap() if DEBUG else None

    def mm(out, lhsT, rhs, start, stop, r, w):
        S.add("pe", lambda e: e.matmul(out, lhsT=lhsT, rhs=rhs, start=start, stop=stop), r, w)

    def tr(out, in_, ident, r, w):
        S.add("pe", lambda e: e.transpose(out=out, in_=in_, identity=ident), r, w)

    def act(out, in_, func, r, w, bias=None, scale=None, accum=None):
        kw = {}
        if bias is not None:
            kw["bias"] = bias
        if scale is not None:
            kw["scale"] = scale
        if accum is not None:
            kw["accum_out"] = accum
        S.add("act", lambda e: e.activation(out=out, in_=in_, func=func, **kw), r, w)

    def ts(eng, out, in0, s1, s2, op0, op1, r, w):
        if op1 is None:
            S.add(eng, lambda e: e.tensor_scalar(out=out, in0=in0, scalar1=s1, scalar2=None, op0=op0), r, w)
        else:
            S.add(eng, lambda e: e.tensor_scalar(out=out, in0=in0, scalar1=s1, scalar2=s2, op0=op0, op1=op1), r, w)

    def stt(out, in0, scalar, in1, op0, op1, r, w):
        S.add("dve", lambda e: e.scalar_tensor_tensor(out=out, in0=in0, scalar=scalar, in1=in1, op0=op0, op1=op1), r, w)

    def tt(eng, out, in0, in1, op, r, w):
        S.add(eng, lambda e: e.tensor_tensor(out=out, in0=in0, in1=in1, op=op), r, w)

    def cp(eng, out, in_, r, w):
        if eng == "act":
            S.add("act", lambda e: e.copy(out=out, in_=in_), r, w)
        else:
            S.add(eng, lambda e: e.tensor_copy(out=out, in_=in_), r, w)

    def mset(eng, ap, val, w):
        S.add(eng, lambda e: e.memset(ap, val), (), w)

    def dma(q, out, in_, r, w, slow=False):
        if slow:
            return S.add(q, lambda e: e.dma_start(out=out, in_=in_, allow_slow_non_contiguous=True), r, w, dma=True)
        return S.add(q, lambda e: e.dma_start(out=out, in_=in_), r, w, dma=True)

    def store(out, in_, r, slow=False):
        if S.rec is not None:
            if slow:
                fn = lambda e: e.dma_start(out=out, in_=in_, allow_slow_non_contiguous=True)
            else:
                fn = lambda e: e.dma_start(out=out, in_=in_)
            S.rec.append(lambda: out_ops.append(S._add("sp", fn, r, (), True)))
            return
        out_ops.append(dma("sp", out, in_, r, (), slow=slow))

    with ExitStack() as es:
        uniq = {"n": 0}

        def sbt(stack, name, shape, dt):
            uniq["n"] += 1
            return stack.enter_context(nc.sbuf_tensor("%s_%d" % (name, uniq["n"]), shape, dt))

        psb = [es.enter_context(nc.psum_tensor("psb%d" % i, [128, 512], F32)) for i in range(8)]
        bank_state = {"next": 0, "reserved": set()}

        def newbank():
            pool = bank_state.get("pool")
            if pool is not None:
                key = "next_%s" % (pool,)
                while True:
                    b = pool[bank_state.get(key, 0) % len(pool)]
                    bank_state[key] = bank_state.get(key, 0) + 1
                    if b not in bank_state["reserved"]:
                        return b
            while True:
                b = bank_state["next"] % 8
                bank_state["next"] += 1
                if b not in bank_state["reserved"]:
                    return b

        def with_pool(pool, f, *a):
            old = bank_state.get("pool")
            bank_state["pool"] = pool
            try:
                return S.record(f, *a)
            finally:
                bank_state["pool"] = old

        def pk(b):
            return "ps%d" % b

        def hold(*bs):
            bank_state["reserved"].update(bs)

        def release(*bs):
            bank_state["reserved"].difference_update(bs)

        ones_f = sbt(es, "ones_f", [128, 128], F32)
        zeros_f = sbt(es, "zeros_f", [128, 128], F32)
        ident_f = sbt(es, "ident_f", [128, 128], F32)
        ident_bf = sbt(es, "ident_bf", [128, 128], BF16)
        U_f = sbt(es, "U_f", [128, 128], F32)
        maskBD = sbt(es, "maskBD", [128, 128], F32)
        negm4 = sbt(es, "negm4", [128, 4, 128], BF16)
        m01 = sbt(es, "m01", [128, 512], F32)
        R48 = sbt(es, "R48", [48, 16, 128], F32)
        L48 = sbt(es, "L48", [48, 128], F32)
        gwb = sbt(es, "gwb", [128, 128], F32)
        dtb_b = sbt(es, "dtb_b", [128, 16], F32)
        A_b = sbt(es, "A_b", [128, 16], F32)
        D_b = sbt(es, "D_b", [128, 16], F32)
        lbraw = sbt(es, "lbraw", [128, 2, 8], F32)
        lb_col = sbt(es, "lb_col", [128, 8], F32)
        c0_col = sbt(es, "c0_col", [128, 8], F32)
        c1_col = sbt(es, "c1_col", [128, 8], F32)
        gwbh = sbt(es, "gwbh", [128, 128], F32)
        half_f = sbt(es, "half_f", [128, 1], F32)
        tgS = sbt(es, "tgS", [16, 128], F32)
        oml_col = sbt(es, "oml_col", [128, 8], F32)
        cw_col = sbt(es, "cw_col", [128, 12, 4], F32)
        cb_col = sbt(es, "cb_col", [128, 12], F32)
        fw_col = sbt(es, "fw_col", [128, 22, 3], F32)
        fb_col = sbt(es, "fb_col", [128, 22], F32)
        S_h = sbt(es, "S_h", [128, 8, 128], F32)
        ST = sbt(es, "ST", [128, 1024], F32)
        ST_bf = sbt(es, "ST_bf", [128, 1024], BF16)
        halo_s = sbt(es, "halo_s", [128, 12, 3], F32)
        halo_f = sbt(es, "halo_f", [128, 22, 2], F32)
        hT = sbt(es, "hT", [128, 8, 1040], BF16)
        big = sbt(es, "big", [128, 9, 1024], F32)
        mix = big[:].bitcast(BF16)
        x2 = big
        wst = [sbt(es, "wst%d" % i, [128, 8, 512], BF16) for i in range(2)]
        wdt = sbt(es, "wdt", [128, 8, 16], BF16)
        wslot = {"n": 0}
        qS = sbt(es, "qS", [128, 8, 16], F32)
        fS = sbt(es, "fS", [128, 8, 16], F32)
        kS = sbt(es, "kS", [128, 8, 16], F32)
        vS = sbt(es, "vS", [16, 1024], F32)
        gsS = sbt(es, "gsS", [16, 1024], F32)
        tmpS = sbt(es, "tmpS", [128, 32], F32)
        xrawS = sbt(es, "xrawS", [128, 12, 16], F32)
        zS = sbt(es, "zS", [16, 1024], BF16)
        dtS_raw = sbt(es, "dtS_raw", [16, 16], F32)

        def next_wslot():
            i = wslot["n"] % 2
            wslot["n"] += 1
            return i

        def aff(out, in_, pattern, cmp, fill, base, cm, r, w):
            S.add("pool", lambda e: e.affine_select(out=out, in_=in_, pattern=pattern, compare_op=cmp, fill=fill,
                                                    base=base, channel_multiplier=cm), r, w)

        mset("pool", ones_f[:], 1.0, ["ones_f"])
        mset("pool", zeros_f[:], 0.0, ["zeros_f"])
        aff(ident_f[:], ones_f[:], [[-1, 128]], ALU.is_equal, 0.0, 0, 1, ["ones_f"], ["ident_f"])
        cp("pool", ident_bf[:], ident_f[:], ["ident_f"], ["ident_bf"])
        aff(U_f[:], ones_f[:], [[1, 128]], ALU.is_ge, 0.0, 0, -1, ["ones_f"], ["U_f"])
        cp("pool", maskBD[:], U_f[:], ["U_f"], ["maskBD"])
        mset("pool", maskBD[0:64, 64:128], 0.0, ["maskBD"])
        negf = sbt(es, "negf", [128, 128], F32)
        aff(negf[:], zeros_f[:], [[1, 128]], ALU.is_ge, NEG, 0, -1, ["zeros_f"], ["negf"])
        for i in range(4):
            cp("pool", negm4[:, i, :], negf[:], ["negf"], ["negm4"])
        mset("pool", m01[:], 1.0, ["m01"])
        mset("pool", m01[:].rearrange("p (c t) -> p c t", t=64)[:, :, 0:1], 0.0, ["m01"])
        mset("pool", R48[:], 0.0, ["R48c"])
        mset("pool", L48[:], 0.0, ["L48c"])
        mset("pool", L48[0:16, :], 1.0, ["L48c"])
        negones = sbt(es, "negones", [128, 1], F32)
        mset("pool", negones[:], -1.0, ["negones"])
        aff(R48[32:48, :, :], negones[32:48, 0:1].unsqueeze(1).to_broadcast([16, 16, 128]), [[-1, 16], [0, 128]],
            ALU.is_equal, 0.0, 0, 1, ["negones", "R48c"], ["R48c"])
        mset("pool", ST[:], 0.0, ["ST"])
        mset("pool", ST_bf[:], 0.0, ["ST_bf"])
        mset("pool", S_h[:], 0.0, ["S_h"])
        mset("pool", halo_s[:], 0.0, ["halo_s"])
        mset("pool", halo_f[:], 0.0, ["halo_f"])

        dma("sp", gwb[:], hgrn_norm_w.partition_broadcast(128), (), ["gwb"])
        dma("sp", dtb_b[:], ssm_dt_bias.partition_broadcast(128), (), ["dtb_b"])
        dma("sp", A_b[:], ssm_a_log.partition_broadcast(128), (), ["A_b"])
        dma("sp", D_b[:], ssm_d.partition_broadcast(128), (), ["D_b"])
        for r_ in range(2):
            dma("sp", lbraw[:, r_, :], hgrn_lb[r_].rearrange("(h p) -> p h", p=128), (), ["lbraw"], slow=True)
        for j_ in range(4):
            dma("sp", cw_col[:, :, j_], ssm_conv_w[j_].rearrange("(c p) -> p c", p=128), (), ["cw_col"], slow=True)
        dma("sp", cb_col[:], ssm_conv_b.rearrange("(c p) -> p c", p=128), (), ["cb_col"], slow=True)
        for j_ in range(3):
            dma("sp", fw_col[:, :, j_], ffn_conv_w[j_].rearrange("(c p) -> p c", p=128), (), ["fw_col"], slow=True)
        dma("sp", fb_col[:], ffn_conv_b.rearrange("(c p) -> p c", p=128), (), ["fb_col"], slow=True)
        dma("pool", wdt[:], w_in[:, C_DT:C_DT + 16].rearrange("(k p) n -> p k n", p=128), (), ["wdt"])
        act(A_b[:], A_b[:], AF.Exp, ["A_b"], ["A_b"])
        ts("dve", A_b[:], A_b[:], -1.0, None, ALU.mult, None, ["A_b"], ["A_b"])
        tt("dve", lb_col[:], lbraw[:, 0, :], lbraw[:, 1, :], ALU.subtract, ["lbraw"], ["lb_col"])
        act(oml_col[:], lb_col[:], AF.Sigmoid, ["lb_col"], ["oml_col"], scale=-1.0)
        act(lb_col[:], lb_col[:], AF.Sigmoid, ["lb_col"], ["lb_col"])
        ts("dve", c1_col[:], oml_col[:], 0.5, None, ALU.mult, None, ["oml_col"], ["c1_col"])
        tt("dve", c0_col[:], lb_col[:], c1_col[:], ALU.add, ["lb_col", "c1_col"], ["c0_col"])
        ts("dve", gwbh[:], gwb[:], 0.5, None, ALU.mult, None, ["gwb"], ["gwbh"])
        mset("pool", half_f[:], 0.5, ["half_f"])

        def rstd_small(ssq, n_feat, r, w):
            act(ssq, ssq, AF.Ln, r, w, bias=EPS, scale=1.0 / n_feat)
            act(ssq, ssq, AF.Exp, w, w, scale=-0.5)

        def load_w(dst, src, r, w):
            return dma("pool", dst, src, r, w)

        def wq(slot, a, b):
            return ["wst%dq%d" % (slot, j) for j in range(a // 128, (b - 1) // 128 + 1)]

        def wv(src, c0, n):
            return src[:, c0:c0 + n].rearrange("(k p) n -> p k n", p=128)

        job_list = []
        for sb_ in range(2):
            for wb in range(3):
                job_list.append((("ssdx", sb_, wb), [(0, 512, wv(w_in, C_X + wb * 512, 512))]))
            for wb in range(2):
                job_list.append((("ssdz", sb_, wb), [(0, 512, wv(w_in, C_Z + wb * 512, 512))]))
            for h_ in range(8):
                job_list.append((("hgrn", sb_, h_), [(i * 128, 128, wv(w_in, c0 + h_ * 128, 128))
                                                     for i, c0 in enumerate((C_Q, C_F, C_I, C_G))]))
            for fg_ in range(2):
                for pr in range(6):
                    ft0 = fg_ * 11 + pr * 2
                    n = 256 if pr < 5 else 128
                    job_list.append((("ffn", sb_, fg_, pr), [(0, n, wv(w_up, ft0 * 128, n)),
                                                             (256, n, wv(w_up, DFF + ft0 * 128, n))]))
        job_index = {k: i for i, (k, _) in enumerate(job_list)}
        wjobs = {}
        jstate = {"issued": 0}

        def w_use(key):
            idx = job_index[key]
            while jstate["issued"] <= min(idx + 1, len(job_list) - 1):
                k_, parts = job_list[jstate["issued"]]
                slot = next_wslot()
                for (c0, n, src) in parts:
                    load_w(wst[slot][:, :, c0:c0 + n], src, (), wq(slot, c0, c0 + n))
                wjobs[k_] = slot
                jstate["issued"] += 1
            return wjobs[key]

        SAMPLE = True

        def ttr(out, in0, in1, accum, r, w):
            S.add("dve", lambda e: e.scalar_tensor_tensor(out=out, in0=in0, scalar=1.0, in1=in1, op0=ALU.mult,
                                                          op1=ALU.mult, accum_out=accum), r, w)

        def build_selS(stack):
            selS = sbt(stack, "selS", [16, 16, 128], F32)
            aff(selS[:], ones_f[0:16, 0:1].unsqueeze(1).to_broadcast([16, 16, 128]), [[-1, 16], [0, 128]],
                ALU.is_equal, 0.0, 0, 1, ["ones_f"], ["selS"])
            return selS

        def sample_hgrn(ss):
            if True:
                selS = build_selS(ss)
                eyeS = sbt(ss, "eyeS", [128, 16, 16], F32)
                Qm = sbt(ss, "Qm", [128, 8, 256], F32)
                Sin = [sbt(ss, "SinH%d" % i, [128, 8, 128], F32) for i in range(3)]
                tmpH = [sbt(ss, "tmpH%d" % i, [128, 8, 128], F32) for i in range(2)]
                gwS = sbt(ss, "gwS", [16, 1024], F32)
                junkS = sbt(ss, "junkS", [16, 128], F32)
                ssqS = sbt(ss, "ssqS", [16, 8], F32)
                aff(eyeS[:], half_f[:, 0:1].unsqueeze(1).to_broadcast([128, 16, 16]), [[1, 16], [-1, 16]],
                    ALU.is_equal, 0.0, 0, 0, ["half_f"], ["eyeS"])
                for h in range(8):
                    tt("dve", Qm[:, h, :].rearrange("p (j c) -> p j c", c=16),
                       qS[:, h, :].unsqueeze(1).to_broadcast([128, 16, 16]), eyeS[:], ALU.mult, ["qS", "eyeS"], ["Qm"])
                bo_ = [newbank(), newbank()]
                bank_state["reserved"].update(bo_)
                def h_s1(j):
                    sl = j % 3
                    sk = "SinH%d" % sl
                    tj = j % 2
                    dma("sp", Sin[sl][:], st_h[j].rearrange("h k v -> k h v"), (), [sk])
                    bvb = [newbank(), newbank()]
                    for half in range(2):
                        mm(psb[bvb[half]][:, :], selS[:, j, :], vS[:, half * 512:(half + 1) * 512], True, True,
                           ["selS", "vS"], [pk(bvb[half])])
                    for h in range(8):
                        act(tmpH[tj][:, h, :], psb[bvb[h // 4]][:, (h % 4) * 128:(h % 4 + 1) * 128], AF.Identity,
                            [pk(bvb[h // 4]), "kS"], ["tmpH%d" % tj], scale=kS[:, h, j:j + 1])

                def h_s2(j):
                    sl = j % 3
                    sk = "SinH%d" % sl
                    tj = j % 2
                    tt("pool", Sin[sl][:], Sin[sl][:], fS[:, :, j:j + 1].to_broadcast([128, 8, 128]), ALU.mult,
                       [sk, "fS"], [sk])
                    tt("dve", Sin[sl][:], Sin[sl][:], tmpH[tj][:], ALU.add, [sk, "tmpH%d" % tj], [sk])
                    for h in range(8):
                        Sv = Sin[sl][:, h, :]
                        mm(psb[bo_[h // 4]][0:16, (h % 4) * 128:(h % 4 + 1) * 128], Qm[:, h, j * 16:(j + 1) * 16], Sv,
                           j == 0 and h % 4 == 0, j == 15 and h % 4 == 3, ["Qm", sk], [pk(bo_[h // 4])])
                    store(hg_s[j].rearrange("h k v -> k h v"), Sin[sl][:], [sk])

                h_s1(0)
                for j in range(16):
                    if j + 1 < 16:
                        h_s1(j + 1)
                    h_s2(j)
                for h in range(8):
                    oc = psb[bo_[h // 4]][0:16, (h % 4) * 128:(h % 4 + 1) * 128]
                    act(junkS[:], oc, AF.Square, [pk(bo_[h // 4])], ["junkS", "ssqS"], accum=ssqS[:, h:h + 1])
                rstd_small(ssqS[:], 128, ["ssqS"], ["ssqS"])
                tt("dve", gwS[:].rearrange("p (h v) -> p h v", v=128), gsS[:].rearrange("p (h v) -> p h v", v=128),
                   gwbh[0:16, :].unsqueeze(1).to_broadcast([16, 8, 128]), ALU.mult, ["gsS", "gwbh"], ["gwS"])
                for h in range(8):
                    oc = psb[bo_[h // 4]][0:16, (h % 4) * 128:(h % 4 + 1) * 128]
                    stt(mix[0:16, 8, h * 128:(h + 1) * 128], oc, ssqS[:, h:h + 1], gwS[:, h * 128:(h + 1) * 128],
                        ALU.mult, ALU.mult, [pk(bo_[h // 4]), "ssqS", "gwS"], ["mix8"])
                bank_state["reserved"].difference_update(bo_)

        def sample_ssd(ss):
            if True:
                selS = build_selS(ss)
                E = sbt(ss, "E", [16, 8, 128], F32)
                stcs_t = [sbt(ss, "stcs_t%d" % i, [16, 3, 128], F32) for i in range(2)]
                CST = [sbt(ss, "CST%d" % i, [128, 3, 16], F32) for i in range(2)]
                accS = [sbt(ss, "accS%d" % i, [128, 16], F32) for i in range(2)]
                xcS = sbt(ss, "xcS", [128, 12, 16], F32)
                xtokS = [sbt(ss, "xtokS%d" % i, [16, 128], F32) for i in range(2)]
                dtS = sbt(ss, "dtS", [16, 32], F32)
                dtdecT = sbt(ss, "dtdecT", [16, 34], F32)
                dE = sbt(ss, "dE", [128, 8, 34], F32)
                XdtE = sbt(ss, "XdtE", [128, 8, 16], F32)
                BCtok = sbt(ss, "BCtok", [16, 512], F32)
                Sin = [sbt(ss, "SinS%d" % i, [128, 8, 128], F32) for i in range(3)]
                tmpP = [sbt(ss, "tmpP%d" % i, [128, 8, 128], F32) for i in range(2)]
                junkP = sbt(ss, "junkP", [128, 128], F32)
                yS = sbt(ss, "yS", [128, 8, 16], F32)
                dcol = sbt(ss, "dcol", [16, 1], F32)
                ytokS = sbt(ss, "ytokS", [16, 1024], F32)
                junkS = sbt(ss, "junkS2", [16, 512], F32)
                ssqS = sbt(ss, "ssqS2", [16, 2], F32)
                snwS = sbt(ss, "snwS", [16, 1024], F32)
                dma("sp", snwS[:], ssm_norm_w.partition_broadcast(16), (), ["snwS"])
                dma("sp", dcol[:], ssm_d.rearrange("(h o) -> h o", o=1), (), ["dcol"])
                for a_ in range(2):
                    aff(E[:, :, a_ * 64:(a_ + 1) * 64], ones_f[0:16, 0:1].unsqueeze(1).to_broadcast([16, 8, 64]),
                        [[-2, 8], [0, 64]], ALU.is_equal, 0.0, -a_, 1, ["ones_f"], ["E"])
                mset("pool", yS[:], 0.0, ["yS"])
                store(cs_s[:, 0:2, :], st_cs[:, 1:3, :], [])
                for ft in range(12):
                    ri = ft % 2
                    dma("sp", stcs_t[ri][:], st_cs[:, :, ft * 128:(ft + 1) * 128], (), ["stcs_t%d" % ri])
                    b = newbank()
                    for r_ in range(3):
                        tr(psb[b][:, r_ * 16:(r_ + 1) * 16], stcs_t[ri][:, r_, :], ident_f[0:16, 0:16],
                           ["stcs_t%d" % ri, "ident_f"], [pk(b)])
                    cp("act", CST[ri][:].rearrange("p r j -> p (r j)"), psb[b][:, 0:48], [pk(b)], ["CST%d" % ri])
                    ts("dve", accS[ri][:], xrawS[:, ft, :], cw_col[:, ft, 3:4], cb_col[:, ft:ft + 1], ALU.mult, ALU.add,
                       ["xrawS", "cw_col", "cb_col"], ["accS%d" % ri])
                    for jj in (2, 1, 0):
                        stt(accS[ri][:], CST[ri][:, jj, :], cw_col[:, ft, jj:jj + 1], accS[ri][:], ALU.mult, ALU.add,
                            ["CST%d" % ri, "cw_col", "accS%d" % ri], ["accS%d" % ri])
                    act(xcS[:, ft, :], accS[ri][:], AF.Silu, ["accS%d" % ri], ["xcS"])
                    b2 = newbank()
                    tr(psb[b2][0:16, 0:128], xrawS[:, ft, :], ident_f[:], ["xrawS", "ident_f"], [pk(b2)])
                    cp("act", xtokS[ri][:], psb[b2][0:16, 0:128], [pk(b2)], ["xtokS%d" % ri])
                    store(cs_s[:, 2, ft * 128:(ft + 1) * 128], xtokS[ri][:], ["xtokS%d" % ri])
                tt("dve", dtS[:, 0:16], dtS_raw[:], dtb_b[0:16, :], ALU.add, ["dtS_raw", "dtb_b"], ["dtS"])
                act(dtS[:, 0:16], dtS[:, 0:16], AF.Exp, ["dtS"], ["dtS"])
                act(dtS[:, 0:16], dtS[:, 0:16], AF.Ln, ["dtS"], ["dtS"], bias=1.0)
                tt("dve", dtS[:, 16:32], dtS[:, 0:16], A_b[0:16, :], ALU.mult, ["dtS", "A_b"], ["dtS"])
                act(dtS[:, 16:32], dtS[:, 16:32], AF.Exp, ["dtS"], ["dtS"])
                b = newbank()
                for i in range(2):
                    tr(psb[b][0:16, i * 16:(i + 1) * 16], dtS[:, i * 16:(i + 1) * 16], ident_f[0:16, 0:16],
                       ["dtS", "ident_f"], [pk(b)])
                cp("act", dtdecT[:, 0:32], psb[b][0:16, 0:32], [pk(b)], ["dtdecT"])
                cp("dve", dtdecT[:, 32:34], dcol[:, 0:1].to_broadcast([16, 2]), ["dcol", "dtdecT"], ["dtdecT"])
                b = newbank()
                for c in range(8):
                    mm(psb[b][:, c * 34:(c + 1) * 34], E[:, c, :], dtdecT[:], True, True, ["E", "dtdecT"], [pk(b)])
                cp("act", dE[:].rearrange("p c x -> p (c x)"), psb[b][:, 0:272], [pk(b)], ["dE"])
                tt("dve", XdtE[:], xcS[:, 0:8, :], dE[:, :, 0:16], ALU.mult, ["xcS", "dE"], ["XdtE"])
                b = newbank()
                for i in range(4):
                    tr(psb[b][0:16, i * 128:(i + 1) * 128], xcS[:, 8 + i, :], ident_f[:], ["xcS", "ident_f"], [pk(b)])
                cp("act", BCtok[:], psb[b][0:16, :], [pk(b)], ["BCtok"])
                sbank = {}

                def s_s1(j):
                    sl = j % 3
                    sk = "SinS%d" % sl
                    tj = j % 2
                    dma("sp", Sin[sl][:], st_s[j].rearrange("(c q) n -> q c n", q=128), (), [sk])
                    b = newbank()
                    hold(b)
                    sbank[j] = b
                    mm(psb[b][:, :], selS[:, j, :], BCtok[:], True, True, ["selS", "BCtok"], [pk(b)])
                    for c in range(8):
                        g = c // 4
                        act(tmpP[tj][:, c, :], psb[b][:, g * 128:(g + 1) * 128], AF.Identity, [pk(b), "XdtE"],
                            ["tmpP%d" % tj], scale=XdtE[:, c, j:j + 1])

                def s_s2(j):
                    sl = j % 3
                    sk = "SinS%d" % sl
                    tj = j % 2
                    b = sbank[j]
                    tt("pool", Sin[sl][:], Sin[sl][:], dE[:, :, 16 + j:17 + j].to_broadcast([128, 8, 128]), ALU.mult,
                       [sk, "dE"], [sk])
                    tt("dve", Sin[sl][:], Sin[sl][:], tmpP[tj][:], ALU.add, [sk, "tmpP%d" % tj], [sk])
                    tt("dve", tmpP[tj][:].rearrange("p (g i) n -> p g i n", g=2),
                       Sin[sl][:].rearrange("p (g i) n -> p g i n", g=2),
                       psb[b][:, 256:512].rearrange("p (g n) -> p g n", g=2).unsqueeze(2).to_broadcast([128, 2, 4, 128]),
                       ALU.mult, [sk, pk(b), "tmpP%d" % tj], ["tmpP%d" % tj])
                    release(b)
                    S.add("dve", lambda e, o=yS[:, :, j], i_=tmpP[tj][:]: e.tensor_reduce(
                        out=o, in_=i_, axis=mybir.AxisListType.X, op=ALU.add), ["tmpP%d" % tj, "yS"], ["yS"])
                    store(sm_s[j].rearrange("(c q) n -> q c n", q=128), Sin[sl][:], [sk])

                s_s1(0)
                for j in range(16):
                    if j + 1 < 16:
                        s_s1(j + 1)
                    s_s2(j)
                for c in range(8):
                    stt(yS[:, c, :], xcS[:, c, :], dE[:, c, 32:33], yS[:, c, :], ALU.mult, ALU.add,
                        ["xcS", "dE", "yS"], ["yS"])
                for half in range(2):
                    b = newbank()
                    for i in range(4):
                        tr(psb[b][0:16, i * 128:(i + 1) * 128], yS[:, half * 4 + i, :], ident_f[:], ["yS", "ident_f"], [pk(b)])
                    tt("dve", ytokS[:, half * 512:(half + 1) * 512], psb[b][0:16, :], zS[:, half * 512:(half + 1) * 512],
                       ALU.mult, [pk(b), "zS"], ["ytokS"])
                for g in range(2):
                    act(junkS[:], ytokS[:, g * 512:(g + 1) * 512], AF.Square, ["ytokS"], ["junkS2", "ssqS2"],
                        accum=ssqS[:, g:g + 1])
                rstd_small(ssqS[:], 512, ["ssqS2"], ["ssqS2"])
                for g in range(2):
                    stt(mix[0:16, 8, 1024 + g * 512:1024 + (g + 1) * 512], ytokS[:, g * 512:(g + 1) * 512],
                        ssqS[:, g:g + 1], snwS[:, g * 512:(g + 1) * 512], ALU.mult, ALU.mult,
                        ["ytokS", "ssqS2", "snwS"], ["mix8"])

        def ffn_sample_tile(ft, ftl, ri, stcf_t, CFT, grawS, gaccS, gtokS, actT, bvvS):
            if not SAMPLE:
                return
            if ft == 0:
                store(cf_s[:, 0, :], st_cf[:, 1, :], [])
            dma("sp", stcf_t[ri][:], st_cf[:, :, ft * 128:(ft + 1) * 128], (), ["stcf_t%d" % ri])
            b = newbank()
            for r_ in range(2):
                tr(psb[b][:, r_ * 16:(r_ + 1) * 16], stcf_t[ri][:, r_, :], ident_f[0:16, 0:16],
                   ["stcf_t%d" % ri, "ident_f"], [pk(b)])
            cp("act", CFT[ri][:].rearrange("p r j -> p (r j)"), psb[b][:, 0:32], [pk(b)], ["CFT%d" % ri])
            ts("dve", gaccS[ri][:], grawS[ri][:], fw_col[:, ft, 2:3], fb_col[:, ft:ft + 1], ALU.mult, ALU.add,
               ["grawS%d" % ri, "fw_col", "fb_col"], ["gaccS%d" % ri])
            for jj in (1, 0):
                stt(gaccS[ri][:], CFT[ri][:, jj, :], fw_col[:, ft, jj:jj + 1], gaccS[ri][:], ALU.mult, ALU.add,
                    ["CFT%d" % ri, "fw_col", "gaccS%d" % ri], ["gaccS%d" % ri])
            act(gaccS[ri][:], gaccS[ri][:], AF.Silu, ["gaccS%d" % ri], ["gaccS%d" % ri])
            tt("dve", actT[:, ftl, 1024:1040], gaccS[ri][:], psb[bvvS][:, 0:16], ALU.mult,
               ["gaccS%d" % ri, pk(bvvS)], ["actT%d" % ftl])
            b2 = newbank()
            tr(psb[b2][0:16, 0:128], grawS[ri][:], ident_f[:], ["grawS%d" % ri, "ident_f"], [pk(b2)])
            cp("act", gtokS[ri][:], psb[b2][0:16, 0:128], [pk(b2)], ["gtokS%d" % ri])
            store(cf_s[:, 1, ft * 128:(ft + 1) * 128], gtokS[ri][:], ["gtokS%d" % ri])

        for sb in range(2):
            tok0 = sb * 1024
            ntile = 9 if sb == 1 else 8
            ncols = 1040 if sb == 1 else 1024
            blocks = [(0, 512), (512, 512)] + ([(1024, 16)] if sb == 1 else [])

            def rows_of(t):
                return 16 if t == 8 else 128

            S.mark("A%d" % sb)
            with ExitStack() as sa:
                xt = [sbt(sa, "xt%d" % i, [128, D], F32) for i in range(2)]
                sq = sbt(sa, "sq", [128, D], F32)
                hn = [sbt(sa, "hn%d" % i, [128, D], BF16) for i in range(2)]
                ssA = sbt(sa, "ssA", [128, 2], F32)
                w1b = sbt(sa, "w1b", [128, D], F32)
                dma("sp", w1b[:], norm1_w.partition_broadcast(128), (), ["w1b"])
                def a_s1(t):
                    i = t % 2
                    rw = rows_of(t)
                    src = xs if t == 8 else xp[tok0 + t * 128: tok0 + (t + 1) * 128, :]
                    dma("sp", xt[i][0:rw, :], src, (), ["xt%d" % i])
                    act(sq[0:rw, :], xt[i][0:rw, :], AF.Square, ["xt%d" % i], ["sq", "ssA%d" % i], accum=ssA[0:rw, i:i + 1])
                    rstd_small(ssA[0:rw, i:i + 1], D, ["ssA%d" % i], ["ssA%d" % i])

                def a_s2(t):
                    i = t % 2
                    rw = rows_of(t)
                    stt(hn[i][0:rw, :], xt[i][0:rw, :], ssA[0:rw, i:i + 1], w1b[0:rw, :], ALU.mult, ALU.mult,
                        ["xt%d" % i, "ssA%d" % i, "w1b"], ["hn%d" % i])
                    b = newbank()
                    pT = psb[b][:].bitcast(BF16)
                    for k in range(8):
                        tr(pT[:, k * 128:k * 128 + rw], hn[i][0:rw, k * 128:(k + 1) * 128], ident_bf[0:rw, 0:rw],
                           ["hn%d" % i, "ident_bf"], [pk(b)])
                    cp("act", hT[:, :, t * 128:t * 128 + rw],
                       pT[:, 0:1024].rearrange("p (k c) -> p k c", k=8)[:, :, 0:rw], [pk(b)], ["hT"])

                a_s1(0)
                if sb == 0:
                    w_use(("ssdx", 0, 0))
                for t in range(ntile):
                    if t + 1 < ntile:
                        a_s1(t + 1)
                    a_s2(t)
            S.barrier()

            def wview(c0, n):
                return w_in[:, c0:c0 + n].rearrange("(k p) n -> p k n", p=128)

            S.mark("B1_%d" % sb)
            with ExitStack() as s1:
                xc = sbt(s1, "xc", [128, 12, 1024], BF16)
                zs = sbt(s1, "zs", [128, 8, 1024], BF16)
                xraw = [sbt(s1, "xraw%d" % i, [128, 3 + 1024], F32) for i in range(2)]
                cacc = [sbt(s1, "cacc%d" % i, [128, 1024], F32) for i in range(2)]
                dtraw = sbt(s1, "dtraw", [128, 8, 16], F32)
                dtv = sbt(s1, "dtv", [128, 8, 16], F32)
                a_ext = sbt(s1, "a_ext", [128, 8, 48], F32)
                Xtok = sbt(s1, "Xtok", [128, 1024], BF16)
                Xdt = sbt(s1, "Xdt", [128, 1024], BF16)
                Xd_ = [sbt(s1, "Xd%d" % i, [128, 1024], BF16) for i in range(2)]
                XD = sbt(s1, "XD", [128, 1024], BF16)
                Btok_ = [sbt(s1, "Btok%d" % i, [128, 256], BF16) for i in range(2)]
                ex3_ = [sbt(s1, "ex3_%d" % i, [128, 48], F32) for i in range(2)]
                AcsT_sb = sbt(s1, "AcsT_sb", [16, 128], F32)
                GT_sb = sbt(s1, "GT_sb", [128, 256], F32)
                Lx = [sbt(s1, "Lx%d" % i, [128, 512], F32) for i in range(2)]
                M_sb = sbt(s1, "M_sb", [128, 16, 128], BF16)
                ytmp = sbt(s1, "ytmp", [128, 1024], F32)
                sqs = sbt(s1, "sqs", [128, 512], F32)
                ssq2 = sbt(s1, "ssq2", [128, 2], F32)
                snwb = sbt(s1, "snwb", [128, 1024], F32)
                dma("sp", snwb[:], ssm_norm_w.partition_broadcast(128), (), ["snwb"])

                xsteps = [(ft, bi, c0b) for ft in range(12) for bi, (c0b, nb) in enumerate(blocks) if nb == 512]
                xst = {}

                def x_s1(n):
                    ft, bi, c0b = xsteps[n]
                    ri = ft % 2
                    xr = xraw[ri]
                    j = ft % 4
                    if bi == 0:
                        if j == 0:
                            xst["slot"] = w_use(("ssdx", sb, ft // 4))
                        cp("act", xr[:, 0:3], halo_s[:, ft, :], ["halo_s%d" % ft], ["xraw%dh" % ri])
                    slot = xst["slot"]
                    todo = [(c0b, 512)] + ([(1024, 16)] if (bi == 1 and sb == 1) else [])
                    for (c0, nb) in todo:
                        b = newbank()
                        for k in range(8):
                            mm(psb[b][:, 0:nb], wst[slot][:, k, j * 128:(j + 1) * 128], hT[:, k, c0:c0 + nb],
                               k == 0, k == 7, wq(slot, j * 128, (j + 1) * 128) + ["hT"], [pk(b)])
                        if nb == 16:
                            cp("act", xrawS[:, ft, :], psb[b][:, 0:16], [pk(b)], ["xrawS"])
                        else:
                            cp("act", xr[:, 3 + c0:3 + c0 + 512], psb[b][:, 0:512], [pk(b)], ["xraw%db%d" % (ri, bi)])
                    if bi == 1:
                        cp("act", halo_s[:, ft, :], xr[:, 1024:1027], ["xraw%db1" % ri], ["halo_s%d" % ft])

                def x_s2(n):
                    ft, bi, c0b = xsteps[n]
                    ri = ft % 2
                    xr = xraw[ri]
                    cb_ = cacc[ri][:, c0b:c0b + 512]
                    ck = "cacc%d_%d" % (ri, bi)
                    xk = ["xraw%dh" % ri, "xraw%db0" % ri] + (["xraw%db1" % ri] if bi == 1 else [])
                    ts("dve", cb_, xr[:, 3 + c0b:3 + c0b + 512], cw_col[:, ft, 3:4], cb_col[:, ft:ft + 1],
                       ALU.mult, ALU.add, xk + ["cw_col", "cb_col"], [ck])
                    for jj in (2, 1, 0):
                        stt(cb_, xr[:, jj + c0b:jj + c0b + 512], cw_col[:, ft, jj:jj + 1], cb_, ALU.mult, ALU.add,
                            xk + ["cw_col", ck], [ck])

                def x_s3(n):
                    ft, bi, c0b = xsteps[n]
                    ri = ft % 2
                    act(xc[:, ft, c0b:c0b + 512], cacc[ri][:, c0b:c0b + 512], AF.Silu, ["cacc%d_%d" % (ri, bi)], ["xc"])

                x_s1(0)
                for n in range(len(xsteps)):
                    if n + 1 < len(xsteps):
                        x_s1(n + 1)
                    x_s2(n)
                    x_s3(n)
                if sb == 1:
                    for ft in range(12):
                        store(cs_p[:, ft * 128:(ft + 1) * 128].rearrange("t p -> p t"), halo_s[:, ft, :], ["halo_s%d" % ft],
                              slow=True)
                for wb in range(2):
                    slot = w_use(("ssdz", sb, wb))
                    for t in range(ntile):
                        rw = rows_of(t)
                        b = newbank()
                        for k in range(8):
                            mm(psb[b][0:rw, :], hT[:, k, t * 128:t * 128 + rw], wst[slot][:, k, :], k == 0, k == 7,
                               wq(slot, 0, 512) + ["hT"], [pk(b)])
                        if t == 8:
                            act(zS[:, wb * 512:(wb + 1) * 512], psb[b][0:16, :], AF.Silu, [pk(b)], ["zS"])
                        else:
                            act(zs[:, t, wb * 512:(wb + 1) * 512], psb[b][:, :], AF.Silu, [pk(b)], ["zs"])
                b = newbank()
                for t in range(ntile):
                    rw = rows_of(t)
                    for k in range(8):
                        mm(psb[b][0:rw, t * 16:(t + 1) * 16], hT[:, k, t * 128:t * 128 + rw], wdt[:, k, :], k == 0, k == 7,
                           ["wdt", "hT"], [pk(b)])
                cp("dve", dtraw[:], psb[b][:, 0:128].rearrange("p (t h) -> p t h", h=16), [pk(b)], ["dtraw"])
                if sb == 1:
                    cp("dve", dtS_raw[:], psb[b][0:16, 128:144], [pk(b)], ["dtS_raw"])
                tt("dve", dtv[:], dtraw[:], dtb_b[:].unsqueeze(1).to_broadcast([128, 8, 16]), ALU.add,
                   ["dtraw", "dtb_b"], ["dtv"])
                act(dtv[:], dtv[:], AF.Exp, ["dtv"], ["dtv"])
                act(dtv[:], dtv[:], AF.Ln, ["dtv"], ["dtv"], bias=1.0)
                mset("pool", a_ext[:], 0.0, ["a_ext"])
                for o_ in (0, 32):
                    tt("dve", a_ext[:, :, o_:o_ + 16], dtv[:], A_b[:].unsqueeze(1).to_broadcast([128, 8, 16]), ALU.mult,
                       ["dtv", "A_b", "a_ext"], ["a_ext"])

                R48f = R48[:].rearrange("p h s -> p (h s)")
                bYs = {}

                def stage1(c):
                    p_ = c % 2
                    ex3, Btok, Xd = ex3_[p_], Btok_[p_], Xd_[p_]
                    kx, kb_, kd_ = "ex3_%d" % p_, "Btok%d" % p_, "Xd%d" % p_
                    cols = slice(c * 128, (c + 1) * 128)
                    bX = newbank()
                    pX = psb[bX][:].bitcast(BF16)
                    for ft in range(8):
                        tr(pX[:, ft * 128:(ft + 1) * 128], xc[:, ft, cols], ident_bf[:], ["xc", "ident_bf"], [pk(bX)])
                    cp("act", Xtok[:], pX[:, 0:1024], [pk(bX)], ["Xtok"])
                    bB = newbank()
                    pB = psb[bB][:].bitcast(BF16)
                    for g in range(2):
                        tr(pB[:, g * 128:(g + 1) * 128], xc[:, 8 + g, cols], ident_bf[:], ["xc", "ident_bf"], [pk(bB)])
                    cp("act", Btok[:], pB[:, 0:256], [pk(bB)], [kb_])
                    bC = newbank()
                    mm(psb[bC][:, 0:16], U_f[:], a_ext[:, c, 0:16], True, True, ["U_f", "a_ext"], [pk(bC)])
                    mm(psb[bC][:, 16:32], ones_f[:], a_ext[:, c, 0:16], True, True, ["ones_f", "a_ext"], [pk(bC)])
                    mm(psb[bC][0:48, 128:256], a_ext[:, c, :], U_f[:], True, True, ["U_f", "a_ext"], [pk(bC)])
                    cp("dve", ex3[:, 0:32], psb[bC][:, 0:32], [pk(bC)], [kx])
                    tt("dve", ex3[:, 32:48], ex3[:, 16:32], ex3[:, 0:16], ALU.subtract, [kx], [kx])
                    act(ex3[:], ex3[:], AF.Exp, [kx], [kx])
                    cp("act", AcsT_sb[:], psb[bC][0:16, 128:256], [pk(bC)], ["AcsT_sb"])
                    cp("dve", L48[32:48, :], psb[bC][32:48, 128:256], [pk(bC)], ["L48"])
                    aff(R48[0:16, :, :], AcsT_sb[:].unsqueeze(1).to_broadcast([16, 16, 128]), [[-1, 16], [0, 128]],
                        ALU.is_equal, 0.0, 0, 1, ["AcsT_sb"], ["R48"])
                    X3 = Xtok[:].rearrange("p (h q) -> p h q", q=64)
                    tt("dve", Xdt[:].rearrange("p (h q) -> p h q", q=64), X3,
                       dtv[:, c, :].unsqueeze(2).to_broadcast([128, 16, 64]), ALU.mult, ["Xtok", "dtv"], ["Xdt"])
                    tt("pool", Xd[:].rearrange("p (h q) -> p h q", q=64), Xdt[:].rearrange("p (h q) -> p h q", q=64),
                       ex3[:, 32:48].unsqueeze(2).to_broadcast([128, 16, 64]), ALU.mult, ["Xdt", kx], [kd_])
                    tt("pool", XD[:].rearrange("p (h q) -> p h q", q=64), X3,
                       D_b[:].unsqueeze(2).to_broadcast([128, 16, 64]), ALU.mult, ["Xtok", "D_b"], ["XD"])
                    bG = newbank()
                    for g in range(2):
                        mm(psb[bG][:, g * 128:(g + 1) * 128], xc[:, 8 + g, cols], xc[:, 10 + g, cols], True, True,
                           ["xc"], [pk(bG)])
                    cp("act", GT_sb[:], psb[bG][:, 0:256], [pk(bG)], ["GT_sb"])
                    for hg in range(4):
                        b = newbank()
                        g = hg // 2
                        mm(psb[b][:, :], L48[:, :], R48f[:, hg * 512:(hg + 1) * 512], True, False,
                           ["L48", "L48c", "R48", "R48c"], [pk(b)])
                        mm(psb[b][:, :], ident_bf[:], negm4[:].rearrange("p h s -> p (h s)"), False, True,
                           ["ident_bf", "negm4"], [pk(b)])
                        li = hg % 2
                        act(Lx[li][:], psb[b][:, :], AF.Exp, [pk(b)], ["Lx%d" % li])
                        tt("dve", M_sb[:, hg * 4:(hg + 1) * 4, :], Lx[li][:].rearrange("p (h s) -> p h s", s=128),
                           GT_sb[:, g * 128:(g + 1) * 128].unsqueeze(1).to_broadcast([128, 4, 128]), ALU.mult,
                           ["Lx%d" % li, "GT_sb"], ["M_sb"])
                    bY = [newbank(), newbank()]
                    hold(*bY)
                    bYs[c] = bY
                    for h in range(16):
                        bk_ = bY[h // 8]
                        col = (h % 8) * 64
                        mm(psb[bk_][:, col:col + 64], M_sb[:, h, :], Xdt[:, h * 64:(h + 1) * 64], h % 8 == 0, False,
                           ["M_sb", "Xdt"], [pk(bk_)])
                    for half in range(2):
                        mm(psb[bY[half]][:, :], ident_bf[:], XD[:, half * 512:(half + 1) * 512], False, True,
                           ["ident_bf", "XD"], [pk(bY[half])])

                def stage2(c):
                    p_ = c % 2
                    ex3, Btok, Xd = ex3_[p_], Btok_[p_], Xd_[p_]
                    kx, kb_, kd_ = "ex3_%d" % p_, "Btok%d" % p_, "Xd%d" % p_
                    cols = slice(c * 128, (c + 1) * 128)
                    bY = bYs[c]
                    bO = [newbank(), newbank()]
                    for g in range(2):
                        mm(psb[bO[g]][:, :], xc[:, 10 + g, cols], ST_bf[:, g * 512:(g + 1) * 512], True, True,
                           ["xc", "ST_bf"], [pk(bO[g])])
                    for half in range(2):
                        yh = ytmp[:, half * 512:(half + 1) * 512]
                        tt("dve", yh.rearrange("p (h q) -> p h q", q=64),
                           psb[bO[half]][:, :].rearrange("p (h q) -> p h q", q=64),
                           ex3[:, half * 8:(half + 1) * 8].unsqueeze(2).to_broadcast([128, 8, 64]), ALU.mult,
                           [pk(bO[half]), kx], ["ytmp"])
                        tt("dve", yh, yh, psb[bY[half]][:, :], ALU.add, ["ytmp", pk(bY[half])], ["ytmp"])
                    release(*bY)
                    tt("dve", ytmp[:], ytmp[:], zs[:, c, :], ALU.mult, ["ytmp", "zs"], ["ytmp"])
                    for g in range(2):
                        act(sqs[:], ytmp[:, g * 512:(g + 1) * 512], AF.Square, ["ytmp"], ["sqs", "ssq2"],
                            accum=ssq2[:, g:g + 1])
                    rstd_small(ssq2[:], 512, ["ssq2"], ["ssq2"])
                    for g in range(2):
                        stt(mix[:, c, 1024 + g * 512:1024 + (g + 1) * 512], ytmp[:, g * 512:(g + 1) * 512],
                            ssq2[:, g:g + 1], snwb[:, g * 512:(g + 1) * 512], ALU.mult, ALU.mult,
                            ["ytmp", "ssq2", "snwb"], ["mix%d" % c])
                    bS = [newbank(), newbank()]
                    for g in range(2):
                        mm(psb[bS[g]][:, :], Btok[:, g * 128:(g + 1) * 128], Xd[:, g * 512:(g + 1) * 512], True, True,
                           [kb_, kd_], [pk(bS[g])])
                    ST3 = ST[:].rearrange("p (h q) -> p h q", q=64)
                    tt("dve", ST3, ST3, ex3[:, 16:32].unsqueeze(2).to_broadcast([128, 16, 64]), ALU.mult,
                       ["ST", kx], ["ST"])
                    for g in range(2):
                        tt("dve", ST[:, g * 512:(g + 1) * 512], ST[:, g * 512:(g + 1) * 512], psb[bS[g]][:, :], ALU.add,
                           ["ST", pk(bS[g])], ["ST"])
                    cp("act", ST_bf[:], ST[:], ["ST"], ["ST_bf"])

                P1, P2 = (0, 1, 2, 3, 4, 5), (6, 7)
                for th in with_pool(P1, stage1, 0):
                    th()
                for c in range(8):
                    A_ = with_pool(P1, stage1, c + 1) if c + 1 < 8 else []
                    B_ = with_pool(P2, stage2, c)
                    for th in Sched.merge(A_, B_):
                        th()
                if sb == 1:
                    STo = sbt(s1, "STo", [128, 8, 128], F32)
                    for half in range(2):
                        b = newbank()
                        for i in range(4):
                            cc = half * 4 + i
                            tr(psb[b][:, i * 128:(i + 1) * 128], ST[:, cc * 128:(cc + 1) * 128], ident_f[:],
                               ["ST", "ident_f"], [pk(b)])
                        cp("act", STo[:, half * 4:(half + 1) * 4, :], psb[b][:, :].rearrange("p (c n) -> p c n", n=128),
                           [pk(b)], ["STo"])
                    store(sm_p.rearrange("(c q) n -> q c n", q=128), STo[:], ["STo"])
            S.barrier()

            S.mark("B2_%d" % sb)
            with ExitStack() as s2:
                def F32s(name, n):
                    return [sbt(s2, "%s%d" % (name, i), [128, 512], F32) for i in range(n)]

                def B16s(name, n, shape):
                    return [sbt(s2, "%s%d" % (name, i), shape, BF16) for i in range(n)]
                tq1 = F32s("tq", 1)[0]
                qs_ = F32s("qs", 3)
                ff_ = F32s("ff", 3)
                kk_ = F32s("kk", 3)
                lf1 = F32s("lf", 1)[0]
                bb1 = F32s("bb", 1)[0]
                enb1 = F32s("enb", 1)[0]
                eb_ = F32s("eb", 2)
                qb_ = B16s("qb", 2, [128, 512])
                kb_ = B16s("kb", 2, [128, 512])
                kd1 = B16s("kd", 1, [128, 512])[0]
                kdT_ = B16s("kdT", 2, [128, 512])
                qbz_ = B16s("qbz", 2, [128, 8, 128])
                v_ = B16s("v", 4, [128, 4, 128])
                tg1 = sbt(s2, "tg", [128, 4, 128], F32)
                gw_ = [sbt(s2, "gw%d" % i, [128, 4, 128], F32) for i in range(4)]
                ATm_all = B16s("ATm", 2, [128, 4, 128])
                Sall_f = [sbt(s2, "Sall_f%d" % i, [128, 9, 128], F32) for i in range(2)]
                Sall_b = B16s("Sall_b", 2, [128, 9, 128])
                Sbf0 = sbt(s2, "Sbf0", [128, 128], BF16)
                junk = sbt(s2, "junk", [128, 128], F32)
                ssq4 = [sbt(s2, "ssq4%d" % i, [128, 4], F32) for i in range(2)]
                for i in range(2):
                    mset("pool", qbz_[i][:], 0.0, ["qbz%d" % i])
                items = [(h, bi, c0b) for h in range(8) for bi, (c0b, nb) in enumerate(blocks) if nb == 512]
                NI = len(items)
                slots = {}
                SEG = lambda: S.rec.append(None)

                def stageA(k):
                    h, bi, c0b = items[k]
                    s2_, s3_ = "q%d" % (k % 3), "v%d" % (k % 4)
                    qs, ff, kk, v_sb, gw = qs_[k % 3], ff_[k % 3], kk_[k % 3], v_[k % 4], gw_[k % 4]
                    if bi == 0:
                        slots[h] = w_use(("hgrn", sb, h))
                    slot = slots[h]
                    wkq, wkf, wkvg = wq(slot, 0, 128), wq(slot, 128, 256), wq(slot, 256, 512)
                    bv = [newbank(), newbank()]
                    hold(*bv)
                    for t4 in range(4):
                        t = c0b // 128 + t4
                        bk_ = bv[t4 // 2]
                        col = (t4 % 2) * 256
                        for kc in range(8):
                            mm(psb[bk_][:, col:col + 256], hT[:, kc, t * 128:(t + 1) * 128], wst[slot][:, kc, 256:512],
                               kc == 0, kc == 7, wkvg + ["hT"], [pk(bk_)])
                    bq = newbank()
                    hold(bq)
                    for kc in range(8):
                        mm(psb[bq][:, :], wst[slot][:, kc, 0:128], hT[:, kc, c0b:c0b + 512], kc == 0, kc == 7,
                           wkq + ["hT"], [pk(bq)])
                    SEG()
                    bf_ = newbank()
                    for kc in range(8):
                        mm(psb[bf_][:, :], wst[slot][:, kc, 128:256], hT[:, kc, c0b:c0b + 512], kc == 0, kc == 7,
                           wkf + ["hT"], [pk(bf_)])
                    SEG()
                    for half in range(2):
                        pv = psb[bv[half]][:, :].rearrange("p (t c) -> p t c", c=256)
                        act(tg1[:, half * 2:(half + 1) * 2, :], pv[:, :, 128:256], AF.Tanh, [pk(bv[half])], ["tg"],
                            scale=0.5)
                    act(tq1[:], psb[bq][:, :], AF.Tanh, [pk(bq)], ["tq"], scale=0.5)
                    SEG()
                    act(ff[:], psb[bf_][:, :], AF.Tanh, [pk(bf_)], ["ff" + s2_], scale=0.5)
                    for half in range(2):
                        pv = psb[bv[half]][:, :].rearrange("p (t c) -> p t c", c=256)
                        cp("act", v_sb[:, half * 2:(half + 1) * 2, :], pv[:, :, 0:128], [pk(bv[half])], ["v" + s3_])
                    SEG()
                    for half in range(2):
                        pv = psb[bv[half]][:, :].rearrange("p (t c) -> p t c", c=256)
                        stt(gw[:, half * 2:(half + 1) * 2, :], tg1[:, half * 2:(half + 1) * 2, :], 1.0, pv[:, :, 128:256],
                            ALU.add, ALU.mult, ["tg", pk(bv[half])], ["gw" + s3_])
                    release(*bv)
                    stt(qs[:], tq1[:], 1.0, psb[bq][:, :], ALU.add, ALU.mult, ["tq", pk(bq)], ["qs" + s2_])
                    release(bq)
                    ts("dve", ff[:], ff[:], c1_col[:, h:h + 1], c0_col[:, h:h + 1], ALU.mult, ALU.add,
                       ["ff" + s2_, "c1_col", "c0_col"], ["ff" + s2_])
                    ts("dve", kk[:], ff[:], -1.0, 1.0, ALU.mult, ALU.add, ["ff" + s2_], ["kk" + s2_])
                    tt("pool", gw[:], gw[:], gwbh[:].unsqueeze(1).to_broadcast([128, 4, 128]), ALU.mult,
                       ["gw" + s3_, "gwbh"], ["gw" + s3_])
                    if bi == 0 and sb == 1:
                        b = newbank()
                        for kc in range(8):
                            mm(psb[b][:, 0:16], wst[slot][:, kc, 0:128], hT[:, kc, 1024:1040], kc == 0, kc == 7,
                               wkq + ["hT"], [pk(b)])
                        for kc in range(8):
                            mm(psb[b][:, 16:32], wst[slot][:, kc, 128:256], hT[:, kc, 1024:1040], kc == 0, kc == 7,
                               wkf + ["hT"], [pk(b)])
                        act(tmpS[:, 0:32], psb[b][:, 0:32], AF.Tanh, [pk(b)], ["tmpS"], scale=0.5)
                        stt(qS[:, h, :], tmpS[:, 0:16], 1.0, psb[b][:, 0:16], ALU.add, ALU.mult, ["tmpS", pk(b)], ["qS"])
                        ts("dve", fS[:, h, :], tmpS[:, 16:32], c1_col[:, h:h + 1], c0_col[:, h:h + 1], ALU.mult, ALU.add,
                           ["tmpS", "c1_col", "c0_col"], ["fS"])
                        ts("dve", kS[:, h, :], fS[:, h, :], -1.0, 1.0, ALU.mult, ALU.add, ["fS"], ["kS"])
                        b = newbank()
                        for kc in range(8):
                            mm(psb[b][0:16, 0:256], hT[:, kc, 1024:1040], wst[slot][:, kc, 256:512], kc == 0, kc == 7,
                               wkvg + ["hT"], [pk(b)])
                        cp("act", vS[:, h * 128:(h + 1) * 128], psb[b][0:16, 0:128], [pk(b)], ["vS"])
                        act(tgS[:], psb[b][0:16, 128:256], AF.Tanh, [pk(b)], ["tgS"], scale=0.5)
                        stt(gsS[:, h * 128:(h + 1) * 128], tgS[:], 1.0, psb[b][0:16, 128:256], ALU.add, ALU.mult,
                            ["tgS", pk(b)], ["gsS"])

                def stageB(k):
                    h, bi, c0b = items[k]
                    s2_ = str(k % 2)
                    sq_ = "q%d" % (k % 3)
                    qs, ff, kk = qs_[k % 3], ff_[k % 3], kk_[k % 3]
                    eb, qb, kb, kdT, qbz = eb_[k % 2], qb_[k % 2], kb_[k % 2], kdT_[k % 2], qbz_[k % 2]
                    act(lf1[:], ff[:], AF.Ln, ["ff" + sq_], ["lf"])
                    SEG()
                    S.add("dve", lambda e, o=bb1[:], d0=m01[:], d1=lf1[:]: e.tensor_tensor_scan(
                        out=o, data0=d0, data1=d1, initial=0.0, op0=ALU.mult, op1=ALU.add), ["m01", "lf"], ["bb"])
                    SEG()
                    act(eb[:], bb1[:], AF.Exp, ["bb"], ["eb" + s2_])
                    act(enb1[:], bb1[:], AF.Exp, ["bb"], ["enb"], scale=-1.0)
                    SEG()
                    stt(qb[:], qs[:], 0.5, eb[:], ALU.mult, ALU.mult, ["qs" + sq_, "eb" + s2_], ["qb" + s2_])
                    tt("dve", kb[:], kk[:], enb1[:], ALU.mult, ["kk" + sq_, "enb"], ["kb" + s2_])
                    tt("pool", kd1[:].rearrange("p (c t) -> p c t", t=64), kb[:].rearrange("p (c t) -> p c t", t=64),
                       eb[:].rearrange("p (c t) -> p c t", t=64)[:, :, 63:64].to_broadcast([128, 8, 64]), ALU.mult,
                       ["kb" + s2_, "eb" + s2_], ["kd"])
                    qbz_view = qbz[:].rearrange("p c x -> p (c x)").rearrange(
                        "p (pr j i) -> p pr j i", j=4, i=64)[:, :, 0:4:3, :]
                    cp("pool", qbz_view, qb[:].rearrange("p (pr two i) -> p pr two i", two=2, i=64),
                       ["qb" + s2_], ["qbz" + s2_])
                    SEG()
                    pK = psb[4][:].bitcast(BF16)
                    for t4 in range(4):
                        tr(pK[:, t4 * 128:(t4 + 1) * 128], kd1[:, t4 * 128:(t4 + 1) * 128], ident_bf[:],
                           ["kd", "ident_bf"], [pk(4)])
                    cp("act", kdT[:], pK[:, 0:512], [pk(4)], ["kdT" + s2_])

                def stageC(k):
                    h, bi, c0b = items[k]
                    x_ = k % 2
                    s2_, s3_ = str(k % 2), "v%d" % (k % 4)
                    eb, qb, kb, qbz, kdT, v_sb, gw, ssq = (eb_[x_], qb_[x_], kb_[x_], qbz_[x_], kdT_[x_], v_[k % 4],
                                                           gw_[k % 4], ssq4[x_])
                    Sf, Sb16, ATm = Sall_f[x_], Sall_b[x_], ATm_all[x_]
                    bA, bSa, bSb, bo = 4, 5, 6, 7
                    for t4 in range(4):
                        mm(psb[bA][:, t4 * 128:(t4 + 1) * 128], kb[:, t4 * 128:(t4 + 1) * 128],
                           qb[:, t4 * 128:(t4 + 1) * 128], True, True, ["kb" + s2_, "qb" + s2_], [pk(bA)])
                    for t4 in range(4):
                        mm(psb[bSa][:, t4 * 128:(t4 + 1) * 128], kdT[0:64, t4 * 128:(t4 + 1) * 128], v_sb[0:64, t4, :],
                           True, True, ["kdT" + s2_, "v" + s3_], [pk(bSa)])
                        mm(psb[bSb][:, t4 * 128:(t4 + 1) * 128], kdT[64:128, t4 * 128:(t4 + 1) * 128], v_sb[64:128, t4, :],
                           True, True, ["kdT" + s2_, "v" + s3_], [pk(bSb)])
                    SEG()
                    tt("dve", ATm[:], psb[bA][:, :].rearrange("p (t s) -> p t s", s=128),
                       maskBD[:].unsqueeze(1).to_broadcast([128, 4, 128]), ALU.mult, [pk(bA), "maskBD"], ["ATm" + s2_])
                    if bi == 0:
                        prev_f, prev_b, pk_f, pk_b = S_h[:, h, :], Sbf0[:], "S_h%d" % h, "Sbf0"
                        cp("pool", Sbf0[:], S_h[:, h, :], ["S_h%d" % h], ["Sbf0"])
                    else:
                        prev_f, prev_b = Sall_f[1 - x_][:, 8, :], Sall_b[1 - x_][:, 8, :]
                        pk_f, pk_b = "Sf%d" % (1 - x_), "Sb%d" % (1 - x_)
                    for c in range(8):
                        src_ = prev_f if c == 0 else Sf[:, c, :]
                        bank = bSa if c % 2 == 0 else bSb
                        stt(Sf[:, c + 1, :], src_, eb[:, c * 64 + 63:c * 64 + 64],
                            psb[bank][:, (c // 2) * 128:(c // 2 + 1) * 128], ALU.mult, ALU.add,
                            ["Sf" + s2_, pk_f, "eb" + s2_, pk(bank)], ["Sf" + s2_])
                    SEG()
                    cp("act", Sb16[:, 1:5, :], Sf[:, 1:5, :], ["Sf" + s2_], ["Sb" + s2_])
                    cp("act", Sb16[:, 5:9, :], Sf[:, 5:9, :], ["Sf" + s2_], ["Sb" + s2_])
                    SEG()
                    for t4 in range(4):
                        ca_, cb_ = 2 * t4, 2 * t4 + 1
                        oc = psb[bo][:, t4 * 128:(t4 + 1) * 128]
                        mm(oc, ATm[:, t4, :], v_sb[:, t4, :], True, False, ["ATm" + s2_, "v" + s3_], [pk(bo)])
                        before = prev_b if ca_ == 0 else Sb16[:, ca_, :]
                        mm(oc, qbz[:, ca_, :], before, False, False, ["qbz" + s2_, "Sb" + s2_, pk_b], [pk(bo)])
                        mm(oc, qbz[:, cb_, :], Sb16[:, cb_, :], False, True, ["qbz" + s2_, "Sb" + s2_], [pk(bo)])
                    SEG()
                    for t4 in range(4):
                        act(junk[:], psb[bo][:, t4 * 128:(t4 + 1) * 128], AF.Square, [pk(bo)], ["junk", "ssq" + s2_],
                            accum=ssq[:, t4:t4 + 1])
                    rstd_small(ssq[:], 128, ["ssq" + s2_], ["ssq" + s2_])
                    SEG()
                    for t4 in range(4):
                        t = c0b // 128 + t4
                        stt(mix[:, t, h * 128:(h + 1) * 128], psb[bo][:, t4 * 128:(t4 + 1) * 128], ssq[:, t4:t4 + 1],
                            gw[:, t4, :], ALU.mult, ALU.mult, [pk(bo), "ssq" + s2_, "gw" + s3_], ["mix%d" % t])
                    if bi == 1:
                        cp("dve", S_h[:, h, :], Sf[:, 8, :], ["Sf" + s2_], ["S_h%d" % h])
                        if sb == 1:
                            store(hg_p[h], S_h[:, h, :], ["S_h%d" % h])

                def segs(lst):
                    out, cur = [], []
                    for th in lst:
                        if th is None:
                            out.append(cur)
                            cur = []
                        else:
                            cur.append(th)
                    out.append(cur)
                    return out

                def emit_iter(kc, ka, kb2):
                    cs = segs(with_pool((0, 1, 2, 3), stageC, kc)) if kc is not None else [[]] * 6
                    as_ = segs(with_pool((0, 1, 2, 3), stageA, ka)) if ka is not None else [[]] * 5
                    bs = segs(with_pool((0, 1, 2, 3), stageB, kb2)) if kb2 is not None else [[]] * 5
                    c1, c2, c3, c4, c5, c6 = cs
                    a1a, a1b, a2a, a2b, a3 = as_
                    b1, b2, b3, b4, b5 = bs
                    for seg in (c1, b1, b2, a1a, c2, b3, c3, b4, c4, a1b, c5, b5, c6, a2a, a2b, a3):
                        for th in seg:
                            th()

                emit_iter(None, 0, None)
                emit_iter(None, 1, None)
                emit_iter(None, 2, 0)
                for i in range(NI):
                    emit_iter(i, i + 3 if i + 3 < NI else None, i + 1 if i + 1 < NI else None)
            S.barrier()
            if sb == 1:
                with ExitStack() as ssm:
                    A_ = with_pool((0, 1, 2, 3), sample_hgrn, ssm)
                    B_ = with_pool((4, 5, 6, 7), sample_ssd, ssm)
                    for th in Sched.merge(A_, B_):
                        th()
                S.barrier()

            S.mark("C%d" % sb)
            if DEBUG:
                store(dbg_mix[sb][:, 0:8, :], mix[:, 0:8, :], ["mix%d" % t for t in range(9)])
                S.barrier()
            with ExitStack() as s3:
                Wout = sbt(s3, "Wout", [128, 16, 1024], BF16)
                mixT = [sbt(s3, "mixT%d" % i, [128, 16, 128], BF16) for i in range(2)]
                xtc = [sbt(s3, "xtc%d" % i, [128, D], F32) for i in range(2)]
                hnc = [sbt(s3, "hnc%d" % i, [128, D], BF16) for i in range(2)]
                sqc = sbt(s3, "sqc", [128, D], F32)
                ssC = sbt(s3, "ssC", [128, 2], F32)
                w2b = sbt(s3, "w2b", [128, D], F32)
                dma("sp", w2b[:], norm2_w.partition_broadcast(128), (), ["w2b"])
                for q4 in range(4):
                    load_w(Wout[:, q4 * 4:(q4 + 1) * 4, :],
                           w_out[q4 * 512:(q4 + 1) * 512, :].rearrange("(k p) n -> p k n", p=128), (), ["Wout%d" % q4])
                def c_s1(t):
                    i = t % 2
                    rw = rows_of(t)
                    src = xs if t == 8 else xp[tok0 + t * 128: tok0 + (t + 1) * 128, :]
                    dma("sp", xtc[i][0:rw, :], src, (), ["xtc%d" % i])
                    for half in range(2):
                        b = newbank()
                        pT = psb[b][:].bitcast(BF16)
                        for j in range(8):
                            fc = half * 8 + j
                            tr(pT[:, j * 128:j * 128 + rw], mix[0:rw, t, fc * 128:(fc + 1) * 128], ident_bf[0:rw, 0:rw],
                               ["mix%d" % t, "ident_bf"], [pk(b)])
                        cp("act" if half == 0 else "dve", mixT[i][:, half * 8:(half + 1) * 8, 0:rw],
                           pT[:, 0:1024].rearrange("p (k c) -> p k c", k=8)[:, :, 0:rw], [pk(b)], ["mixT%d" % i])

                def c_s2(t):
                    i = t % 2
                    rw = rows_of(t)
                    bo2 = [newbank(), newbank()]
                    for nh in range(2):
                        for fc in range(16):
                            mm(psb[bo2[nh]][0:rw, :], mixT[i][:, fc, 0:rw], Wout[:, fc, nh * 512:(nh + 1) * 512],
                               fc == 0, fc == 15, ["mixT%d" % i, "Wout%d" % (fc // 4)], [pk(bo2[nh])])
                    for nh in range(2):
                        tt("dve", x2[0:rw, t, nh * 512:(nh + 1) * 512], xtc[i][0:rw, nh * 512:(nh + 1) * 512],
                           psb[bo2[nh]][0:rw, :], ALU.add, ["xtc%d" % i, pk(bo2[nh])], ["x2_%d" % t])
                    act(sqc[0:rw, :], x2[0:rw, t, :], AF.Square, ["x2_%d" % t], ["sqc", "ssC%d" % i],
                        accum=ssC[0:rw, i:i + 1])
                    rstd_small(ssC[0:rw, i:i + 1], D, ["ssC%d" % i], ["ssC%d" % i])
                    stt(hnc[i][0:rw, :], x2[0:rw, t, :], ssC[0:rw, i:i + 1], w2b[0:rw, :], ALU.mult, ALU.mult,
                        ["x2_%d" % t, "ssC%d" % i, "w2b"], ["hnc%d" % i])

                def c_s3(t):
                    i = t % 2
                    rw = rows_of(t)
                    b = newbank()
                    pT = psb[b][:].bitcast(BF16)
                    for k in range(8):
                        tr(pT[:, k * 128:k * 128 + rw], hnc[i][0:rw, k * 128:(k + 1) * 128], ident_bf[0:rw, 0:rw],
                           ["hnc%d" % i, "ident_bf"], [pk(b)])
                    cp("act", hT[:, :, t * 128:t * 128 + rw],
                       pT[:, 0:1024].rearrange("p (k c) -> p k c", k=8)[:, :, 0:rw], [pk(b)], ["hT"])

                c_s1(0)
                for t in range(ntile):
                    if t + 1 < ntile:
                        c_s1(t + 1)
                    c_s2(t)
                    if t >= 1:
                        c_s3(t - 1)
                c_s3(ntile - 1)
            S.barrier()

            S.mark("D%d" % sb)
            with ExitStack() as s4:
                actT = sbt(s4, "actT", [128, 11, 1040], BF16)
                Wd = sbt(s4, "Wd", [128, 11, 1024], BF16)
                graw = [sbt(s4, "graw%d" % i, [128, 2 + 1024], F32) for i in range(2)]
                gacc = [sbt(s4, "gacc%d" % i, [128, 1024], F32) for i in range(2)]
                if sb == 1:
                    stcf_t = [sbt(s4, "stcf_t%d" % i, [16, 2, 128], F32) for i in range(2)]
                    CFT = [sbt(s4, "CFT%d" % i, [128, 2, 16], F32) for i in range(2)]
                    grawS = [sbt(s4, "grawS%d" % i, [128, 16], F32) for i in range(2)]
                    gaccS = [sbt(s4, "gaccS%d" % i, [128, 16], F32) for i in range(2)]
                    gtokS = [sbt(s4, "gtokS%d" % i, [16, 128], F32) for i in range(2)]
                yt = [sbt(s4, "yt%d" % i, [128, D], F32) for i in range(2)]
                sqe = sbt(s4, "sqe", [128, D], F32)
                ssE = sbt(s4, "ssE", [128, 2], F32)
                wfb = sbt(s4, "wfb", [128, D], F32)
                dma("sp", wfb[:], final_norm_w.partition_broadcast(128), (), ["wfb"])

                def phase_e(t):
                    i = t % 2
                    rw = rows_of(t)
                    act(sqe[0:rw, :], x2[0:rw, t, :], AF.Square, ["x2_%d" % t], ["sqe", "ssE%d" % i],
                        accum=ssE[0:rw, i:i + 1])
                    rstd_small(ssE[0:rw, i:i + 1], D, ["ssE%d" % i], ["ssE%d" % i])
                    stt(yt[i][0:rw, :], x2[0:rw, t, :], ssE[0:rw, i:i + 1], wfb[0:rw, :], ALU.mult, ALU.mult,
                        ["x2_%d" % t, "ssE%d" % i, "wfb"], ["yt%d" % i])
                    dst = y_s if t == 8 else y_p[tok0 + t * 128: tok0 + (t + 1) * 128, :]
                    store(dst, yt[i][0:rw, :], ["yt%d" % i])

                for fg in range(2):
                    steps = [(ftl, bi, c0b) for ftl in range(11) for bi, (c0b, nb) in enumerate(blocks) if nb == 512]
                    fst = {}

                    def f_s1(n):
                        ftl, bi, c0b = steps[n]
                        ft = fg * 11 + ftl
                        ri = ft % 2
                        gr = graw[ri]
                        if bi == 0:
                            fst[("slot", ftl)] = w_use(("ffn", sb, fg, ftl // 2))
                            if ftl == 0:
                                for f2 in range(11):
                                    load_w(Wd[:, f2, :], w_down[(fg * 11 + f2) * 128:(fg * 11 + f2 + 1) * 128, :], (),
                                           ["Wd%d" % f2])
                            cp("act", gr[:, 0:2], halo_f[:, ft, :], ["halo_f%d" % ft], ["graw%dh" % ri])
                        slot = fst[("slot", ftl)]
                        gc0 = (ftl % 2) * 128
                        vc0 = 256 + (ftl % 2) * 128
                        todo = [(c0b, 512)] + ([(1024, 16)] if (bi == 1 and sb == 1) else [])
                        for (c0, nb) in todo:
                            bg = newbank()
                            for k in range(8):
                                mm(psb[bg][:, 0:nb], wst[slot][:, k, gc0:gc0 + 128], hT[:, k, c0:c0 + nb], k == 0, k == 7,
                                   wq(slot, gc0, gc0 + 128) + ["hT"], [pk(bg)])
                            bvv = newbank()
                            hold(bvv)
                            for k in range(8):
                                mm(psb[bvv][:, 0:nb], wst[slot][:, k, vc0:vc0 + 128], hT[:, k, c0:c0 + nb], k == 0, k == 7,
                                   wq(slot, vc0, vc0 + 128) + ["hT"], [pk(bvv)])
                            if nb == 16:
                                fst[("vS", ftl)] = bvv
                                cp("act", grawS[ri][:], psb[bg][:, 0:16], [pk(bg)], ["grawS%d" % ri])
                            else:
                                fst[("v", n)] = bvv
                                cp("act", gr[:, 2 + c0:2 + c0 + 512], psb[bg][:, 0:512], [pk(bg)], ["graw%db%d" % (ri, bi)])
                        if bi == 1:
                            cp("act", halo_f[:, ft, :], gr[:, 1024:1026], ["graw%db1" % ri], ["halo_f%d" % ft])

                    def f_s2(n):
                        ftl, bi, c0b = steps[n]
                        ft = fg * 11 + ftl
                        ri = ft % 2
                        gr = graw[ri]
                        gb = gacc[ri][:, c0b:c0b + 512]
                        ak = "gacc%db%d" % (ri, bi)
                        gk = ["graw%dh" % ri, "graw%db0" % ri] + (["graw%db1" % ri] if bi == 1 else [])
                        ts("dve", gb, gr[:, 2 + c0b:2 + c0b + 512], fw_col[:, ft, 2:3], fb_col[:, ft:ft + 1],
                           ALU.mult, ALU.add, gk + ["fw_col", "fb_col"], [ak])
                        for jj in (1, 0):
                            stt(gb, gr[:, jj + c0b:jj + c0b + 512], fw_col[:, ft, jj:jj + 1], gb, ALU.mult, ALU.add,
                                gk + ["fw_col", ak], [ak])

                    def f_s3(n):
                        ftl, bi, c0b = steps[n]
                        ri = (fg * 11 + ftl) % 2
                        gb = gacc[ri][:, c0b:c0b + 512]
                        ak = "gacc%db%d" % (ri, bi)
                        act(gb, gb, AF.Silu, [ak], [ak])

                    def f_s4(n):
                        ftl, bi, c0b = steps[n]
                        ft = fg * 11 + ftl
                        ri = ft % 2
                        gb = gacc[ri][:, c0b:c0b + 512]
                        ak = "gacc%db%d" % (ri, bi)
                        bvv = fst[("v", n)]
                        tt("dve", actT[:, ftl, c0b:c0b + 512], gb, psb[bvv][:, :], ALU.mult, [ak, pk(bvv)],
                           ["actT%d" % ftl])
                        release(bvv)
                        if bi == 1 and sb == 1:
                            ffn_sample_tile(ft, ftl, ri, stcf_t, CFT, grawS, gaccS, gtokS, actT, fst[("vS", ftl)])
                            release(fst[("vS", ftl)])

                    NS = len(steps)
                    f_s1(0)
                    for n in range(NS):
                        if n + 1 < NS:
                            f_s1(n + 1)
                        f_s2(n)
                        if n >= 1:
                            f_s4(n - 1)
                        f_s3(n)
                    f_s4(NS - 1)
                    if sb == 1:
                        for ftl2 in range(11):
                            ft2 = fg * 11 + ftl2
                            store(cf_p[:, ft2 * 128:(ft2 + 1) * 128].rearrange("t p -> p t"), halo_f[:, ft2, :],
                                  ["halo_f%d" % ft2], slow=True)
                    for t in range(ntile):
                        rw = rows_of(t)
                        b2 = [newbank(), newbank()]
                        for nh in range(2):
                            for ftl in range(11):
                                mm(psb[b2[nh]][0:rw, :], actT[:, ftl, t * 128:t * 128 + rw], Wd[:, ftl, nh * 512:(nh + 1) * 512],
                                   ftl == 0, ftl == 10, ["actT%d" % ftl, "Wd%d" % ftl], [pk(b2[nh])])
                        if fg == 1 and t >= 1:
                            phase_e(t - 1)
                        for nh in range(2):
                            tt("dve", x2[0:rw, t, nh * 512:(nh + 1) * 512], x2[0:rw, t, nh * 512:(nh + 1) * 512],
                               psb[b2[nh]][0:rw, :], ALU.add, ["x2_%d" % t, pk(b2[nh])], ["x2_%d" % t])
                    if fg == 1:
                        phase_e(ntile - 1)
            S.barrier()

        if CUT is not None:
            cut_at = S.marks[CUT]
            S.ops = S.ops[:cut_at]
            out_ops[:] = [o for o in out_ops if o.idx < cut_at]
        S.final_wait(out_ops)
        S.emit(nc, es)
    return nc


_NC_CACHE = {}


def kernel(**inputs):
    f32 = np.float32
    g = {k: np.ascontiguousarray(np.asarray(v, dtype=f32)) for k, v in inputs.items()}
    if "nc" not in _NC_CACHE:
        _NC_CACHE["nc"] = build_nc()
    nc = _NC_CACHE["nc"]
    shared = {
        "norm1_w": g["norm1_w"][0], "w_in": g["w_in"][0], "hgrn_lb": g["hgrn_lb"],
        "hgrn_norm_w": g["hgrn_norm_w"][0], "ssm_conv_w": g["ssm_conv_w"][0], "ssm_conv_b": g["ssm_conv_b"][0],
        "ssm_dt_bias": g["ssm_dt_bias"][0], "ssm_a_log": g["ssm_a_log"][0], "ssm_d": g["ssm_d"][0],
        "ssm_norm_w": g["ssm_norm_w"][0], "w_out": g["w_out"][0], "norm2_w": g["norm2_w"][0],
        "w_up": g["w_up"][0], "ffn_conv_w": g["ffn_conv_w"][0], "ffn_conv_b": g["ffn_conv_b"][0],
        "w_down": g["w_down"][0], "final_norm_w": g["final_norm_w"],
    }
    in_maps = []
    for c in range(NCORES):
        sl = slice(16 * c, 16 * c + 16)
        m = dict(shared)
        m["xp"] = g["x_prompt"][c]
        m["xs"] = g["x_sample"][sl, 0, :]
        m["st_h"] = g["state_hgrn"][0, sl]
        m["st_s"] = g["state_ssm"][0, sl].reshape(16, 1024, 128)
        m["st_cs"] = g["state_conv_ssm"][0, sl]
        m["st_cf"] = g["state_conv_ffn"][0, sl]
        in_maps.append({k: np.ascontiguousarray(v) for k, v in m.items()})
    res = run_bass_kernel_spmd(nc, in_maps, core_ids=list(range(NCORES)))
    R = res.results
    y_prompt = np.stack([R[c]["y_p"] for c in range(NCORES)], 0)
    y_sample = np.concatenate([R[c]["y_s"] for c in range(NCORES)], 0)[:, None, :]
    hgrn_p = np.stack([R[c]["hg_p"] for c in range(NCORES)], 0)[None]
    hgrn_s = np.concatenate([R[c]["hg_s"] for c in range(NCORES)], 0)[None]
    ssm_p = np.stack([R[c]["sm_p"].reshape(16, 64, 128) for c in range(NCORES)], 0)[None]
    ssm_s = np.concatenate([R[c]["sm_s"].reshape(16, 16, 64, 128) for c in range(NCORES)], 0)[None]
    cs_p_ = np.stack([R[c]["cs_p"] for c in range(NCORES)], 0)[None]
    cs_s_ = np.concatenate([R[c]["cs_s"] for c in range(NCORES)], 0)[None]
    cf_p_ = np.stack([R[c]["cf_p"] for c in range(NCORES)], 0)[None]
    cf_s_ = np.concatenate([R[c]["cf_s"] for c in range(NCORES)], 0)[None]
    outs = (y_prompt, y_sample, hgrn_p, hgrn_s, ssm_p, ssm_s, cs_p_, cs_s_, cf_p_, cf_s_)
    return tuple(np.ascontiguousarray(o, dtype=f32) for o in outs)
```
